# Optimizing a Trainium2 kernel written in Bass

```python
import math
import jax
import jax.numpy as jnp
from jax import lax
import numpy as np

D_MODEL = 2048
BATCH = 1
SEQ = 8192
DEPTH = 4

GRID_W = 64
CTX_LEN = 256
ROPE_THETA = 10000.0
NORM_EPS = 1e-6
Q_BLOCK = 128

A_HEADS = 4
A_KV_HEADS = 2
A_HEAD_DIM = 128
A_WIDTH = A_HEADS * A_HEAD_DIM
A_KV_WIDTH = A_KV_HEADS * A_HEAD_DIM
LRU_WIDTH = 512
LRU_BLOCKS = 4
LRU_CONV = 4
LRU_C = 8.0
DN_HEADS = 4
DN_HEAD_DIM = 128
DN_WIDTH = DN_HEADS * DN_HEAD_DIM
DN_CONV = 4
DN_CHUNK = 64
MLA_HEADS = 4
MLA_Q_RANK = 384
MLA_KV_RANK = 256
MLA_NOPE = 128
MLA_ROPE = 64
MLA_V = 128
MLA_WIDTH = MLA_HEADS * MLA_V

MIX_WIDTH = A_WIDTH + LRU_WIDTH + DN_WIDTH + MLA_WIDTH
IN_SPLITS = (
    A_WIDTH, A_KV_WIDTH, A_KV_WIDTH, A_WIDTH,
    LRU_WIDTH, LRU_WIDTH,
    3 * DN_WIDTH, DN_WIDTH, 4 * DN_HEADS,
    MLA_Q_RANK, MLA_KV_RANK, MLA_ROPE, MLA_WIDTH,
)
IN_WIDTH = sum(IN_SPLITS)

kernel_name = 'hybrid_parallel_heads_dit_block'


def _rms_norm(x, w):
    xf = x.astype(jnp.float32)
    y = xf * lax.rsqrt(jnp.mean(xf * xf, axis=-1, keepdims=True) + NORM_EPS)
    return (y * w).astype(x.dtype)


def _l2_norm(x):
    return x * lax.rsqrt(jnp.sum(x * x, axis=-1, keepdims=True) + NORM_EPS)


def _split_heads(x, n):
    b, t, _ = x.shape
    return x.reshape(b, t, n, -1).transpose(0, 2, 1, 3)


def _merge_heads(x):
    b, n, t, d = x.shape
    return x.transpose(0, 2, 1, 3).reshape(b, t, n * d)


def _rope_1d(x, pos):
    half = x.shape[-1] // 2
    inv_freq = ROPE_THETA ** (-jnp.arange(half, dtype=jnp.float32) / half)
    ang = pos.astype(jnp.float32)[:, None] * inv_freq
    cos, sin = jnp.cos(ang), jnp.sin(ang)
    x1, x2 = x[..., :half], x[..., half:]
    return jnp.concatenate([x1 * cos - x2 * sin, x1 * sin + x2 * cos], axis=-1).astype(x.dtype)


def _rope_2d(x, rows, cols):
    half = x.shape[-1] // 2
    return jnp.concatenate([_rope_1d(x[..., :half], rows), _rope_1d(x[..., half:], cols)], axis=-1)


def _attend(q, k, v, scale):
    s = jnp.einsum('bkgqd,bktd->bkgqt', q, k).astype(jnp.float32) * scale
    p = jax.nn.softmax(s, axis=-1).astype(v.dtype)
    return jnp.einsum('bkgqt,bktd->bkgqd', p, v)


def _blocked_attention(q, k, v, scale):
    b, hk, g, s, d = q.shape
    nb = s // Q_BLOCK
    qb = jnp.moveaxis(q.reshape(b, hk, g, nb, Q_BLOCK, d), 3, 0)
    ob = lax.map(lambda qi: _attend(qi, k, v, scale), qb)
    return jnp.moveaxis(ob, 0, 3).reshape(b, hk, g, s, v.shape[-1])


def _dwconv_centred(x, w):
    k_w = w.shape[0]
    left = k_w // 2
    t = x.shape[1]
    xp = jnp.pad(x, ((0, 0), (left, k_w - 1 - left), (0, 0)))
    y = xp[:, 0:t] * w[0]
    for j in range(1, k_w):
        y = y + xp[:, j:j + t] * w[j]
    return y


def _block_diag(x, w):
    b, t, _ = x.shape
    nb, bw, _ = w.shape
    return jnp.einsum('btnj,njk->btnk', x.reshape(b, t, nb, bw), w).reshape(b, t, nb * bw)


def _lin_combine(left, right):
    a1, b1 = left
    a2, b2 = right
    return a1 * a2, a2 * b1 + b2


def _lru_scan(a, b, h0):
    a_cum, h = lax.associative_scan(_lin_combine, (a, b), axis=1)
    h = h + a_cum * h0[:, None]
    return h, h[:, -1]


def _gated_delta_chunked(q, k, v, g, beta, h0):
    b, h, t, dk = q.shape
    dv = v.shape[-1]
    c = DN_CHUNK
    n = t // c

    def chunks(a):
        return a.reshape(a.shape[:2] + (n, c) + a.shape[3:])

    q = chunks(q * dk ** -0.5)
    k = chunks(k)
    v = chunks(v)
    beta = chunks(beta)
    g = jnp.cumsum(chunks(g), axis=-1)
    idx = jnp.arange(c)
    incl = idx[:, None] >= idx[None, :]
    strict = idx[:, None] > idx[None, :]
    decay = jnp.exp(jnp.where(incl, g[..., :, None] - g[..., None, :], -jnp.inf))
    kb = k * beta[..., None]
    lower = jnp.where(strict, jnp.einsum('bhncd,bhnsd->bhncs', kb, k) * decay, 0.0)
    eye = jnp.eye(c, dtype=lower.dtype)
    rhs = jnp.concatenate([v * beta[..., None], kb * jnp.exp(g)[..., None]], axis=-1)
    sol = lax.linalg.triangular_solve(lower + eye, rhs, left_side=True, lower=True, unit_diagonal=True)
    u, w = sol[..., :dv], sol[..., dv:]
    qk = jnp.where(incl, jnp.einsum('bhncd,bhnsd->bhncs', q, k) * decay, 0.0)
    g_last = g[..., -1]
    k_dec = k * jnp.exp(g_last[..., None] - g)[..., None]
    q_dec = q * jnp.exp(g)[..., None]

    def step(s, inp):
        q_c, k_c, u_c, w_c, qk_c, gl_c = inp
        v_new = u_c - jnp.einsum('bhcd,bhde->bhce', w_c, s)
        o = jnp.einsum('bhcd,bhde->bhce', q_c, s) + jnp.einsum('bhcs,bhse->bhce', qk_c, v_new)
        s = s * jnp.exp(gl_c)[..., None, None] + jnp.einsum('bhcd,bhce->bhde', k_c, v_new)
        return s, o

    xs = tuple(jnp.moveaxis(a, 2, 0) for a in (q_dec, k_dec, u, w, qk, g_last))
    s_final, o = lax.scan(step, h0, xs)
    return jnp.moveaxis(o, 0, 2).reshape(b, h, t, dv), s_final


def _bidirectional(scan, ctx_fwd, lat_fwd, ctx_bwd, lat_bwd, h0, t_axis, need_ctx):
    def flip(args):
        return tuple(jnp.flip(a, t_axis) for a in args)

    yc_f, s_f = scan(*ctx_fwd, h0)
    yl_f, _ = scan(*lat_fwd, s_f)
    yc_b, s_b = scan(*flip(ctx_bwd), h0)
    yl_b, _ = scan(*flip(lat_bwd), s_b)
    y_lat = yl_f + jnp.flip(yl_b, t_axis)
    y_ctx = yc_f + jnp.flip(yc_b, t_axis) if need_ctx else None
    return y_lat, y_ctx


def _gqa_mixer(p, pc, qn, kn, rows, cols, need_ctx):
    uq, uk, uv, z = p
    ucq, uck, ucv, zc = pc
    b, s, _ = uq.shape
    grp = A_HEADS // A_KV_HEADS
    scale = A_HEAD_DIM ** -0.5
    q = _rope_2d(_rms_norm(_split_heads(uq, A_HEADS), qn), rows, cols)
    k = _rope_2d(_rms_norm(_split_heads(uk, A_KV_HEADS), kn), rows, cols)
    v = _split_heads(uv, A_KV_HEADS)
    kc = _rms_norm(_split_heads(uck, A_KV_HEADS), kn)
    vc = _split_heads(ucv, A_KV_HEADS)
    o = _blocked_attention(q.reshape(b, A_KV_HEADS, grp, s, A_HEAD_DIM),
                           jnp.concatenate([kc, k], axis=2), jnp.concatenate([vc, v], axis=2), scale)
    y = _merge_heads(o.reshape(b, A_HEADS, s, A_HEAD_DIM)) * jax.nn.silu(z)
    if not need_ctx:
        return y, None
    qc = _rms_norm(_split_heads(ucq, A_HEADS), qn)
    oc = _attend(qc.reshape(b, A_KV_HEADS, grp, -1, A_HEAD_DIM), kc, vc, scale)
    yc = _merge_heads(oc.reshape(b, A_HEADS, -1, A_HEAD_DIM)) * jax.nn.silu(zc)
    return y, yc


def _rglru_mixer(p, pc, conv_w, conv_b, w_a, b_a, w_x, b_x, lam, need_ctx):
    ux, z = p
    ucx, zc = pc
    xl = (_dwconv_centred(ux, conv_w) + conv_b).astype(jnp.float32)
    xc = (_dwconv_centred(ucx, conv_w) + conv_b).astype(jnp.float32)

    def gates(xs, d):
        r = jax.nn.sigmoid(_block_diag(xs, w_a[d]) + b_a[d])
        i = jax.nn.sigmoid(_block_diag(xs, w_x[d]) + b_x[d])
        log_a = -LRU_C * r * jax.nn.softplus(-lam[d])
        return jnp.exp(log_a), jnp.sqrt(-jnp.expm1(2.0 * log_a)) * (i * xs)

    h0 = jnp.zeros((ux.shape[0], LRU_WIDTH), jnp.float32)
    hl, hc = _bidirectional(_lru_scan, gates(xc, 0), gates(xl, 0), gates(xc, 1), gates(xl, 1),
                            h0, 1, need_ctx)
    y = hl.astype(ux.dtype) * jax.nn.silu(z)
    yc = hc.astype(ux.dtype) * jax.nn.silu(zc) if need_ctx else None
    return y, yc


def _deltanet_mixer(p, pc, conv_w, a_log, dt_bias, norm_w, need_ctx):
    def prep(uqkv, uab):
        b, t, _ = uqkv.shape
        qkv = jax.nn.silu(_dwconv_centred(uqkv, conv_w)).astype(jnp.float32)
        q, k, v = jnp.split(qkv, 3, axis=-1)
        q = _l2_norm(_split_heads(q, DN_HEADS))
        k = _l2_norm(_split_heads(k, DN_HEADS))
        v = _split_heads(v, DN_HEADS)
        ab = uab.astype(jnp.float32).reshape(b, t, 2, 2, DN_HEADS)
        beta = jax.nn.sigmoid(ab[..., 0, :]).transpose(2, 0, 3, 1)
        g = (-jnp.exp(a_log.astype(jnp.float32))[:, None, :, None]
             * jax.nn.softplus(ab[..., 1, :] + dt_bias).transpose(2, 0, 3, 1))
        return q, k, v, g, beta

    uqkv, z, uab = p
    ucqkv, zc, ucab = pc
    q, k, v, g, beta = prep(uqkv, uab)
    qc, kc, vc, gc, betac = prep(ucqkv, ucab)
    h0 = jnp.zeros((uqkv.shape[0], DN_HEADS, DN_HEAD_DIM, DN_HEAD_DIM), jnp.float32)
    ol, oc = _bidirectional(_gated_delta_chunked,
                            (qc, kc, vc, gc[0], betac[0]), (q, k, v, g[0], beta[0]),
                            (qc, kc, vc, gc[1], betac[1]), (q, k, v, g[1], beta[1]),
                            h0, 2, need_ctx)

    def out(o, zz):
        return _merge_heads(_rms_norm(o, norm_w).astype(zz.dtype) * jax.nn.silu(_split_heads(zz, DN_HEADS)))

    return out(ol, z), (out(oc, zc) if need_ctx else None)


def _mla_mixer(p, pc, q_norm_w, kv_norm_w, w_uq, w_ukv, qn, kn, rows, cols, need_ctx):
    def project_q(ucq):
        return _rms_norm(_split_heads(_rms_norm(ucq, q_norm_w) @ w_uq, MLA_HEADS), qn)

    def project_kv(uckv, ukr):
        kv = _split_heads(_rms_norm(uckv, kv_norm_w) @ w_ukv, MLA_HEADS)
        k_nope, v = kv[..., :MLA_NOPE], kv[..., MLA_NOPE:]
        k_rope = jnp.broadcast_to(ukr[:, None], k_nope.shape[:-1] + (MLA_ROPE,))
        k = _rms_norm(jnp.concatenate([k_nope, k_rope], axis=-1), kn)
        return k, v

    def rotate(t):
        return jnp.concatenate([t[..., :MLA_NOPE], _rope_2d(t[..., MLA_NOPE:], rows, cols)], axis=-1)

    ucq, uckv, ukr, z = p
    ccq, cckv, ckr, zc = pc
    scale = (MLA_NOPE + MLA_ROPE) ** -0.5
    q = rotate(project_q(ucq))
    k, v = project_kv(uckv, ukr)
    k = rotate(k)
    kc, vc = project_kv(cckv, ckr)
    o = _blocked_attention(q[:, :, None], jnp.concatenate([kc, k], axis=2),
                           jnp.concatenate([vc, v], axis=2), scale)[:, :, 0]
    y = _merge_heads(o) * jax.nn.silu(z)
    if not need_ctx:
        return y, None
    oc = _attend(project_q(ccq)[:, :, None], kc, vc, scale)[:, :, 0]
    return y, _merge_heads(oc) * jax.nn.silu(zc)


def _layer(x, ctx, c, c_ctx, rows, cols, need_ctx,
           norm_w, w_ada, b_ada, w_in, w_out, attn_q_norm, attn_k_norm,
           lru_conv_w, lru_conv_b, lru_w_a, lru_b_a, lru_w_x, lru_b_x, lru_lambda,
           dn_conv_w, dn_a_log, dn_dt_bias, dn_norm_w,
           mla_q_norm, mla_kv_norm, mla_w_uq, mla_w_ukv, mla_q_qk_norm, mla_k_qk_norm):
    shift, scale, gate = jnp.split(jax.nn.silu(c) @ w_ada + b_ada, 3, axis=-1)
    shift_c, scale_c, gate_c = jnp.split(jax.nn.silu(c_ctx) @ w_ada + b_ada, 3, axis=-1)
    h = _rms_norm(x, norm_w) * (1.0 + scale[:, None]) + shift[:, None]
    hc = _rms_norm(ctx, norm_w) * (1.0 + scale_c) + shift_c
    offsets = np.cumsum(IN_SPLITS)[:-1].tolist()
    p = jnp.split(h @ w_in, offsets, axis=-1)
    pc = jnp.split(hc @ w_in, offsets, axis=-1)
    y_a, yc_a = _gqa_mixer(p[0:4], pc[0:4], attn_q_norm, attn_k_norm, rows, cols, need_ctx)
    y_b, yc_b = _rglru_mixer(p[4:6], pc[4:6], lru_conv_w, lru_conv_b, lru_w_a, lru_b_a,
                             lru_w_x, lru_b_x, lru_lambda, need_ctx)
    y_c, yc_c = _deltanet_mixer(p[6:9], pc[6:9], dn_conv_w, dn_a_log, dn_dt_bias, dn_norm_w, need_ctx)
    y_d, yc_d = _mla_mixer(p[9:13], pc[9:13], mla_q_norm, mla_kv_norm, mla_w_uq, mla_w_ukv,
                           mla_q_qk_norm, mla_k_qk_norm, rows, cols, need_ctx)
    x = x + gate[:, None] * (jnp.concatenate([y_a, y_b, y_c, y_d], axis=-1) @ w_out)
    if need_ctx:
        ctx = ctx + gate_c * (jnp.concatenate([yc_a, yc_b, yc_c, yc_d], axis=-1) @ w_out)
    return x, ctx


def setup_inputs(seed: int = 0) -> dict:
    key = jax.random.key(seed)
    ks = iter(jax.random.split(key, 40))
    f32 = jnp.float32
    L = DEPTH

    def nrm(shape, scale):
        return scale * jax.random.normal(next(ks), shape, f32)

    def gain(shape):
        return 1.0 + nrm(shape, 0.01)

    u_lam = jax.random.uniform(next(ks), (L, 2, LRU_WIDTH), f32, 0.9, 0.999)
    dt = jnp.exp(jax.random.uniform(next(ks), (L, 2, DN_HEADS), f32, math.log(1e-3), math.log(1e-1)))
    bw = LRU_WIDTH // LRU_BLOCKS
    return {
        'x': nrm((BATCH, SEQ, D_MODEL), 1.0),
        'c': nrm((BATCH, D_MODEL), 1.0),
        'ctx': nrm((BATCH, CTX_LEN, D_MODEL), 1.0),
        'c_ctx': nrm((D_MODEL,), 1.0),
        'norm_w': gain((L, D_MODEL)),
        'w_ada': nrm((L, D_MODEL, 3 * D_MODEL), 0.5 * D_MODEL ** -0.5),
        'b_ada': nrm((L, 3 * D_MODEL), 0.01),
        'w_in': nrm((L, D_MODEL, IN_WIDTH), D_MODEL ** -0.5),
        'w_out': nrm((L, MIX_WIDTH, D_MODEL), MIX_WIDTH ** -0.5),
        'attn_q_norm': gain((L, A_HEAD_DIM)),
        'attn_k_norm': gain((L, A_HEAD_DIM)),
        'lru_conv_w': nrm((L, LRU_CONV, LRU_WIDTH), LRU_CONV ** -0.5),
        'lru_conv_b': nrm((L, LRU_WIDTH), 0.01),
        'lru_w_a': nrm((L, 2, LRU_BLOCKS, bw, bw), bw ** -0.5),
        'lru_b_a': nrm((L, 2, LRU_WIDTH), 0.01),
        'lru_w_x': nrm((L, 2, LRU_BLOCKS, bw, bw), bw ** -0.5),
        'lru_b_x': nrm((L, 2, LRU_WIDTH), 0.01),
        'lru_lambda': jnp.log(u_lam) - jnp.log1p(-u_lam),
        'dn_conv_w': nrm((L, DN_CONV, 3 * DN_WIDTH), DN_CONV ** -0.5),
        'dn_a_log': jnp.log(jax.random.uniform(next(ks), (L, 2, DN_HEADS), f32, 1.0, 16.0)),
        'dn_dt_bias': dt + jnp.log(-jnp.expm1(-dt)),
        'dn_norm_w': gain((L, DN_HEAD_DIM)),
        'mla_q_norm': gain((L, MLA_Q_RANK)),
        'mla_kv_norm': gain((L, MLA_KV_RANK)),
        'mla_w_uq': nrm((L, MLA_Q_RANK, MLA_HEADS * (MLA_NOPE + MLA_ROPE)), MLA_Q_RANK ** -0.5),
        'mla_w_ukv': nrm((L, MLA_KV_RANK, MLA_HEADS * (MLA_NOPE + MLA_V)), MLA_KV_RANK ** -0.5),
        'mla_q_qk_norm': gain((L, MLA_NOPE + MLA_ROPE)),
        'mla_k_qk_norm': gain((L, MLA_NOPE + MLA_ROPE)),
    }


def reference(x, c, ctx, c_ctx, norm_w, w_ada, b_ada, w_in, w_out, attn_q_norm, attn_k_norm,
              lru_conv_w, lru_conv_b, lru_w_a, lru_b_a, lru_w_x, lru_b_x, lru_lambda,
              dn_conv_w, dn_a_log, dn_dt_bias, dn_norm_w,
              mla_q_norm, mla_kv_norm, mla_w_uq, mla_w_ukv, mla_q_qk_norm, mla_k_qk_norm):
    n_tok = x.shape[1]
    ROWS = n_tok // GRID_W
    rows = jnp.repeat(jnp.arange(ROWS), GRID_W)
    cols = jnp.tile(jnp.arange(GRID_W), ROWS)
    for l in range(DEPTH):
        x, ctx = _layer(x, ctx, c, c_ctx, rows, cols, l < DEPTH - 1,
                        norm_w[l], w_ada[l], b_ada[l], w_in[l], w_out[l], attn_q_norm[l], attn_k_norm[l],
                        lru_conv_w[l], lru_conv_b[l], lru_w_a[l], lru_b_a[l], lru_w_x[l], lru_b_x[l],
                        lru_lambda[l], dn_conv_w[l], dn_a_log[l], dn_dt_bias[l], dn_norm_w[l],
                        mla_q_norm[l], mla_kv_norm[l], mla_w_uq[l], mla_w_ukv[l],
                        mla_q_qk_norm[l], mla_k_qk_norm[l])
    return x
```

```python
import contextlib
import numpy as np
import ml_dtypes
import concourse.bass as bass
import concourse.mybir as mybir
from concourse.bass_utils import run_bass_kernel_spmd

F32 = mybir.dt.float32
BF16 = mybir.dt.bfloat16
AF = mybir.ActivationFunctionType
ALU = mybir.AluOpType
AX = mybir.AxisListType
NPBF = ml_dtypes.bfloat16

NCORES = 8
D = 2048
SEQ = 8192
CTX = 256
DEPTH = 4
TOK = SEQ // NCORES
CTK = CTX // NCORES
NT = TOK + CTK
NALL = SEQ + CTX
EPS = 1e-6
IN_W = 5840


class Tok:
    __slots__ = ("lw", "rd", "name")

    def __init__(self, name=""):
        self.lw = None
        self.rd = {}
        self.name = name


class Prog:
    ENG = ("pe", "act", "dve", "pool", "sp")

    def __init__(self, nc, es, n_dma_sems=8, same_engine_sync=True):
        self.nc = nc
        self.es = es
        self.eng = {"pe": nc.tensor, "act": nc.scalar, "dve": nc.vector,
                    "pool": nc.gpsimd, "sp": nc.sync}
        self.streams = {e: [] for e in self.ENG}
        self.sems = {}
        self.count = {}
        self.waited = {e: {} for e in self.ENG}
        for e in self.ENG:
            self.sems[e] = es.enter_context(nc.semaphore("sem_" + e))
            self.count[e] = 0
        self.dma_pool = {}
        self.dma_rr = {}
        for q in ("sp", "pool", "act"):
            lst = []
            for i in range(n_dma_sems):
                k = "dma_%s_%d" % (q, i)
                self.sems[k] = es.enter_context(nc.semaphore(k))
                self.count[k] = 0
                lst.append(k)
            self.dma_pool[q] = lst
            self.dma_rr[q] = 0
        self.same_engine_sync = same_engine_sync
        self.n_ops = 0

    def _deps(self, eng, reads, writes):
        deps = {}

        def add(k, v):
            if deps.get(k, 0) < v:
                deps[k] = v
        for t in reads:
            if t.lw is not None:
                add(*t.lw)
        for t in writes:
            if t.lw is not None:
                add(*t.lw)
            for k, v in t.rd.items():
                add(k, v)
        return deps

    def _emit_waits(self, eng, deps):
        for k, v in deps.items():
            if k == eng:
                if eng == "pe" or not self.same_engine_sync:
                    continue
            if self.waited[eng].get(k, 0) >= v:
                continue
            self.waited[eng][k] = v
            self.streams[eng].append(("wait", k, v))

    def _mark(self, key, val, reads, writes):
        for t in writes:
            t.lw = (key, val)
            t.rd = {}
        for t in reads:
            if t.rd.get(key, 0) < val:
                t.rd[key] = val

    def op(self, eng, fn, reads=(), writes=()):
        deps = self._deps(eng, reads, writes)
        self._emit_waits(eng, deps)
        self.count[eng] += 1
        self.streams[eng].append(("op", fn, eng, 1))
        self._mark(eng, self.count[eng], reads, writes)
        self.n_ops += 1

    def dma(self, q, out, in_, reads=(), writes=()):
        deps = self._deps(q, reads, writes)
        pool = self.dma_pool[q]
        k = pool[self.dma_rr[q] % len(pool)]
        self.dma_rr[q] += 1
        if self.count[k] > 0:
            deps[k] = max(deps.get(k, 0), self.count[k])
        self._emit_waits(q, deps)
        self.count[k] += 16
        self.streams[q].append(("op", lambda e: e.dma_start(out=out, in_=in_), k, 16))
        self._mark(k, self.count[k], reads, writes)
        self.n_ops += 1

    def wait_all(self, eng, toks):
        deps = {}
        for t in toks:
            if t.lw is not None and deps.get(t.lw[0], 0) < t.lw[1]:
                deps[t.lw[0]] = t.lw[1]
        for k, v in deps.items():
            self.streams[eng].append(("wait", k, v))

    def emit(self):
        nc = self.nc
        with nc.Block() as block:
            def run(ename):
                def body(e):
                    for item in self.streams[ename]:
                        if item[0] == "wait":
                            e.wait_ge(self.sems[item[1]], item[2])
                        else:
                            _, fn, k, inc = item
                            fn(e).then_inc(self.sems[k], inc)
                return body
            block.tensor(run("pe"))
            block.scalar(run("act"))
            block.vector(run("dve"))
            block.gpsimd(run("pool"))
            block.sync(run("sp"))


def _run(nc, in_maps, tag=""):
    import sys, time
    t0 = time.time()
    res = run_bass_kernel_spmd(nc, in_maps, core_ids=list(range(NCORES)))
    print("[launch %s] %.1fs" % (tag, time.time() - t0), file=sys.stderr, flush=True)
    return res.results


ADA_N = 3 * D
ADA_PC = ADA_N // NCORES


def build_k0():
    nc = bass.Bass("TRN2", target_bir_lowering=False)
    cT = nc.dram_tensor("cT", [128, 16, 2], F32, kind="ExternalInput").ap()
    w = nc.dram_tensor("w", [DEPTH, D, ADA_PC], F32, kind="ExternalInput").ap()
    b = nc.dram_tensor("b", [DEPTH, 2, ADA_PC], F32, kind="ExternalInput").ap()
    o = nc.dram_tensor("o", [DEPTH, 2, ADA_PC], F32, kind="ExternalOutput").ap()
    with contextlib.ExitStack() as es:
        P = Prog(nc, es)
        sb = lambda name, shape, dt: es.enter_context(nc.sbuf_tensor(name, shape, dt))
        ps = lambda name, shape, dt: es.enter_context(nc.psum_tensor(name, shape, dt))
        c_sb = sb("c_sb", [128, 16, 2], F32)
        s_sb = sb("s_sb", [128, 16, 2], F32)
        b_sb = sb("b_sb", [2, DEPTH, ADA_PC], F32)
        o_sb = sb("o_sb", [2, DEPTH, ADA_PC], F32)
        NB = 4
        w_sb = [sb("w_sb%d" % i, [128, 4, ADA_PC], F32) for i in range(NB)]
        acc = [ps("acc%d" % i, [2, 512], F32) for i in range(4)]
        t_c, t_s, t_b, t_o = Tok(), Tok(), Tok(), Tok()
        t_w = [Tok() for _ in range(NB)]
        t_acc = [Tok() for _ in range(4)]
        P.dma("sp", c_sb[:], cT, writes=[t_c])
        P.dma("sp", b_sb[:], b.rearrange("l m n -> m l n"), writes=[t_b])
        P.op("act", lambda e: e.activation(out=s_sb[:], in_=c_sb[:], func=AF.Silu), reads=[t_c], writes=[t_s])
        wi = 0
        for l in range(DEPTH):
            wv = w[l].rearrange("(c p) n -> p c n", p=128)
            for g in range(4):
                bi = wi % NB
                wi += 1
                q = "sp" if (wi % 2) else "pool"
                P.dma(q, w_sb[bi][:], wv[:, g * 4:(g + 1) * 4, :], writes=[t_w[bi]])
                for kk in range(4):
                    kc = g * 4 + kk
                    for j, (n0, n1) in enumerate(((0, 512), (512, 768))):
                        a = acc[(l % 2) * 2 + j]
                        P.op("pe", lambda e, a=a, bi=bi, kk=kk, kc=kc, n0=n0, n1=n1: e.matmul(
                            a[:, 0:n1 - n0], lhsT=s_sb[:, kc, :], rhs=w_sb[bi][:, kk, n0:n1],
                            start=(kc == 0), stop=(kc == 15)),
                            reads=[t_s, t_w[bi]], writes=[t_acc[(l % 2) * 2 + j]])
            for j, (n0, n1) in enumerate(((0, 512), (512, 768))):
                a = acc[(l % 2) * 2 + j]
                P.op("dve", lambda e, a=a, l=l, n0=n0, n1=n1: e.tensor_tensor(
                    out=o_sb[:, l, n0:n1], in0=a[:, 0:n1 - n0], in1=b_sb[:, l, n0:n1], op=ALU.add),
                    reads=[t_acc[(l % 2) * 2 + j], t_b], writes=[t_o])
        P.dma("sp", o.rearrange("l m n -> m l n"), o_sb[:], reads=[t_o], writes=[t_o])
        P.wait_all("sp", [t_o])
        P.emit()
    return nc


def run_k0(inputs):
    c = np.asarray(inputs["c"], np.float32).reshape(D)
    cc = np.asarray(inputs["c_ctx"], np.float32).reshape(D)
    cm = np.stack([c, cc], axis=1)
    cT = np.ascontiguousarray(cm.reshape(16, 128, 2).transpose(1, 0, 2))
    w_ada = np.asarray(inputs["w_ada"], np.float32)
    b_ada = np.asarray(inputs["b_ada"], np.float32)
    nc = build_k0()
    in_maps = []
    for i in range(NCORES):
        sl = slice(i * ADA_PC, (i + 1) * ADA_PC)
        bb = np.ascontiguousarray(np.broadcast_to(b_ada[:, None, sl], (DEPTH, 2, ADA_PC)))
        in_maps.append({"cT": cT, "w": np.ascontiguousarray(w_ada[:, :, sl]), "b": bb})
    res = _run(nc, in_maps, "k0")
    ada = np.concatenate([r["o"] for r in res], axis=2)
    return ada


def _col_tiles():
    tiles = []
    def add(name, c0, n, step=128):
        for i in range(0, n, step):
            tiles.append((name, c0 + i, min(step, n - i)))
    add("qa", 0, 512); add("ka", 512, 256); add("va", 768, 256); add("z", 1024, 512)
    add("ux", 1536, 512); add("z", 2048, 512)
    add("uqkv", 2560, 1536); add("z", 4096, 512); add("uab", 4608, 16)
    add("cq", 4624, 384); add("ckv", 5008, 256); add("kr", 5264, 64); add("z", 5328, 512)
    return tiles
COL_TILES = _col_tiles()
NCT = len(COL_TILES)
TBLK = ((0, 512), (512, 1024), (1024, NT))


def build_k1():
    nc = bass.Bass("TRN2", target_bir_lowering=False)
    dt_in = lambda name, shape, dt=F32: nc.dram_tensor(name, shape, dt, kind="ExternalInput").ap()
    dt_out = lambda name, shape, dt=F32: nc.dram_tensor(name, shape, dt, kind="ExternalOutput").ap()
    x_d = dt_in("x", [NT, D])
    mod_d = dt_in("mod", [128, 5, 16])
    w_d = dt_in("w", [NCT, 128, 16, 128])
    gn_d = dt_in("gn", [128, 8])
    g2_d = dt_in("g2", [128, 4])
    wuq_d = dt_in("wuq", [128, 3, 768])
    wukv_d = dt_in("wukv", [128, 2, 1024])
    cs_d = dt_in("cs", [128, 2, NT])
    cs64_d = dt_in("cs64", [64, 2, NT])
    cst_d = dt_in("cst", [128, 3, 128], BF16)
    qa_o = dt_out("qa", [4, 128, NT], BF16)
    ka_o = dt_out("ka", [2, 128, NT], BF16)
    va_o = dt_out("va", [2, 128, NT], BF16)
    zs_o = dt_out("zs", [16, 128, NT])
    ux_o = dt_out("ux", [4, 128, NT])
    uqkv_o = dt_out("uqkv", [12, 128, NT])
    uab_o = dt_out("uab", [16, NT])
    qdn_o = dt_out("qdn", [4, 128, NT], BF16)
    qdr_o = dt_out("qdr", [4, 64, NT], BF16)
    kdn_o = dt_out("kdn", [4, 128, NT], BF16)
    kdr_o = dt_out("kdr", [4, 64, NT], BF16)
    vd_o = dt_out("vd", [4, 128, NT], BF16)
    out_toks = []
    with contextlib.ExitStack() as es:
        P = Prog(nc, es)
        sb = lambda name, shape, dt=F32: es.enter_context(nc.sbuf_tensor("s_" + name, shape, dt))
        ps = lambda name, shape, dt=F32: es.enter_context(nc.psum_tensor("p_" + name, shape, dt))
        mod = sb("mod", [128, 5, 16]); t_mod = Tok()
        gain = sb("gain", [128, 2, 16]); t_gain = Tok()
        gn = sb("gn", [128, 8]); g2 = sb("g2", [128, 4]); t_gn = Tok()
        gs = sb("gs", [128, 12]); t_gs = Tok()
        wuq_f = sb("wuq_f", [128, 3, 768]); wuq = sb("wuq", [128, 3, 768], BF16); t_wuq = Tok()
        wukv_f = sb("wukv_f", [128, 2, 1024]); wukv = sb("wukv", [128, 2, 1024], BF16); t_wukv = Tok()
        cs = sb("cs", [128, 2, NT]); cs64 = sb("cs64", [64, 2, NT]); t_cs = Tok()
        cst = sb("cst", [128, 3, 128], BF16); t_cst = Tok()
        ones = sb("ones", [128, 128], BF16); t_ones = Tok()
        hT = sb("hT", [128, 16, NT], BF16); t_hT = [Tok() for _ in range(9)]
        P.dma("sp", mod[:], mod_d, writes=[t_mod])
        P.dma("sp", gn[:], gn_d, writes=[t_gn])
        P.dma("sp", g2[:], g2_d, writes=[t_gn])
        P.dma("sp", cst[:], cst_d, writes=[t_cst])
        P.dma("sp", wuq_f[:], wuq_d, writes=[t_wuq])
        P.dma("sp", wukv_f[:], wukv_d, writes=[t_wukv])
        P.dma("sp", cs[:], cs_d, writes=[t_cs])
        P.dma("sp", cs64[:], cs64_d, writes=[t_cs])
        P.op("pool", lambda e: e.memset(ones[:], 1.0), writes=[t_ones])
        P.op("pool", lambda e: e.tensor_copy(out=wuq[:], in_=wuq_f[:]), reads=[t_wuq], writes=[t_wuq])
        P.op("pool", lambda e: e.tensor_copy(out=wukv[:], in_=wukv_f[:]), reads=[t_wukv], writes=[t_wukv])
        for m in range(2):
            P.op("dve", lambda e, m=m: e.scalar_tensor_tensor(
                out=gain[:, m, :], in0=mod[:, 1 + 2 * m, :], scalar=1.0, in1=mod[:, 0, :],
                op0=ALU.add, op1=ALU.mult), reads=[t_mod], writes=[t_gain])
        for (o0, o1, src, s0, fac) in ((0, 2, gn, 0, 128.0 ** 0.5), (2, 5, gn, 2, 384.0 ** 0.5),
                                        (5, 7, gn, 5, 256.0 ** 0.5), (7, 11, g2, 0, 192.0 ** 0.5)):
            P.op("dve", lambda e, o0=o0, o1=o1, src=src, s0=s0, fac=fac: e.tensor_scalar(
                out=gs[:, o0:o1], in0=src[:, s0:s0 + (o1 - o0)], scalar1=fac, scalar2=None, op0=ALU.mult),
                reads=[t_gn], writes=[t_gs])

        xt = [sb("xt%d" % i, [128, D]) for i in range(2)]; t_xt = [Tok(), Tok()]
        xn = [sb("xn%d" % i, [128, D], BF16) for i in range(2)]; t_xn = [Tok(), Tok()]
        ssq = sb("ssq", [128, 16]); t_ssq = Tok()
        tp_all = ps("tp", [128, 2, 4, 128], BF16); tp = [tp_all[:, 0], tp_all[:, 1]]; t_tp = [Tok()] * 2
        tpi = 0
        for ti in range(9):
            r0 = ti * 128
            np_ = 128 if ti < 8 else CTK
            m = 0 if ti < 8 else 1
            b = ti % 2
            P.dma("sp", xt[b][0:np_, :], x_d[r0:r0 + np_, :], writes=[t_xt[b]])
            P.op("act", lambda e, b=b, np_=np_, ti=ti: e.activation(
                out=xn[b][0:np_, :], in_=xt[b][0:np_, :], func=AF.Square, accum_out=ssq[0:np_, ti:ti + 1]),
                reads=[t_xt[b]], writes=[t_xn[b], t_ssq])
            P.op("dve", lambda e, np_=np_, ti=ti: e.tensor_scalar(
                out=ssq[0:np_, ti:ti + 1], in0=ssq[0:np_, ti:ti + 1], scalar1=1.0 / D, scalar2=EPS,
                op0=ALU.mult, op1=ALU.add), reads=[t_ssq], writes=[t_ssq])
            P.op("act", lambda e, np_=np_, ti=ti: e.activation(
                out=ssq[0:np_, ti:ti + 1], in_=ssq[0:np_, ti:ti + 1], func=AF.Sqrt), reads=[t_ssq], writes=[t_ssq])
            P.op("dve", lambda e, np_=np_, ti=ti: e.reciprocal(
                out=ssq[0:np_, ti:ti + 1], in_=ssq[0:np_, ti:ti + 1]), reads=[t_ssq], writes=[t_ssq])
            P.op("dve", lambda e, b=b, np_=np_, ti=ti: e.tensor_scalar(
                out=xn[b][0:np_, :], in0=xt[b][0:np_, :], scalar1=ssq[0:np_, ti:ti + 1], scalar2=None,
                op0=ALU.mult), reads=[t_xt[b], t_ssq], writes=[t_xn[b]])
            for cg in range(4):
                pb = tpi % 2; tpi += 1
                for j in range(4):
                    c = cg * 4 + j
                    P.op("pe", lambda e, pb=pb, j=j, c=c, b=b, np_=np_: e.transpose(
                        tp[pb][:, j, 0:np_], xn[b][0:np_, c * 128:(c + 1) * 128], cst[0:np_, 0, 0:np_]),
                        reads=[t_xn[b], t_cst], writes=[t_tp[pb]])
                for j in range(4):
                    c = cg * 4 + j
                    eng = "act" if j % 2 == 0 else "dve"
                    if eng == "act":
                        P.op("act", lambda e, pb=pb, j=j, c=c, r0=r0, np_=np_, m=m: e.activation(
                            out=hT[:, c, r0:r0 + np_], in_=tp[pb][:, j, 0:np_], func=AF.Identity,
                            scale=gain[:, m, c:c + 1], bias=mod[:, 2 + 2 * m, c:c + 1]),
                            reads=[t_tp[pb], t_gain, t_mod], writes=[t_hT[ti]])
                    else:
                        P.op("dve", lambda e, pb=pb, j=j, c=c, r0=r0, np_=np_, m=m: e.tensor_scalar(
                            out=hT[:, c, r0:r0 + np_], in0=tp[pb][:, j, 0:np_], scalar1=gain[:, m, c:c + 1],
                            scalar2=mod[:, 2 + 2 * m, c:c + 1], op0=ALU.mult, op1=ALU.add),
                            reads=[t_tp[pb], t_gain, t_mod], writes=[t_hT[ti]])

        NWB = 3
        wst = [sb("wst%d" % i, [128, 16, 128]) for i in range(2)]; t_wst = [Tok() for _ in range(2)]
        wbf = [sb("wbf%d" % i, [128, 16, 128], BF16) for i in range(NWB)]; t_wbf = [Tok() for _ in range(NWB)]
        pj = [ps("pj%d" % i, [128, 3, 512]) for i in range(2)]; t_pj = [Tok(), Tok()]
        aux = ps("aux", [128, 1, 512]); t_aux = Tok()
        raw = [sb("raw%d" % i, [128, NT]) for i in range(5)]; t_raw = [Tok() for _ in range(5)]
        sqb = [sb("sqb%d" % i, [128, NT], BF16) for i in range(3)]; t_sqb = [Tok() for _ in range(3)]
        rstd = sb("rstd", [128, NT]); t_rstd = Tok()
        nbf = [sb("nbf%d" % i, [128, NT], BF16) for i in range(4)]; t_nbf = [Tok() for _ in range(4)]
        t1 = sb("t1", [128, NT]); t_t1 = Tok()
        t2 = sb("t2", [128, NT]); t_t2 = Tok()
        ob = [sb("ob%d" % i, [128, NT], BF16) for i in range(2)]; t_ob = [Tok(), Tok()]
        of = [sb("of%d" % i, [128, NT]) for i in range(2)]; t_of = [Tok(), Tok()]
        kr_raw = sb("kr_raw", [64, NT]); t_kr = Tok()
        kr_sq = sb("kr_sq", [64, NT], BF16); t_krsq = Tok()
        cnt = {"ob": 0, "of": 0}

        def store_bf(dst_ap, rows, producer):
            i = cnt["ob"] % 2; cnt["ob"] += 1
            producer(ob[i], t_ob[i])
            tk = Tok(); out_toks.append(tk)
            P.dma("sp", dst_ap, ob[i][0:rows, :], reads=[t_ob[i]], writes=[tk])

        def store_f32(dst_ap, rows, producer):
            i = cnt["of"] % 2; cnt["of"] += 1
            producer(of[i], t_of[i])
            tk = Tok(); out_toks.append(tk)
            P.dma("sp", dst_ap, of[i][0:rows, :], reads=[t_of[i]], writes=[tk])

        def ss_to_rstd(pieces, n_eps):
            for bi, (a0, a1) in enumerate(TBLK):
                for pi, (sqt, tk, rows) in enumerate(pieces):
                    P.op("pe", lambda e, sqt=sqt, rows=rows, a0=a0, a1=a1, bi=bi, pi=pi: e.matmul(
                        aux[:, 0, 0:a1 - a0], lhsT=ones[0:rows, :], rhs=sqt[0:rows, a0:a1],
                        start=(pi == 0), stop=(pi == len(pieces) - 1)),
                        reads=[tk, t_ones], writes=[t_aux])
                P.op("dve", lambda e, a0=a0, a1=a1, bi=bi: e.tensor_scalar(
                    out=rstd[:, a0:a1], in0=aux[:, 0, 0:a1 - a0], scalar1=n_eps, scalar2=None,
                    op0=ALU.add), reads=[t_aux], writes=[t_rstd])
            P.op("act", lambda e: e.activation(out=rstd[:], in_=rstd[:], func=AF.Sqrt), reads=[t_rstd], writes=[t_rstd])
            P.op("dve", lambda e: e.reciprocal(out=rstd[:], in_=rstd[:]), reads=[t_rstd], writes=[t_rstd])

        def rope_store(dst_ap, rows, nb, tnb, cst_idx, cs_t):
            P.op("pool", lambda e: e.tensor_tensor(out=t1[0:rows, :], in0=nb[0:rows, :], in1=cs_t[0:rows, 0, :],
                                                    op=ALU.mult), reads=[tnb, t_cs], writes=[t_t1])
            for bi, (a0, a1) in enumerate(TBLK):
                P.op("pe", lambda e, a0=a0, a1=a1, bi=bi: e.matmul(
                    aux[0:rows, 0, 0:a1 - a0], lhsT=cst[0:rows, cst_idx, 0:rows], rhs=nb[0:rows, a0:a1],
                    start=True, stop=True), reads=[tnb, t_cst], writes=[t_aux])
                P.op("dve", lambda e, a0=a0, a1=a1, bi=bi: e.tensor_tensor(
                    out=t2[0:rows, a0:a1], in0=aux[0:rows, 0, 0:a1 - a0], in1=cs_t[0:rows, 1, a0:a1],
                    op=ALU.mult), reads=[t_aux, t_cs], writes=[t_t2])
            store_bf(dst_ap, rows, lambda o, to: P.op("pool", lambda e: e.tensor_tensor(
                out=o[0:rows, :], in0=t1[0:rows, :], in1=t2[0:rows, :], op=ALU.add),
                reads=[t_t1, t_t2], writes=[to]))

        zi = 0
        ci = {"qa": 0, "ka": 0, "va": 0, "ux": 0, "uqkv": 0, "cq": 0, "ckv": 0}
        for ct, (name, c0, ncol) in enumerate(COL_TILES):
            wb = ct % NWB
            ws = ct % 2
            P.dma("sp", wst[ws][:], w_d[ct], writes=[t_wst[ws]])
            P.op("pool", lambda e, wb=wb, ws=ws: e.tensor_copy(out=wbf[wb][:], in_=wst[ws][:]),
                 reads=[t_wst[ws]], writes=[t_wbf[wb]])
            pb = ct % 2
            for bi, (a0, a1) in enumerate(TBLK):
                tks = t_hT[0:4] if bi == 0 else (t_hT[4:8] if bi == 1 else t_hT[8:9])
                for c in range(16):
                    P.op("pe", lambda e, pb=pb, bi=bi, a0=a0, a1=a1, c=c, wb=wb, ncol=ncol: e.matmul(
                        pj[pb][0:ncol, bi, 0:a1 - a0], lhsT=wbf[wb][:, c, 0:ncol], rhs=hT[:, c, a0:a1],
                        start=(c == 0), stop=(c == 15)), reads=[t_wbf[wb]] + tks, writes=[t_pj[pb]])

            def evac(dst, tdst, rows, func=AF.Identity, pb=pb):
                for bi, (a0, a1) in enumerate(TBLK):
                    P.op("act", lambda e, bi=bi, a0=a0, a1=a1: e.activation(
                        out=dst[0:rows, a0:a1], in_=pj[pb][0:rows, bi, 0:a1 - a0], func=func),
                        reads=[t_pj[pb]], writes=[tdst])

            if name in ("qa", "ka"):
                i = ci[name]; ci[name] += 1
                evac(raw[0], t_raw[0], 128)
                evac(sqb[0], t_sqb[0], 128, AF.Square)
                ss_to_rstd([(sqb[0], t_sqb[0], 128)], 128 * EPS)
                gcol = 0 if name == "qa" else 1
                P.op("dve", lambda e, gcol=gcol: e.scalar_tensor_tensor(
                    out=nbf[0][:], in0=raw[0][:], scalar=gs[:, gcol:gcol + 1], in1=rstd[:],
                    op0=ALU.mult, op1=ALU.mult), reads=[t_raw[0], t_gs, t_rstd], writes=[t_nbf[0]])
                rope_store((qa_o if name == "qa" else ka_o)[i], 128, nbf[0], t_nbf[0], 1, cs)
            elif name == "va":
                i = ci[name]; ci[name] += 1
                store_bf(va_o[i], 128, lambda o, to: evac(o, to, 128))
            elif name == "z":
                store_f32(zs_o[zi], 128, lambda o, to: evac(o, to, 128, AF.Silu)); zi += 1
            elif name == "ux":
                i = ci[name]; ci[name] += 1
                store_f32(ux_o[i], 128, lambda o, to: evac(o, to, 128))
            elif name == "uqkv":
                i = ci[name]; ci[name] += 1
                store_f32(uqkv_o[i], 128, lambda o, to: evac(o, to, 128))
            elif name == "uab":
                store_f32(uab_o, 16, lambda o, to: evac(o, to, 16))
            elif name in ("cq", "ckv"):
                i = ci[name]; ci[name] += 1
                nch = 3 if name == "cq" else 2
                evac(raw[i], t_raw[i], 128)
                evac(sqb[i], t_sqb[i], 128, AF.Square)
                if i == nch - 1:
                    ss_to_rstd([(sqb[k], t_sqb[k], 128) for k in range(nch)], (384 if name == "cq" else 256) * EPS)
                    g0 = 2 if name == "cq" else 5
                    for k in range(nch):
                        P.op("dve", lambda e, k=k, g0=g0: e.scalar_tensor_tensor(
                            out=nbf[k][:], in0=raw[k][:], scalar=gs[:, g0 + k:g0 + k + 1], in1=rstd[:],
                            op0=ALU.mult, op1=ALU.mult), reads=[t_raw[k], t_gs, t_rstd], writes=[t_nbf[k]])
                    if name == "cq":
                        for h in range(4):
                            for (pbi, col0, rows) in ((0, h * 192, 128), (1, h * 192 + 128, 64)):
                                for bi, (a0, a1) in enumerate(TBLK):
                                    for k in range(3):
                                        P.op("pe", lambda e, pbi=pbi, col0=col0, rows=rows, bi=bi, a0=a0, a1=a1, k=k: e.matmul(
                                            pj[pbi][0:rows, bi, 0:a1 - a0], lhsT=wuq[:, k, col0:col0 + rows],
                                            rhs=nbf[k][:, a0:a1], start=(k == 0), stop=(k == 2)),
                                            reads=[t_wuq, t_nbf[k]], writes=[t_pj[pbi]])
                            evac(raw[3], t_raw[3], 128, pb=0)
                            evac(sqb[0], t_sqb[0], 128, AF.Square, pb=0)
                            evac(raw[4], t_raw[4], 64, pb=1)
                            evac(sqb[1], t_sqb[1], 64, AF.Square, pb=1)
                            ss_to_rstd([(sqb[0], t_sqb[0], 128), (sqb[1], t_sqb[1], 64)], 192 * EPS)
                            store_bf(qdn_o[h], 128, lambda o, to: P.op("dve", lambda e: e.scalar_tensor_tensor(
                                out=o[:], in0=raw[3][:], scalar=gs[:, 7:8], in1=rstd[:], op0=ALU.mult, op1=ALU.mult),
                                reads=[t_raw[3], t_gs, t_rstd], writes=[to]))
                            P.op("dve", lambda e: e.scalar_tensor_tensor(
                                out=nbf[3][0:64, :], in0=raw[4][0:64, :], scalar=gs[0:64, 8:9], in1=rstd[0:64, :],
                                op0=ALU.mult, op1=ALU.mult), reads=[t_raw[4], t_gs, t_rstd], writes=[t_nbf[3]])
                            rope_store(qdr_o[h], 64, nbf[3], t_nbf[3], 2, cs64)
            elif name == "kr":
                evac(kr_raw, t_kr, 64)
                evac(kr_sq, t_krsq, 64, AF.Square)
                for h in range(4):
                    for (pbi, col0) in ((0, h * 256), (1, h * 256 + 128)):
                        for bi, (a0, a1) in enumerate(TBLK):
                            for k in range(2):
                                P.op("pe", lambda e, pbi=pbi, col0=col0, bi=bi, a0=a0, a1=a1, k=k: e.matmul(
                                    pj[pbi][:, bi, 0:a1 - a0], lhsT=wukv[:, k, col0:col0 + 128],
                                    rhs=nbf[k][:, a0:a1], start=(k == 0), stop=(k == 1)),
                                    reads=[t_wukv, t_nbf[k]], writes=[t_pj[pbi]])
                    evac(raw[3], t_raw[3], 128, pb=0)
                    evac(sqb[0], t_sqb[0], 128, AF.Square, pb=0)
                    store_bf(vd_o[h], 128, lambda o, to: evac(o, to, 128, pb=1))
                    ss_to_rstd([(sqb[0], t_sqb[0], 128), (kr_sq, t_krsq, 64)], 192 * EPS)
                    store_bf(kdn_o[h], 128, lambda o, to: P.op("dve", lambda e: e.scalar_tensor_tensor(
                        out=o[:], in0=raw[3][:], scalar=gs[:, 9:10], in1=rstd[:], op0=ALU.mult, op1=ALU.mult),
                        reads=[t_raw[3], t_gs, t_rstd], writes=[to]))
                    P.op("dve", lambda e: e.scalar_tensor_tensor(
                        out=nbf[2][0:64, :], in0=kr_raw[0:64, :], scalar=gs[0:64, 10:11], in1=rstd[0:64, :],
                        op0=ALU.mult, op1=ALU.mult), reads=[t_kr, t_gs, t_rstd], writes=[t_nbf[2]])
                    rope_store(kdr_o[h], 64, nbf[2], t_nbf[2], 2, cs64)
        P.wait_all("sp", out_toks)
        P.emit()
    return nc


def _fm(v):
    v = np.asarray(v, np.float32)
    return np.ascontiguousarray(v.reshape(-1, 128).T)


def _rope_tables():
    t = np.arange(SEQ)
    rows = (t // 64).astype(np.float64)
    cols = (t % 64).astype(np.float64)

    def tab(half, reps):
        inv = 10000.0 ** (-np.arange(half, dtype=np.float64) / half)
        ar = rows[None, :] * inv[:, None]
        ac = cols[None, :] * inv[:, None]
        ang = np.concatenate([ar, ar, ac, ac], axis=0)
        return np.cos(ang), np.sin(ang)
    c128, s128 = tab(32, 4)
    c64, s64 = tab(16, 4)
    return (c128.astype(np.float32), s128.astype(np.float32), c64.astype(np.float32), s64.astype(np.float32))


def _rot_mats():
    def rot(n):
        q = n // 4
        R = np.zeros((n, n), np.float32)
        for base in (0, 2 * q):
            for i in range(q):
                R[base + i, base + q + i] = -1.0
                R[base + q + i, base + i] = 1.0
        return R
    cst = np.zeros((128, 3, 128), np.float32)
    cst[:, 0, :] = np.eye(128, dtype=np.float32)
    cst[:, 1, :] = rot(128).T
    cst[0:64, 2, 0:64] = rot(64).T
    return cst.astype(NPBF)


_CONST = {}


def _consts():
    if not _CONST:
        _CONST["rope"] = _rope_tables()
        _CONST["cst"] = _rot_mats()
    return _CONST


def prep_k1(inputs, l, x_full, ctx_full, ada):
    cs = _consts()
    c128, s128, c64, s64 = cs["rope"]
    w_in = np.asarray(inputs["w_in"][l], np.float32)
    wt = np.zeros((NCT, 128, 16, 128), np.float32)
    for ct, (name, c0, ncol) in enumerate(COL_TILES):
        wt[ct, :, :, 0:ncol] = w_in[:, c0:c0 + ncol].reshape(16, 128, ncol).transpose(1, 0, 2)
    mod = np.stack([_fm(inputs["norm_w"][l]), _fm(ada[l, 0, 2048:4096]), _fm(ada[l, 0, 0:2048]),
                    _fm(ada[l, 1, 2048:4096]), _fm(ada[l, 1, 0:2048])], axis=1)
    gn = np.zeros((128, 8), np.float32)
    gn[:, 0] = inputs["attn_q_norm"][l]
    gn[:, 1] = inputs["attn_k_norm"][l]
    gn[:, 2:5] = _fm(inputs["mla_q_norm"][l])
    gn[:, 5:7] = _fm(inputs["mla_kv_norm"][l])
    g2 = np.zeros((128, 4), np.float32)
    g2[:, 0] = inputs["mla_q_qk_norm"][l][0:128]
    g2[0:64, 1] = inputs["mla_q_qk_norm"][l][128:192]
    g2[:, 2] = inputs["mla_k_qk_norm"][l][0:128]
    g2[0:64, 3] = inputs["mla_k_qk_norm"][l][128:192]
    wuq = np.ascontiguousarray(np.asarray(inputs["mla_w_uq"][l], np.float32).reshape(3, 128, 768).transpose(1, 0, 2))
    wukv = np.ascontiguousarray(np.asarray(inputs["mla_w_ukv"][l], np.float32).reshape(2, 128, 1024).transpose(1, 0, 2))
    in_maps = []
    for i in range(NCORES):
        xs = np.concatenate([x_full[i * TOK:(i + 1) * TOK], ctx_full[i * CTK:(i + 1) * CTK]], axis=0)
        csa = np.zeros((128, 2, NT), np.float32)
        csa[:, 0, 0:TOK] = c128[:, i * TOK:(i + 1) * TOK]
        csa[:, 1, 0:TOK] = s128[:, i * TOK:(i + 1) * TOK]
        csa[:, 0, TOK:] = 1.0
        csb = np.zeros((64, 2, NT), np.float32)
        csb[:, 0, 0:TOK] = c64[:, i * TOK:(i + 1) * TOK]
        csb[:, 1, 0:TOK] = s64[:, i * TOK:(i + 1) * TOK]
        csb[:, 0, TOK:] = 1.0
        in_maps.append({"x": np.ascontiguousarray(xs, np.float32), "mod": mod, "w": wt, "gn": gn, "g2": g2,
                        "wuq": wuq, "wukv": wukv, "cs": csa, "cs64": csb, "cst": cs["cst"]})
    return in_maps


NKT = NALL // 128
VW = 136


def build_k2a():
    nc = bass.Bass("TRN2", target_bir_lowering=False)
    dt_in = lambda name, shape, dt=F32: nc.dram_tensor(name, shape, dt, kind="ExternalInput").ap()
    dt_out = lambda name, shape, dt=F32: nc.dram_tensor(name, shape, dt, kind="ExternalOutput").ap()
    qm_d = dt_in("qm", [128, 8, NT], BF16)
    qr_d = dt_in("qr", [64, 4, NT], BF16)
    km_d = dt_in("km", [6, 128, NALL], BF16)
    kr_d = dt_in("kr", [4, 64, NALL], BF16)
    va_d = dt_in("va", [6, 128, NKT, VW], BF16)
    o_d = dt_out("o", [8, NT, 128])
    out_toks = []
    with contextlib.ExitStack() as es:
        P = Prog(nc, es)
        sb = lambda name, shape, dt=F32: es.enter_context(nc.sbuf_tensor("s_" + name, shape, dt))
        ps = lambda name, shape, dt=F32: es.enter_context(nc.psum_tensor("p_" + name, shape, dt))
        qm = sb("qm", [128, 8, NT], BF16); qr = sb("qr", [64, 4, NT], BF16); t_q = Tok()
        km = [sb("km%d" % i, [128, NALL], BF16) for i in range(2)]; t_km = [Tok(), Tok()]
        kr = [sb("kr%d" % i, [64, NALL], BF16) for i in range(2)]; t_kr = [Tok(), Tok()]
        va = [sb("va%d" % i, [128, NKT, VW], BF16) for i in range(2)]; t_va = [Tok(), Tok()]
        NS = 3
        s_ps = [ps("s%d" % i, [128, 512]) for i in range(NS)]; t_s = [Tok() for _ in range(NS)]
        pT = [sb("pT%d" % i, [128, 512], BF16) for i in range(NS)]; t_p = [Tok() for _ in range(NS)]
        o_ps = [ps("o%d" % i, [128, 512]) for i in range(4)]; t_o = [Tok() for _ in range(4)]
        rcp = sb("rcp", [128, 4]); t_rcp = Tok()
        o_sb = [sb("osb%d" % i, [128, 4, 128]) for i in range(2)]; t_osb = [Tok(), Tok()]
        P.dma("sp", qm[:], qm_d, writes=[t_q])
        P.dma("sp", qr[:], qr_d, writes=[t_q])
        slot = -1
        kvi = 0
        nfin = 0
        for h in range(8):
            mla = h >= 4
            if mla or h % 2 == 0:
                slot += 1
                sl = slot % 2
                P.dma("sp", km[sl][:], km_d[kvi], writes=[t_km[sl]])
                if mla:
                    P.dma("sp", kr[sl][:], kr_d[h - 4], writes=[t_kr[sl]])
                P.dma("sp", va[sl][:], va_d[kvi], writes=[t_va[sl]])
                kvi += 1
            scale = (192.0 if mla else 128.0) ** -0.5
            def do_tile(h, sl, mla, scale, q0, q1, nkt, ob):
                nq = q1 - q0
                nj = (nq + 127) // 128

                def S(kt):
                    i = kt % NS
                    rd = [t_q, t_km[sl]] + ([t_kr[sl]] if mla else [])
                    P.op("pe", lambda e, i=i, kt=kt: e.matmul(
                        s_ps[i][:, 0:nq], lhsT=km[sl][:, kt * 128:(kt + 1) * 128], rhs=qm[:, h, q0:q1],
                        start=True, stop=not mla), reads=rd, writes=[t_s[i]])
                    if mla:
                        P.op("pe", lambda e, i=i, kt=kt: e.matmul(
                            s_ps[i][:, 0:nq], lhsT=kr[sl][:, kt * 128:(kt + 1) * 128], rhs=qr[:, h - 4, q0:q1],
                            start=False, stop=True), reads=rd, writes=[t_s[i]])

                def E(kt):
                    i = kt % NS
                    P.op("act", lambda e, i=i: e.activation(out=pT[i][:, 0:nq], in_=s_ps[i][:, 0:nq], func=AF.Exp,
                                                            scale=scale), reads=[t_s[i]], writes=[t_p[i]])

                def PV(kt):
                    i = kt % NS
                    for j in range(nj):
                        m = min(128, nq - j * 128)
                        P.op("pe", lambda e, i=i, j=j, m=m, kt=kt: e.matmul(
                            o_ps[j][0:m, 0:129], lhsT=pT[i][:, j * 128:j * 128 + m], rhs=va[sl][:, kt, 0:129],
                            start=(kt == 0), stop=(kt == nkt - 1)), reads=[t_p[i], t_va[sl]], writes=[t_o[j]])

                for kt in range(min(NS, nkt)):
                    S(kt)
                    E(kt)
                for kt in range(nkt):
                    PV(kt)
                    if kt + NS < nkt:
                        S(kt + NS)
                        E(kt + NS)
                for j in range(nj):
                    m = min(128, nq - j * 128)
                    P.op("dve", lambda e, j=j, m=m: e.reciprocal(out=rcp[0:m, j:j + 1], in_=o_ps[j][0:m, 128:129]),
                         reads=[t_o[j]], writes=[t_rcp])
                    P.op("dve", lambda e, j=j, m=m, ob=ob: e.tensor_scalar(
                        out=o_sb[ob][0:m, j, :], in0=o_ps[j][0:m, 0:128], scalar1=rcp[0:m, j:j + 1], scalar2=None,
                        op0=ALU.mult), reads=[t_o[j], t_rcp], writes=[t_osb[ob]])
                tk = Tok(); out_toks.append(tk)
                if nq == 512:
                    P.dma("sp", o_d[h, q0:q1, :].rearrange("(j p) d -> p j d", p=128), o_sb[ob][:],
                          reads=[t_osb[ob]], writes=[tk])
                else:
                    P.dma("sp", o_d[h, q0:q1, :], o_sb[ob][0:nq, 0, :], reads=[t_osb[ob]], writes=[tk])

            for (q0, q1, nkt) in ((0, 512, NKT), (512, 1024, NKT), (1024, NT, 2)):
                do_tile(h, sl, mla, scale, q0, q1, nkt, nfin % 2)
                nfin += 1
        P.wait_all("sp", out_toks)
        P.emit()
    return nc


SEGS = ((0, CTX), (CTX, NALL))
BLK512 = [(i, min(i + 512, NALL)) for i in range(0, NALL, 512)]


def emit_conv5(P, eng_a, eng_b, y, ty, x, tx, w, tw, wcol0, bias_ap=None):
    for (s0, s1) in SEGS:
        if bias_ap is None:
            P.op(eng_a, lambda e, s0=s0, s1=s1: e.tensor_scalar(
                out=y[:, s0:s1], in0=x[:, s0:s1], scalar1=w[:, wcol0 + 2:wcol0 + 3], scalar2=None, op0=ALU.mult),
                reads=[tx, tw], writes=[ty])
        else:
            P.op(eng_a, lambda e, s0=s0, s1=s1: e.tensor_scalar(
                out=y[:, s0:s1], in0=x[:, s0:s1], scalar1=w[:, wcol0 + 2:wcol0 + 3], scalar2=bias_ap,
                op0=ALU.mult, op1=ALU.add), reads=[tx, tw], writes=[ty])
        for o in (-2, -1, 1, 2):
            a0 = s0 + max(0, -o)
            a1 = s1 - max(0, o)
            P.op(eng_a, lambda e, a0=a0, a1=a1, o=o: e.scalar_tensor_tensor(
                out=y[:, a0:a1], in0=x[:, a0 + o:a1 + o], scalar=w[:, wcol0 + o + 2:wcol0 + o + 3], in1=y[:, a0:a1],
                op0=ALU.mult, op1=ALU.add), reads=[tx, tw, ty], writes=[ty])


def build_k2b():
    nc = bass.Bass("TRN2", target_bir_lowering=False)
    dt_in = lambda name, shape, dt=F32: nc.dram_tensor(name, shape, dt, kind="ExternalInput").ap()
    dt_out = lambda name, shape, dt=F32: nc.dram_tensor(name, shape, dt, kind="ExternalOutput").ap()
    ux_d = dt_in("ux", [128, NALL])
    prm_d = dt_in("prm", [128, 16])
    w_d = dt_in("w", [128, 2, 128])
    h_d = dt_out("h", [128, NALL])
    with contextlib.ExitStack() as es:
        P = Prog(nc, es)
        sb = lambda name, shape, dt=F32: es.enter_context(nc.sbuf_tensor("s_" + name, shape, dt))
        ps = lambda name, shape, dt=F32: es.enter_context(nc.psum_tensor("p_" + name, shape, dt))
        ux = sb("ux", [128, NALL]); t_ux = Tok()
        xs = sb("xs", [128, NALL]); t_xs = Tok()
        xb = sb("xb", [128, NALL], BF16); t_xb = Tok()
        av = sb("av", [128, NALL]); t_av = Tok()
        bv = sb("bv", [128, NALL]); t_bv = Tok()
        prm = sb("prm", [128, 16]); t_prm = Tok()
        c1 = sb("c1", [128, 2]); t_c1 = Tok()
        wf = sb("wf", [128, 2, 128]); wb = sb("wb", [128, 2, 128], BF16); t_w = Tok()
        rr = [sb("rr%d" % i, [128, 512]) for i in range(2)]; t_rr = [Tok(), Tok()]
        ii = [sb("ii%d" % i, [128, 512]) for i in range(2)]; t_ii = [Tok(), Tok()]
        pr = [ps("pr%d" % i, [128, 512]) for i in range(2)]; t_pr = [Tok(), Tok()]
        pi = [ps("pi%d" % i, [128, 512]) for i in range(2)]; t_pi = [Tok(), Tok()]
        P.dma("sp", ux[:], ux_d, writes=[t_ux])
        P.dma("sp", prm[:], prm_d, writes=[t_prm])
        P.dma("sp", wf[:], w_d, writes=[t_w])
        P.op("pool", lambda e: e.tensor_copy(out=wb[:], in_=wf[:]), reads=[t_w], writes=[t_w])
        P.op("act", lambda e: e.activation(out=c1[:, 0:1], in_=prm[:, 8:9], func=AF.Exp, scale=-1.0), reads=[t_prm], writes=[t_c1])
        P.op("act", lambda e: e.activation(out=c1[:, 0:1], in_=c1[:, 0:1], func=AF.Ln, bias=1.0), reads=[t_c1], writes=[t_c1])
        P.op("dve", lambda e: e.tensor_scalar(out=c1[:, 1:2], in0=c1[:, 0:1], scalar1=-16.0, scalar2=None, op0=ALU.mult), reads=[t_c1], writes=[t_c1])
        P.op("dve", lambda e: e.tensor_scalar(out=c1[:, 0:1], in0=c1[:, 0:1], scalar1=-8.0, scalar2=None, op0=ALU.mult), reads=[t_c1], writes=[t_c1])
        emit_conv5(P, "dve", "pool", xs, t_xs, ux, t_ux, prm, t_prm, 0, bias_ap=prm[:, 5:6])
        P.op("pool", lambda e: e.tensor_copy(out=xb[:], in_=xs[:]), reads=[t_xs], writes=[t_xb])
        for bi, (a0, a1) in enumerate(BLK512):
            n = a1 - a0
            b = bi % 2
            P.op("pe", lambda e, b=b, a0=a0, a1=a1, n=n: e.matmul(pr[b][:, 0:n], lhsT=wb[:, 0, :], rhs=xb[:, a0:a1], start=True, stop=True),
                 reads=[t_w, t_xb], writes=[t_pr[b]])
            P.op("pe", lambda e, b=b, a0=a0, a1=a1, n=n: e.matmul(pi[b][:, 0:n], lhsT=wb[:, 1, :], rhs=xb[:, a0:a1], start=True, stop=True),
                 reads=[t_w, t_xb], writes=[t_pi[b]])
            P.op("act", lambda e, b=b, n=n: e.activation(out=rr[b][:, 0:n], in_=pr[b][:, 0:n], func=AF.Sigmoid, bias=prm[:, 6:7]),
                 reads=[t_pr[b], t_prm], writes=[t_rr[b]])
            P.op("act", lambda e, b=b, n=n: e.activation(out=ii[b][:, 0:n], in_=pi[b][:, 0:n], func=AF.Sigmoid, bias=prm[:, 7:8]),
                 reads=[t_pi[b], t_prm], writes=[t_ii[b]])
            P.op("act", lambda e, b=b, a0=a0, a1=a1, n=n: e.activation(out=av[:, a0:a1], in_=rr[b][:, 0:n], func=AF.Exp, scale=c1[:, 0:1]),
                 reads=[t_rr[b], t_c1], writes=[t_av])
            P.op("act", lambda e, b=b, n=n: e.activation(out=rr[b][:, 0:n], in_=rr[b][:, 0:n], func=AF.Exp, scale=c1[:, 1:2]),
                 reads=[t_rr[b], t_c1], writes=[t_rr[b]])
            P.op("act", lambda e, b=b, n=n: e.activation(out=rr[b][:, 0:n], in_=rr[b][:, 0:n], func=AF.Sqrt, scale=-1.0, bias=1.0),
                 reads=[t_rr[b]], writes=[t_rr[b]])
            P.op("dve", lambda e, b=b, a0=a0, a1=a1, n=n: e.tensor_tensor(out=ii[b][:, 0:n], in0=ii[b][:, 0:n], in1=xs[:, a0:a1], op=ALU.mult),
                 reads=[t_ii[b], t_xs], writes=[t_ii[b]])
            P.op("dve", lambda e, b=b, a0=a0, a1=a1, n=n: e.tensor_tensor(out=bv[:, a0:a1], in0=ii[b][:, 0:n], in1=rr[b][:, 0:n], op=ALU.mult),
                 reads=[t_ii[b], t_rr[b]], writes=[t_bv])
        SC = 2112
        for i, s0 in enumerate(range(0, NALL, SC)):
            init = 0.0 if i == 0 else ux[:, s0 - 1:s0]
            P.op("dve", lambda e, s0=s0, init=init: e.tensor_tensor_scan(
                out=ux[:, s0:s0 + SC], data0=av[:, s0:s0 + SC], data1=bv[:, s0:s0 + SC], initial=init,
                op0=ALU.mult, op1=ALU.add), reads=[t_av, t_bv, t_ux], writes=[t_ux])
        tk = Tok()
        P.dma("sp", h_d, ux[:], reads=[t_ux], writes=[tk])
        P.wait_all("sp", [tk])
        P.emit()
    return nc


NCH = NALL // 64


def build_c1():
    nc = bass.Bass("TRN2", target_bir_lowering=False)
    dt_in = lambda name, shape, dt=F32: nc.dram_tensor(name, shape, dt, kind="ExternalInput").ap()
    dt_out = lambda name, shape, dt=F32: nc.dram_tensor(name, shape, dt, kind="ExternalOutput").ap()
    u_d = dt_in("u", [3, 128, NALL])
    cw_d = dt_in("cw", [128, 16])
    ab_d = dt_in("ab", [64, 2, NCH])
    sc_d = dt_in("sc", [64, 2])
    o_d = dt_out("qkv", [3, 128, NALL])
    bg_d = dt_out("bg", [64, 2, NCH])
    with contextlib.ExitStack() as es:
        P = Prog(nc, es)
        sb = lambda name, shape, dt=F32: es.enter_context(nc.sbuf_tensor("s_" + name, shape, dt))
        ps = lambda name, shape, dt=F32: es.enter_context(nc.psum_tensor("p_" + name, shape, dt))
        x = [sb("x%d" % i, [128, NALL]) for i in range(2)]; t_x = [Tok(), Tok()]
        y = [sb("y%d" % i, [128, NALL]) for i in range(2)]; t_y = [Tok(), Tok()]
        sq = sb("sq", [128, NALL]); t_sq = Tok()
        cw = sb("cw", [128, 16]); t_cw = Tok()
        ones = sb("ones", [128, 128]); t_ones = Tok()
        ab = sb("ab", [64, 2, NCH]); t_ab = Tok()
        sc = sb("sc", [64, 2]); t_sc = Tok()
        bg = sb("bg", [64, 2, NCH]); t_bg = Tok()
        pp = [ps("pp%d" % i, [128, 512]) for i in range(2)]; t_pp = [Tok(), Tok()]
        P.dma("sp", cw[:], cw_d, writes=[t_cw])
        P.dma("sp", ab[:], ab_d, writes=[t_ab])
        P.dma("sp", sc[:], sc_d, writes=[t_sc])
        P.op("pool", lambda e: e.memset(ones[:], 1.0), writes=[t_ones])
        P.op("act", lambda e: e.activation(out=bg[:, 0, :], in_=ab[:, 0, :], func=AF.Sigmoid), reads=[t_ab], writes=[t_bg])
        P.op("act", lambda e: e.activation(out=bg[:, 1, :], in_=ab[:, 1, :], func=AF.Exp, bias=sc[:, 1:2]), reads=[t_ab, t_sc], writes=[t_bg])
        P.op("act", lambda e: e.activation(out=bg[:, 1, :], in_=bg[:, 1, :], func=AF.Ln, bias=1.0), reads=[t_bg], writes=[t_bg])
        P.op("act", lambda e: e.activation(out=sc[:, 0:1], in_=sc[:, 0:1], func=AF.Exp), reads=[t_sc], writes=[t_sc])
        P.op("dve", lambda e: e.tensor_scalar(out=bg[:, 1, :], in0=bg[:, 1, :], scalar1=sc[:, 0:1], scalar2=-1.0,
                                              op0=ALU.mult, op1=ALU.mult), reads=[t_bg, t_sc], writes=[t_bg])
        tk_bg = Tok()
        P.dma("sp", bg_d, bg[:], reads=[t_bg], writes=[tk_bg])
        outs = [tk_bg]
        for i in range(3):
            b = i % 2
            P.dma("sp", x[b][:], u_d[i], writes=[t_x[b]])
            emit_conv5(P, "dve", "pool", y[b], t_y[b], x[b], t_x[b], cw, t_cw, 5 * i)
            P.op("act", lambda e, b=b: e.activation(out=y[b][:], in_=y[b][:], func=AF.Silu), reads=[t_y[b]], writes=[t_y[b]])
            if i < 2:
                P.op("pool", lambda e, b=b: e.tensor_tensor(out=sq[:], in0=y[b][:], in1=y[b][:], op=ALU.mult),
                     reads=[t_y[b]], writes=[t_sq])
                for bi, (a0, a1) in enumerate(BLK512):
                    n = a1 - a0
                    pb = bi % 2
                    P.op("pe", lambda e, pb=pb, a0=a0, a1=a1, n=n: e.matmul(pp[pb][:, 0:n], lhsT=ones[:], rhs=sq[:, a0:a1],
                                                                      start=True, stop=True), reads=[t_ones, t_sq], writes=[t_pp[pb]])
                    P.op("act", lambda e, pb=pb, a0=a0, a1=a1, n=n, b=b: e.activation(out=x[b][:, a0:a1], in_=pp[pb][:, 0:n], func=AF.Sqrt, bias=EPS),
                         reads=[t_pp[pb]], writes=[t_x[b]])
                P.op("dve", lambda e, b=b: e.reciprocal(out=x[b][:], in_=x[b][:]), reads=[t_x[b]], writes=[t_x[b]])
                fac = (128.0 ** -0.5) if i == 0 else 1.0
                P.op("dve", lambda e, b=b, fac=fac: e.scalar_tensor_tensor(out=y[b][:], in0=y[b][:], scalar=fac, in1=x[b][:],
                                                                       op0=ALU.mult, op1=ALU.mult), reads=[t_y[b], t_x[b]], writes=[t_y[b]])
            tk = Tok(); outs.append(tk)
            P.dma("sp", o_d[i], y[b][:], reads=[t_y[b]], writes=[tk])
        P.wait_all("sp", outs)
        P.emit()
    return nc


GRP = 8


def _dn_consts():
    c = np.zeros((64, 6, 64), np.float32)
    p = np.arange(64)[:, None]
    f = np.arange(64)[None, :]
    c[:, 0] = (p <= f)
    c[:, 1] = -(p <= f).astype(np.float32)
    c[:, 2] = -(p > f).astype(np.float32)
    c[:, 3] = -(f > p).astype(np.float32)
    c[:, 4] = (f >= p)
    c[:, 5] = (f == p)
    return c


def build_c2():
    nc = bass.Bass("TRN2", target_bir_lowering=False)
    dt_in = lambda name, shape, dt=F32: nc.dram_tensor(name, shape, dt, kind="ExternalInput").ap()
    dt_out = lambda name, shape, dt=F32: nc.dram_tensor(name, shape, dt, kind="ExternalOutput").ap()
    qT_d = dt_in("qT", [128, NALL]); kT_d = dt_in("kT", [128, NALL])
    ktm_d = dt_in("ktm", [64, NCH, 128]); vtm_d = dt_in("vtm", [64, NCH, 128])
    bg_d = dt_in("bg", [64, 2, NCH]); bbc_d = dt_in("bbc", [64, NALL]); cst_d = dt_in("cst", [64, 6, 64])
    o_d = dt_out("o", [64, NCH, 128])
    with contextlib.ExitStack() as es:
        P = Prog(nc, es)
        sb = lambda name, shape, dt=F32: es.enter_context(nc.sbuf_tensor("s_" + name, shape, dt))
        ps = lambda name, shape, dt=F32: es.enter_context(nc.psum_tensor("p_" + name, shape, dt))
        cst = sb("cst", [64, 6, 64]); t_cst = Tok()
        bg = sb("bg", [64, 2, NCH]); t_bg = Tok()
        bbc = sb("bbc", [64, NALL]); t_bbc = Tok()
        ones = sb("ones", [64, 128]); t_ones = Tok()
        gb = sb("gb", [64, NCH, 64]); t_gb = Tok()
        gc = sb("gc", [64, NCH]); eg = sb("eg", [64, NCH]); beg = sb("beg", [64, NCH]); dk = sb("dk", [64, NCH]); t_gc = Tok()
        egl = sb("egl", [128, NCH]); t_egl = Tok()
        S = sb("S", [128, 128]); t_S = Tok()
        qT = [sb("qT%d" % i, [128, GRP * 64]) for i in range(2)]; kT = [sb("kT%d" % i, [128, GRP * 64]) for i in range(2)]
        ktm = [sb("ktm%d" % i, [64, GRP, 128]) for i in range(2)]; vtm = [sb("vtm%d" % i, [64, GRP, 128]) for i in range(2)]
        t_in = [Tok(), Tok()]
        E8 = sb("E8", [64, GRP, 64]); ET8 = sb("ET8", [64, GRP, 64]); t_E = Tok()
        Pa = sb("Pa", [64, GRP, 64]); Pb = sb("Pb", [64, GRP, 64]); Pta = sb("Pta", [64, GRP, 64]); Ptb = sb("Ptb", [64, GRP, 64])
        t_P = [Tok(), Tok()]; t_Pt = [Tok(), Tok()]
        Tt = sb("Tt", [64, GRP, 64]); t_Tt = Tok()
        Rv = sb("Rv", [64, GRP, 128]); Rw = sb("Rw", [64, GRP, 128]); t_R = Tok()
        QK = [sb("QK%d" % i, [64, GRP, 64]) for i in range(2)]; kd = [sb("kd%d" % i, [64, GRP, 128]) for i in range(2)]
        U8 = [sb("U8%d" % i, [64, GRP, 128]) for i in range(2)]; W8 = [sb("W8%d" % i, [128, GRP, 64]) for i in range(2)]
        t_prep = [Tok(), Tok()]
        o_sb = [sb("osb%d" % i, [64, GRP, 128]) for i in range(2)]; t_osb = [Tok(), Tok()]
        vn = [sb("vn%d" % i, [64, 128]) for i in range(2)]; t_vn = [Tok(), Tok()]
        tmp = [sb("tmp%d" % i, [64, 128]) for i in range(2)]; t_tmp = [Tok(), Tok()]
        g0 = ps("g0", [64, GRP, 64]); g1 = ps("g1", [64, GRP, 64]); g2 = ps("g2", [64, GRP, 64]); t_g = [Tok(), Tok(), Tok()]
        gU = ps("gU", [64, GRP, 128]); t_gU = Tok()
        gW = ps("gW", [128, GRP, 64]); t_gW = Tok()
        q3 = gW[0:64, 0:2, :].rearrange("p a b -> p (a b)"); t_q3 = t_gW
        q12 = ps("q12", [64, 2, 128]); t_q12 = Tok()
        q4 = ps("q4", [128, 128]); t_q4 = Tok()
        P.dma("sp", cst[:], cst_d, writes=[t_cst])
        P.dma("sp", bg[:], bg_d, writes=[t_bg])
        P.dma("sp", bbc[:], bbc_d, writes=[t_bbc])
        P.op("pool", lambda e: e.memset(ones[:], 1.0), writes=[t_ones])
        P.op("pool", lambda e: e.memset(S[:], 0.0), writes=[t_S])
        P.op("dve", lambda e: e.tensor_copy(out=gb[:], in_=bg[:, 1, :].unsqueeze(2).broadcast_to([64, NCH, 64])),
             reads=[t_bg], writes=[t_gb])
        P.op("pe", lambda e: e.matmul(g0[:, 0:3, :].rearrange("p a b -> p (a b)")[:, 0:NCH], lhsT=cst[:, 0, :], rhs=bg[:, 1, :],
                                      start=True, stop=True), reads=[t_cst, t_bg], writes=[t_g[0]])
        P.op("pe", lambda e: e.matmul(gW[:, 0:3, :].rearrange("p a b -> p (a b)")[:, 0:NCH], lhsT=ones[:, :], rhs=bg[:, 1, :],
                                      start=True, stop=True), reads=[t_ones, t_bg], writes=[t_gW])
        g0f = g0[:, 0:3, :].rearrange("p a b -> p (a b)")[:, 0:NCH]
        gWf = gW[:, 0:3, :].rearrange("p a b -> p (a b)")[:, 0:NCH]
        P.op("dve", lambda e: e.tensor_copy(out=gc[:], in_=g0f), reads=[t_g[0]], writes=[t_gc])
        P.op("act", lambda e: e.activation(out=eg[:], in_=g0f, func=AF.Exp), reads=[t_g[0]], writes=[t_gc])
        P.op("act", lambda e: e.activation(out=egl[:], in_=gWf, func=AF.Exp), reads=[t_gW], writes=[t_egl])
        P.op("dve", lambda e: e.tensor_tensor(out=dk[:], in0=gWf[0:64, :], in1=gc[:], op=ALU.subtract), reads=[t_gW, t_gc], writes=[t_gc])
        P.op("act", lambda e: e.activation(out=dk[:], in_=dk[:], func=AF.Exp), reads=[t_gc], writes=[t_gc])
        P.op("dve", lambda e: e.tensor_tensor(out=beg[:], in0=eg[:], in1=bg[:, 0, :], op=ALU.mult), reads=[t_gc, t_bg], writes=[t_gc])

        def bc_last(ap2d, n):
            return ap2d.unsqueeze(2).broadcast_to([64, ap2d.shape[1], n])

        def bc_mid(ap2d, g):
            return ap2d.unsqueeze(1).broadcast_to([64, g, ap2d.shape[1]])

        def do_group(gi, n0, G):
            b = gi % 2
            c0, c1 = n0 * 64, (n0 + G) * 64
            P.dma("sp", qT[b][:, 0:G * 64], qT_d[:, c0:c1], writes=[t_in[b]])
            P.dma("sp", kT[b][:, 0:G * 64], kT_d[:, c0:c1], writes=[t_in[b]])
            P.dma("sp", ktm[b][:, 0:G, :], ktm_d[:, n0:n0 + G, :], writes=[t_in[b]])
            P.dma("sp", vtm[b][:, 0:G, :], vtm_d[:, n0:n0 + G, :], writes=[t_in[b]])
            for j in range(G):
                n = n0 + j
                ks = kT[b][:, j * 64:(j + 1) * 64]
                P.op("pe", lambda e, j=j, n=n: e.matmul(g0[:, j, :], lhsT=cst[:, 0, :], rhs=gb[:, n, :], start=True, stop=False),
                     reads=[t_cst, t_gb], writes=[t_g[0]])
                P.op("pe", lambda e, j=j, n=n: e.matmul(g0[:, j, :], lhsT=gb[:, n, :], rhs=cst[:, 1, :], start=False, stop=True),
                     reads=[t_cst, t_gb], writes=[t_g[0]])
                P.op("pe", lambda e, j=j, ks=ks: e.matmul(g1[:, j, :], lhsT=ks, rhs=ks, start=True, stop=True),
                     reads=[t_in[b]], writes=[t_g[1]])
                P.op("pe", lambda e, j=j, ks=ks: e.matmul(g2[:, j, :], lhsT=ks, rhs=qT[b][:, j * 64:(j + 1) * 64], start=True, stop=True),
                     reads=[t_in[b]], writes=[t_g[2]])
            P.op("dve", lambda e: e.tensor_scalar(out=E8[:, 0:G, :], in0=g0[:, 0:G, :], scalar1=0.0, scalar2=None, op0=ALU.min),
                 reads=[t_g[0]], writes=[t_E])
            P.op("dve", lambda e: e.tensor_scalar(out=ET8[:, 0:G, :], in0=g0[:, 0:G, :], scalar1=-1.0, scalar2=0.0, op0=ALU.mult, op1=ALU.min),
                 reads=[t_g[0]], writes=[t_E])
            P.op("act", lambda e: e.activation(out=E8[:, 0:G, :], in_=E8[:, 0:G, :], func=AF.Exp), reads=[t_E], writes=[t_E])
            P.op("act", lambda e: e.activation(out=ET8[:, 0:G, :], in_=ET8[:, 0:G, :], func=AF.Exp), reads=[t_E], writes=[t_E])
            P.op("dve", lambda e: e.tensor_tensor(out=Pa[:, 0:G, :], in0=g1[:, 0:G, :], in1=E8[:, 0:G, :], op=ALU.mult), reads=[t_g[1], t_E], writes=[t_P[0]])
            P.op("pool", lambda e: e.tensor_tensor(out=Pa[:, 0:G, :], in0=Pa[:, 0:G, :], in1=bc_mid(cst[:, 2, :], G), op=ALU.mult), reads=[t_cst, t_P[0]], writes=[t_P[0]])
            P.op("pool", lambda e: e.tensor_tensor(out=Pa[:, 0:G, :], in0=Pa[:, 0:G, :], in1=bc_last(bg[:, 0, n0:n0 + G], 64), op=ALU.mult), reads=[t_bg, t_P[0]], writes=[t_P[0]])
            P.op("dve", lambda e: e.tensor_tensor(out=Pta[:, 0:G, :], in0=g1[:, 0:G, :], in1=ET8[:, 0:G, :], op=ALU.mult), reads=[t_g[1], t_E], writes=[t_Pt[0]])
            P.op("pool", lambda e: e.tensor_tensor(out=Pta[:, 0:G, :], in0=Pta[:, 0:G, :], in1=bc_mid(cst[:, 3, :], G), op=ALU.mult), reads=[t_cst, t_Pt[0]], writes=[t_Pt[0]])
            P.op("pool", lambda e: e.tensor_tensor(out=Pta[:, 0:G, :], in0=Pta[:, 0:G, :], in1=bbc[:, c0:c1].rearrange("p (g c) -> p g c", c=64), op=ALU.mult),
                 reads=[t_bbc, t_Pt[0]], writes=[t_Pt[0]])
            P.op("dve", lambda e: e.tensor_tensor(out=QK[b][:, 0:G, :], in0=g2[:, 0:G, :], in1=ET8[:, 0:G, :], op=ALU.mult), reads=[t_g[2], t_E], writes=[t_prep[b]])
            P.op("pool", lambda e: e.tensor_tensor(out=QK[b][:, 0:G, :], in0=QK[b][:, 0:G, :], in1=bc_mid(cst[:, 4, :], G), op=ALU.mult), reads=[t_cst, t_prep[b]], writes=[t_prep[b]])
            P.op("pool", lambda e: e.tensor_tensor(out=Rv[:, 0:G, :], in0=vtm[b][:, 0:G, :], in1=bc_last(bg[:, 0, n0:n0 + G], 128), op=ALU.mult), reads=[t_in[b], t_bg], writes=[t_R])
            P.op("pool", lambda e: e.tensor_tensor(out=Rw[:, 0:G, :], in0=ktm[b][:, 0:G, :], in1=bc_last(beg[:, n0:n0 + G], 128), op=ALU.mult), reads=[t_in[b], t_gc], writes=[t_R])
            P.op("pool", lambda e: e.tensor_tensor(out=kd[b][:, 0:G, :], in0=ktm[b][:, 0:G, :], in1=bc_last(dk[:, n0:n0 + G], 128), op=ALU.mult), reads=[t_in[b], t_gc], writes=[t_prep[b]])
            P.op("pool", lambda e: e.tensor_tensor(out=Tt[:, 0:G, :], in0=Pta[:, 0:G, :], in1=bc_mid(cst[:, 5, :], G), op=ALU.add), reads=[t_cst, t_Pt[0]], writes=[t_Tt])
            Pk, Ptk = [Pa, Pb], [Pta, Ptb]
            for k in range(5):
                a, nx = k % 2, (k + 1) % 2
                for j in range(G):
                    P.op("pe", lambda e, j=j, a=a: e.matmul(g0[:, j, :], lhsT=Ptk[a][:, j, :], rhs=Pk[a][:, j, :], start=True, stop=True),
                         reads=[t_P[a], t_Pt[a]], writes=[t_g[0]])
                    P.op("pe", lambda e, j=j, a=a: e.matmul(g1[:, j, :], lhsT=Pk[a][:, j, :], rhs=Ptk[a][:, j, :], start=True, stop=True),
                         reads=[t_P[a], t_Pt[a]], writes=[t_g[1]])
                P.op("act", lambda e, nx=nx: e.activation(out=Pk[nx][:, 0:G, :], in_=g0[:, 0:G, :], func=AF.Identity), reads=[t_g[0]], writes=[t_P[nx]])
                P.op("dve", lambda e, nx=nx: e.tensor_copy(out=Ptk[nx][:, 0:G, :], in_=g1[:, 0:G, :]), reads=[t_g[1]], writes=[t_Pt[nx]])
                for j in range(G):
                    P.op("pe", lambda e, j=j, nx=nx: e.matmul(g2[:, j, :], lhsT=Pk[nx][:, j, :], rhs=Tt[:, j, :], start=True, stop=True),
                         reads=[t_P[nx], t_Tt], writes=[t_g[2]])
                P.op("dve", lambda e: e.tensor_tensor(out=Tt[:, 0:G, :], in0=Tt[:, 0:G, :], in1=g2[:, 0:G, :], op=ALU.add), reads=[t_g[2], t_Tt], writes=[t_Tt])
            for j in range(G):
                P.op("pe", lambda e, j=j: e.matmul(gU[:, j, :], lhsT=Tt[:, j, :], rhs=Rv[:, j, :], start=True, stop=True), reads=[t_Tt, t_R], writes=[t_gU])
                P.op("pe", lambda e, j=j: e.matmul(gW[:, j, :], lhsT=Rw[:, j, :], rhs=Tt[:, j, :], start=True, stop=True), reads=[t_Tt, t_R], writes=[t_gW])
            P.op("act", lambda e: e.activation(out=U8[b][:, 0:G, :], in_=gU[:, 0:G, :], func=AF.Identity), reads=[t_gU], writes=[t_prep[b]])
            P.op("dve", lambda e: e.tensor_copy(out=W8[b][:, 0:G, :], in_=gW[:, 0:G, :]), reads=[t_gW], writes=[t_prep[b]])
            for j in range(G):
                n = n0 + j
                v = n % 2
                P.op("pe", lambda e, j=j: e.matmul(q3[:, :], lhsT=W8[b][:, j, :], rhs=S[:, :], start=True, stop=True), reads=[t_prep[b], t_S], writes=[t_q3])
                P.op("pe", lambda e, j=j: e.matmul(q12[:, 0, :], lhsT=qT[b][:, j * 64:(j + 1) * 64], rhs=S[:, :], start=True, stop=True), reads=[t_in[b], t_S], writes=[t_q12])
                P.op("dve", lambda e, j=j, v=v: e.tensor_tensor(out=vn[v][:], in0=U8[b][:, j, :], in1=q3[:, :], op=ALU.subtract), reads=[t_prep[b], t_q3], writes=[t_vn[v]])
                P.op("pe", lambda e, j=j, v=v: e.matmul(q12[:, 1, :], lhsT=QK[b][:, j, :], rhs=vn[v][:], start=True, stop=True), reads=[t_prep[b], t_vn[v]], writes=[t_q12])
                P.op("pe", lambda e, j=j, v=v: e.matmul(q4[:, :], lhsT=kd[b][:, j, :], rhs=vn[v][:], start=True, stop=True), reads=[t_prep[b], t_vn[v]], writes=[t_q4])
                P.op("act", lambda e, n=n, v=v: e.activation(out=tmp[v][:], in_=q12[:, 0, :], func=AF.Identity, scale=eg[:, n:n + 1]), reads=[t_q12, t_gc], writes=[t_tmp[v]])
                P.op("dve", lambda e, j=j, v=v: e.tensor_tensor(out=o_sb[b][:, j, :], in0=tmp[v][:], in1=q12[:, 1, :], op=ALU.add), reads=[t_tmp[v], t_q12], writes=[t_osb[b]])
                P.op("dve", lambda e, n=n: e.scalar_tensor_tensor(out=S[:], in0=S[:], scalar=egl[:, n:n + 1], in1=q4[:, :], op0=ALU.mult, op1=ALU.add),
                     reads=[t_S, t_egl, t_q4], writes=[t_S])
            tk = Tok(); outs.append(tk)
            P.dma("sp", o_d[:, n0:n0 + G, :], o_sb[b][:, 0:G, :], reads=[t_osb[b]], writes=[tk])

        outs = []
        gi = 0
        for n0 in range(0, NCH, GRP):
            do_group(gi, n0, min(GRP, NCH - n0))
            gi += 1
        P.wait_all("sp", outs)
        P.emit()
    return nc


def build_k3():
    nc = bass.Bass("TRN2", target_bir_lowering=False)
    dt_in = lambda name, shape, dt=F32: nc.dram_tensor(name, shape, dt, kind="ExternalInput").ap()
    dt_out = lambda name, shape, dt=F32: nc.dram_tensor(name, shape, dt, kind="ExternalOutput").ap()
    m1_d = dt_in("m1", [16, 128, NT])
    m2_d = dt_in("m2", [8, 128, NT])
    zs_d = dt_in("zs", [16, 128, NT])
    nw_d = dt_in("nw", [128, 1])
    gt_d = dt_in("gt", [128, 2, D])
    x_d = dt_in("x", [NT, D])
    w_d = dt_in("w", [4, 128, 16, 512])
    xo_d = dt_out("xo", [NT, D])
    with contextlib.ExitStack() as es:
        P = Prog(nc, es)
        sb = lambda name, shape, dt=F32: es.enter_context(nc.sbuf_tensor("s_" + name, shape, dt))
        ps = lambda name, shape, dt=F32: es.enter_context(nc.psum_tensor("p_" + name, shape, dt))
        yT = sb("yT", [128, 16, NT], BF16); t_y = [Tok() for _ in range(16)]
        a_sb = [sb("a%d" % i, [128, NT]) for i in range(2)]; t_a = [Tok(), Tok()]
        b_sb = [sb("b%d" % i, [128, NT]) for i in range(2)]; t_b = [Tok(), Tok()]
        z_sb = [sb("z%d" % i, [128, NT]) for i in range(2)]; t_z = [Tok(), Tok()]
        sq = sb("sq", [128, NT]); t_sq = Tok()
        rs = sb("rs", [128, NT]); t_rs = Tok()
        nw = sb("nw", [128, 1]); t_nw = Tok()
        gt = sb("gt", [128, 2, D]); t_gt = Tok()
        ones = sb("ones", [128, 128]); t_ones = Tok()
        wst = [sb("wst%d" % i, [128, 16, 512]) for i in range(2)]; t_wst = [Tok(), Tok()]
        wbf = [sb("wbf%d" % i, [128, 16, 512], BF16) for i in range(2)]; t_wbf = [Tok(), Tok()]
        xt = [sb("xt%d" % i, [128, 512]) for i in range(2)]; t_xt = [Tok(), Tok()]
        xo = [sb("xo%d" % i, [128, 512]) for i in range(2)]; t_xo = [Tok(), Tok()]
        pn = ps("pn", [128, 512]); t_pn = Tok()
        po = [ps("po%d" % i, [128, 512]) for i in range(2)]; t_po = [Tok(), Tok()]
        P.dma("sp", nw[:], nw_d, writes=[t_nw])
        P.dma("sp", gt[:], gt_d, writes=[t_gt])
        P.op("pool", lambda e: e.memset(ones[:], 1.0), writes=[t_ones])
        P.op("dve", lambda e: e.tensor_scalar(out=nw[:], in0=nw[:], scalar1=128.0 ** 0.5, scalar2=None, op0=ALU.mult), reads=[t_nw], writes=[t_nw])

        def do_chunk(c):
            b = c % 2
            P.dma("sp", a_sb[b][:], m1_d[c], writes=[t_a[b]])
            P.dma("sp", z_sb[b][:], zs_d[c], writes=[t_z[b]])
            if 4 <= c < 12:
                P.dma("sp", b_sb[b][:], m2_d[c - 4], writes=[t_b[b]])
                P.op("pool", lambda e: e.tensor_tensor(out=a_sb[b][:], in0=a_sb[b][:], in1=b_sb[b][:], op=ALU.add),
                     reads=[t_a[b], t_b[b]], writes=[t_a[b]])
            if 8 <= c < 12:
                P.op("pool", lambda e: e.tensor_tensor(out=sq[:], in0=a_sb[b][:], in1=a_sb[b][:], op=ALU.mult), reads=[t_a[b]], writes=[t_sq])
                for (a0, a1) in TBLK:
                    P.op("pe", lambda e, a0=a0, a1=a1: e.matmul(pn[:, 0:a1 - a0], lhsT=ones[:], rhs=sq[:, a0:a1], start=True, stop=True),
                         reads=[t_ones, t_sq], writes=[t_pn])
                    P.op("act", lambda e, a0=a0, a1=a1: e.activation(out=rs[:, a0:a1], in_=pn[:, 0:a1 - a0], func=AF.Sqrt, bias=128 * EPS),
                         reads=[t_pn], writes=[t_rs])
                P.op("dve", lambda e: e.reciprocal(out=rs[:], in_=rs[:]), reads=[t_rs], writes=[t_rs])
                P.op("dve", lambda e: e.scalar_tensor_tensor(out=a_sb[b][:], in0=a_sb[b][:], scalar=nw[:, 0:1], in1=rs[:], op0=ALU.mult, op1=ALU.mult),
                     reads=[t_a[b], t_nw, t_rs], writes=[t_a[b]])
            P.op("dve", lambda e: e.tensor_tensor(out=yT[:, c, :], in0=a_sb[b][:], in1=z_sb[b][:], op=ALU.mult),
                 reads=[t_a[b], t_z[b]], writes=[t_y[c]])
        for c in range(16):
            do_chunk(c)
        outs = []
        cnt = 0
        for cb in range(4):
            wbi = cb % 2
            P.dma("sp", wst[wbi][:], w_d[cb], writes=[t_wst[wbi]])
            P.op("pool", lambda e, wbi=wbi: e.tensor_copy(out=wbf[wbi][:], in_=wst[wbi][:]), reads=[t_wst[wbi]], writes=[t_wbf[wbi]])

            def do_tile(ti, cb, wbi, i):
                r0 = ti * 128
                np_ = 128 if ti < 8 else CTK
                m = 0 if ti < 8 else 1
                P.dma("sp", xt[i][0:np_, :], x_d[r0:r0 + np_, cb * 512:(cb + 1) * 512], writes=[t_xt[i]])
                for c in range(16):
                    P.op("pe", lambda e, c=c: e.matmul(po[i][0:np_, :], lhsT=yT[:, c, r0:r0 + np_], rhs=wbf[wbi][:, c, :],
                                                      start=(c == 0), stop=(c == 15)), reads=[t_y[c], t_wbf[wbi]], writes=[t_po[i]])
                P.op("dve", lambda e: e.tensor_tensor(out=xo[i][0:np_, :], in0=po[i][0:np_, :], in1=gt[0:np_, m, cb * 512:(cb + 1) * 512], op=ALU.mult),
                     reads=[t_po[i], t_gt], writes=[t_xo[i]])
                P.op("pool", lambda e: e.tensor_tensor(out=xo[i][0:np_, :], in0=xo[i][0:np_, :], in1=xt[i][0:np_, :], op=ALU.add),
                     reads=[t_xo[i], t_xt[i]], writes=[t_xo[i]])
                tk = Tok(); outs.append(tk)
                P.dma("sp", xo_d[r0:r0 + np_, cb * 512:(cb + 1) * 512], xo[i][0:np_, :], reads=[t_xo[i]], writes=[tk])
            for ti in range(9):
                do_tile(ti, cb, wbi, cnt % 2)
                cnt += 1
        P.wait_all("sp", outs)
        P.emit()
    return nc


def _gather(res, name):
    ctx = np.concatenate([np.asarray(r[name])[..., TOK:] for r in res], axis=-1)
    lat = np.concatenate([np.asarray(r[name])[..., :TOK] for r in res], axis=-1)
    return np.concatenate([ctx, lat], axis=-1)


def _flipseg(a):
    return np.concatenate([a[..., :CTX][..., ::-1], a[..., CTX:][..., ::-1]], axis=-1)


def _core_slice(a, i):
    return np.concatenate([a[..., CTX + i * TOK:CTX + (i + 1) * TOK], a[..., i * CTK:(i + 1) * CTK]], axis=-1)


def _taps5(cw4, d):
    out = np.zeros((cw4.shape[1], 5), np.float32)
    if d == 0:
        out[:, 0:4] = cw4.T
    else:
        out[:, 1] = cw4[3]; out[:, 2] = cw4[2]; out[:, 3] = cw4[1]; out[:, 4] = cw4[0]
    return out


_PROGS = {}


def _prog(name, builder):
    if name not in _PROGS:
        _PROGS[name] = builder()
    return _PROGS[name]


def _forward(inputs, depth=DEPTH):
    f32 = lambda a: np.ascontiguousarray(np.asarray(a, np.float32))
    x_full = f32(inputs["x"][0]).copy()
    ctx_full = f32(inputs["ctx"][0]).copy()
    ada = run_k0(inputs)
    for l in range(depth):
        im1 = prep_k1(inputs, l, x_full, ctx_full, ada)
        r1 = _run(_prog("k1", build_k1), im1, "k1")
        ka = _gather(r1, "ka"); kdn = _gather(r1, "kdn"); kdr = _gather(r1, "kdr")
        va = _gather(r1, "va"); vd = _gather(r1, "vd")
        km = np.ascontiguousarray(np.concatenate([ka, kdn], axis=0))
        vall = np.concatenate([va, vd], axis=0)
        vaug = np.zeros((6, 128, NKT, VW), NPBF)
        vaug[:, :, :, 0:128] = vall.transpose(0, 2, 1).reshape(6, NKT, 128, 128).transpose(0, 2, 1, 3)
        vaug[:, :, :, 128] = 1.0
        im2 = []
        for i in range(NCORES):
            qm = np.ascontiguousarray(np.concatenate([np.asarray(r1[i]["qa"]), np.asarray(r1[i]["qdn"])], axis=0).transpose(1, 0, 2))
            qr = np.ascontiguousarray(np.asarray(r1[i]["qdr"]).transpose(1, 0, 2))
            im2.append({"qm": qm, "qr": qr, "km": km, "kr": np.ascontiguousarray(kdr), "va": vaug})
        r2a = _run(_prog("k2a", build_k2a), im2, "k2a")
        ux = _gather(r1, "ux")
        cwl = f32(inputs["lru_conv_w"][l])
        im3 = []
        for i in range(NCORES):
            d, blk = i // 4, i % 4
            sl = slice(blk * 128, (blk + 1) * 128)
            u = ux[blk] if d == 0 else _flipseg(ux[blk])
            prm = np.zeros((128, 16), np.float32)
            prm[:, 0:5] = _taps5(cwl[:, sl], d)
            prm[:, 5] = inputs["lru_conv_b"][l][sl]
            prm[:, 6] = inputs["lru_b_a"][l][d][sl]
            prm[:, 7] = inputs["lru_b_x"][l][d][sl]
            prm[:, 8] = inputs["lru_lambda"][l][d][sl]
            w = np.ascontiguousarray(np.stack([f32(inputs["lru_w_a"][l][d][blk]), f32(inputs["lru_w_x"][l][d][blk])], axis=1))
            im3.append({"ux": f32(u), "prm": prm, "w": w})
        r2b = _run(_prog("k2b", build_k2b), im3, "k2b")
        hdir = [[None] * 4, [None] * 4]
        for i in range(NCORES):
            d, blk = i // 4, i % 4
            h = np.asarray(r2b[i]["h"])
            hdir[d][blk] = h if d == 0 else _flipseg(h)
        uqkv = _gather(r1, "uqkv")
        uab = _gather(r1, "uab")
        cwd = f32(inputs["dn_conv_w"][l])
        im4 = []
        for i in range(NCORES):
            d, hd = i // 4, i % 4
            u = np.stack([uqkv[hd], uqkv[4 + hd], uqkv[8 + hd]], axis=0)
            br, ar = uab[d * 8 + hd], uab[d * 8 + 4 + hd]
            if d == 1:
                u = _flipseg(u); br = _flipseg(br); ar = _flipseg(ar)
            cw = np.zeros((128, 16), np.float32)
            for j in range(3):
                c = j * 4 + hd
                cw[:, 5 * j:5 * j + 5] = _taps5(cwd[:, c * 128:(c + 1) * 128], d)
            ab = np.ascontiguousarray(np.stack([br.reshape(NCH, 64).T, ar.reshape(NCH, 64).T], axis=1))
            sc = np.zeros((64, 2), np.float32)
            sc[:, 0] = inputs["dn_a_log"][l][d][hd]
            sc[:, 1] = inputs["dn_dt_bias"][l][d][hd]
            im4.append({"u": f32(u), "cw": cw, "ab": f32(ab), "sc": sc})
        rc1 = _run(_prog("c1", build_c1), im4, "c1")
        dnc = _dn_consts()
        im5 = []
        for i in range(NCORES):
            qkv = np.asarray(rc1[i]["qkv"]); bg = np.asarray(rc1[i]["bg"])
            tm = lambda a: np.ascontiguousarray(a.T.reshape(NCH, 64, 128).transpose(1, 0, 2))
            brow = bg[:, 0, :].T.reshape(-1)
            im5.append({"qT": f32(qkv[0]), "kT": f32(qkv[1]), "ktm": tm(qkv[1]), "vtm": tm(qkv[2]), "bg": f32(bg),
                        "bbc": np.ascontiguousarray(np.broadcast_to(brow[None, :], (64, NALL))), "cst": dnc})
        rc2 = _run(_prog("c2", build_c2), im5, "c2")
        odir = [[None] * 4, [None] * 4]
        for i in range(NCORES):
            d, hd = i // 4, i % 4
            o = np.asarray(rc2[i]["o"]).transpose(1, 0, 2).reshape(NALL, 128).T
            odir[d][hd] = o if d == 0 else _flipseg(o)
        w_out = f32(inputs["w_out"][l])
        wl = np.ascontiguousarray(w_out.reshape(16, 128, 4, 512).transpose(2, 1, 0, 3))
        gt = np.ascontiguousarray(np.broadcast_to(np.stack([ada[l, 0, 4096:], ada[l, 1, 4096:]], axis=0)[None], (128, 2, D)))
        nw = f32(inputs["dn_norm_w"][l]).reshape(128, 1)
        im6 = []
        for i in range(NCORES):
            o2 = np.asarray(r2a[i]["o"])
            m1 = np.empty((16, 128, NT), np.float32)
            m2 = np.empty((8, 128, NT), np.float32)
            for j in range(4):
                m1[j] = o2[j].T
                m1[12 + j] = o2[4 + j].T
                m1[4 + j] = _core_slice(hdir[0][j], i)
                m2[j] = _core_slice(hdir[1][j], i)
                m1[8 + j] = _core_slice(odir[0][j], i)
                m2[4 + j] = _core_slice(odir[1][j], i)
            im6.append({"m1": m1, "m2": m2, "zs": f32(r1[i]["zs"]), "nw": nw, "gt": gt, "x": im1[i]["x"], "w": wl})
        r3 = _run(_prog("k3", build_k3), im6, "k3")
        for i in range(NCORES):
            xo = np.asarray(r3[i]["xo"])
            x_full[i * TOK:(i + 1) * TOK] = xo[:TOK]
            ctx_full[i * CTK:(i + 1) * CTK] = xo[TOK:]
    return x_full, ctx_full


def kernel(**inputs):
    x_full, _ = _forward(inputs, DEPTH)
    return np.ascontiguousarray(x_full[None].astype(np.float32))
```

```python
import contextlib
import numpy as np
import ml_dtypes
import concourse.bass as bass
import concourse.mybir as mybir
from concourse.bass_utils import run_bass_kernel_spmd

F32 = mybir.dt.float32
BF16 = mybir.dt.bfloat16
AF = mybir.ActivationFunctionType
ALU = mybir.AluOpType
AX = mybir.AxisListType
NPBF = ml_dtypes.bfloat16

NCORES = 8
D = 2048
SEQ = 8192
CTX = 256
DEPTH = 4
TOK = SEQ // NCORES
CTK = CTX // NCORES
NT = TOK + CTK
NALL = SEQ + CTX
EPS = 1e-6
IN_W = 5840


class Tok:
    __slots__ = ("lw", "rd", "name")

    def __init__(self, name=""):
        self.lw = None
        self.rd = {}
        self.name = name


class Prog:
    ENG = ("pe", "act", "dve", "pool", "sp")

    def __init__(self, nc, es, n_dma_sems=8, same_engine_sync=True):
        self.nc = nc
        self.es = es
        self.eng = {"pe": nc.tensor, "act": nc.scalar, "dve": nc.vector,
                    "pool": nc.gpsimd, "sp": nc.sync}
        self.streams = {e: [] for e in self.ENG}
        self.sems = {}
        self.count = {}
        self.waited = {e: {} for e in self.ENG}
        for e in self.ENG:
            self.sems[e] = es.enter_context(nc.semaphore("sem_" + e))
            self.count[e] = 0
        self.dma_pool = {}
        self.dma_rr = {}
        for q in ("sp", "pool", "act"):
            lst = []
            for i in range(n_dma_sems):
                k = "dma_%s_%d" % (q, i)
                self.sems[k] = es.enter_context(nc.semaphore(k))
                self.count[k] = 0
                lst.append(k)
            self.dma_pool[q] = lst
            self.dma_rr[q] = 0
        self.same_engine_sync = same_engine_sync
        self.n_ops = 0

    def _deps(self, eng, reads, writes):
        deps = {}

        def add(k, v):
            if deps.get(k, 0) < v:
                deps[k] = v
        for t in reads:
            if t.lw is not None:
                add(*t.lw)
        for t in writes:
            if t.lw is not None:
                add(*t.lw)
            for k, v in t.rd.items():
                add(k, v)
        return deps

    def _emit_waits(self, eng, deps):
        for k, v in deps.items():
            if k == eng:
                if eng == "pe" or not self.same_engine_sync:
                    continue
            if self.waited[eng].get(k, 0) >= v:
                continue
            self.waited[eng][k] = v
            self.streams[eng].append(("wait", k, v))

    def _mark(self, key, val, reads, writes):
        for t in writes:
            t.lw = (key, val)
            t.rd = {}
        for t in reads:
            if t.rd.get(key, 0) < val:
                t.rd[key] = val

    def op(self, eng, fn, reads=(), writes=()):
        deps = self._deps(eng, reads, writes)
        self._emit_waits(eng, deps)
        self.count[eng] += 1
        self.streams[eng].append(("op", fn, eng, 1))
        self._mark(eng, self.count[eng], reads, writes)
        self.n_ops += 1

    def dma(self, q, out, in_, reads=(), writes=()):
        deps = self._deps(q, reads, writes)
        pool = self.dma_pool[q]
        k = pool[self.dma_rr[q] % len(pool)]
        self.dma_rr[q] += 1
        if self.count[k] > 0:
            deps[k] = max(deps.get(k, 0), self.count[k])
        self._emit_waits(q, deps)
        self.count[k] += 16
        self.streams[q].append(("op", lambda e: e.dma_start(out=out, in_=in_), k, 16))
        self._mark(k, self.count[k], reads, writes)
        self.n_ops += 1

    def wait_all(self, eng, toks):
        deps = {}
        for t in toks:
            if t.lw is not None and deps.get(t.lw[0], 0) < t.lw[1]:
                deps[t.lw[0]] = t.lw[1]
        for k, v in deps.items():
            self.streams[eng].append(("wait", k, v))

    def emit(self):
        nc = self.nc
        with nc.Block() as block:
            def run(ename):
                def body(e):
                    for item in self.streams[ename]:
                        if item[0] == "wait":
                            e.wait_ge(self.sems[item[1]], item[2])
                        else:
                            _, fn, k, inc = item
                            fn(e).then_inc(self.sems[k], inc)
                return body
            block.tensor(run("pe"))
            block.scalar(run("act"))
            block.vector(run("dve"))
            block.gpsimd(run("pool"))
            block.sync(run("sp"))


def _run(nc, in_maps, tag=""):
    import sys, time
    t0 = time.time()
    res = run_bass_kernel_spmd(nc, in_maps, core_ids=list(range(NCORES)))
    print("[launch %s] %.1fs" % (tag, time.time() - t0), file=sys.stderr, flush=True)
    return res.results


ADA_N = 3 * D
ADA_PC = ADA_N // NCORES


def build_k0():
    nc = bass.Bass("TRN2", target_bir_lowering=False)
    cT = nc.dram_tensor("cT", [128, 16, 2], F32, kind="ExternalInput").ap()
    w = nc.dram_tensor("w", [DEPTH, D, ADA_PC], F32, kind="ExternalInput").ap()
    b = nc.dram_tensor("b", [DEPTH, 2, ADA_PC], F32, kind="ExternalInput").ap()
    o = nc.dram_tensor("o", [DEPTH, 2, ADA_PC], F32, kind="ExternalOutput").ap()
    with contextlib.ExitStack() as es:
        P = Prog(nc, es)
        sb = lambda name, shape, dt: es.enter_context(nc.sbuf_tensor(name, shape, dt))
        ps = lambda name, shape, dt: es.enter_context(nc.psum_tensor(name, shape, dt))
        c_sb = sb("c_sb", [128, 16, 2], F32)
        s_sb = sb("s_sb", [128, 16, 2], F32)
        b_sb = sb("b_sb", [2, DEPTH, ADA_PC], F32)
        o_sb = sb("o_sb", [2, DEPTH, ADA_PC], F32)
        NB = 4
        w_sb = [sb("w_sb%d" % i, [128, 4, ADA_PC], F32) for i in range(NB)]
        acc = [ps("acc%d" % i, [2, 512], F32) for i in range(4)]
        t_c, t_s, t_b, t_o = Tok(), Tok(), Tok(), Tok()
        t_w = [Tok() for _ in range(NB)]
        t_acc = [Tok() for _ in range(4)]
        P.dma("sp", c_sb[:], cT, writes=[t_c])
        P.dma("sp", b_sb[:], b.rearrange("l m n -> m l n"), writes=[t_b])
        P.op("act", lambda e: e.activation(out=s_sb[:], in_=c_sb[:], func=AF.Silu), reads=[t_c], writes=[t_s])
        wi = 0
        for l in range(DEPTH):
            wv = w[l].rearrange("(c p) n -> p c n", p=128)
            for g in range(4):
                bi = wi % NB
                wi += 1
                q = "sp" if (wi % 2) else "pool"
                P.dma(q, w_sb[bi][:], wv[:, g * 4:(g + 1) * 4, :], writes=[t_w[bi]])
                for kk in range(4):
                    kc = g * 4 + kk
                    for j, (n0, n1) in enumerate(((0, 512), (512, 768))):
                        a = acc[(l % 2) * 2 + j]
                        P.op("pe", lambda e, a=a, bi=bi, kk=kk, kc=kc, n0=n0, n1=n1: e.matmul(
                            a[:, 0:n1 - n0], lhsT=s_sb[:, kc, :], rhs=w_sb[bi][:, kk, n0:n1],
                            start=(kc == 0), stop=(kc == 15)),
                            reads=[t_s, t_w[bi]], writes=[t_acc[(l % 2) * 2 + j]])
            for j, (n0, n1) in enumerate(((0, 512), (512, 768))):
                a = acc[(l % 2) * 2 + j]
                P.op("dve", lambda e, a=a, l=l, n0=n0, n1=n1: e.tensor_tensor(
                    out=o_sb[:, l, n0:n1], in0=a[:, 0:n1 - n0], in1=b_sb[:, l, n0:n1], op=ALU.add),
                    reads=[t_acc[(l % 2) * 2 + j], t_b], writes=[t_o])
        P.dma("sp", o.rearrange("l m n -> m l n"), o_sb[:], reads=[t_o], writes=[t_o])
        P.wait_all("sp", [t_o])
        P.emit()
    return nc


def run_k0(inputs):
    c = np.asarray(inputs["c"], np.float32).reshape(D)
    cc = np.asarray(inputs["c_ctx"], np.float32).reshape(D)
    cm = np.stack([c, cc], axis=1)
    cT = np.ascontiguousarray(cm.reshape(16, 128, 2).transpose(1, 0, 2))
    w_ada = np.asarray(inputs["w_ada"], np.float32)
    b_ada = np.asarray(inputs["b_ada"], np.float32)
    nc = build_k0()
    in_maps = []
    for i in range(NCORES):
        sl = slice(i * ADA_PC, (i + 1) * ADA_PC)
        bb = np.ascontiguousarray(np.broadcast_to(b_ada[:, None, sl], (DEPTH, 2, ADA_PC)))
        in_maps.append({"cT": cT, "w": np.ascontiguousarray(w_ada[:, :, sl]), "b": bb})
    res = _run(nc, in_maps, "k0")
    ada = np.concatenate([r["o"] for r in res], axis=2)
    return ada


def _col_tiles():
    tiles = []
    def add(name, c0, n, step=128):
        for i in range(0, n, step):
            tiles.append((name, c0 + i, min(step, n - i)))
    add("qa", 0, 512); add("ka", 512, 256); add("va", 768, 256); add("z", 1024, 512)
    add("ux", 1536, 512); add("z", 2048, 512)
    add("uqkv", 2560, 1536); add("z", 4096, 512); add("uab", 4608, 16)
    add("cq", 4624, 384); add("ckv", 5008, 256); add("kr", 5264, 64); add("z", 5328, 512)
    return tiles
COL_TILES = _col_tiles()
NCT = len(COL_TILES)
TBLK = ((0, 512), (512, 1024), (1024, NT))


def build_k1():
    nc = bass.Bass("TRN2", target_bir_lowering=False)
    dt_in = lambda name, shape, dt=F32: nc.dram_tensor(name, shape, dt, kind="ExternalInput").ap()
    dt_out = lambda name, shape, dt=F32: nc.dram_tensor(name, shape, dt, kind="ExternalOutput").ap()
    x_d = dt_in("x", [NT, D])
    mod_d = dt_in("mod", [128, 5, 16])
    w_d = dt_in("w", [NCT, 128, 16, 128])
    gn_d = dt_in("gn", [128, 8])
    g2_d = dt_in("g2", [128, 4])
    wuq_d = dt_in("wuq", [128, 3, 768])
    wukv_d = dt_in("wukv", [128, 2, 1024])
    cs_d = dt_in("cs", [128, 2, NT])
    cs64_d = dt_in("cs64", [64, 2, NT])
    cst_d = dt_in("cst", [128, 3, 128], BF16)
    qa_o = dt_out("qa", [4, 128, NT], BF16)
    ka_o = dt_out("ka", [2, 128, NT], BF16)
    va_o = dt_out("va", [2, 128, NT], BF16)
    zs_o = dt_out("zs", [16, 128, NT])
    ux_o = dt_out("ux", [4, 128, NT])
    uqkv_o = dt_out("uqkv", [12, 128, NT])
    uab_o = dt_out("uab", [16, NT])
    qdn_o = dt_out("qdn", [4, 128, NT], BF16)
    qdr_o = dt_out("qdr", [4, 64, NT], BF16)
    kdn_o = dt_out("kdn", [4, 128, NT], BF16)
    kdr_o = dt_out("kdr", [4, 64, NT], BF16)
    vd_o = dt_out("vd", [4, 128, NT], BF16)
    out_toks = []
    with contextlib.ExitStack() as es:
        P = Prog(nc, es)
        sb = lambda name, shape, dt=F32: es.enter_context(nc.sbuf_tensor("s_" + name, shape, dt))
        ps = lambda name, shape, dt=F32: es.enter_context(nc.psum_tensor("p_" + name, shape, dt))
        mod = sb("mod", [128, 5, 16]); t_mod = Tok()
        gain = sb("gain", [128, 2, 16]); t_gain = Tok()
        gn = sb("gn", [128, 8]); g2 = sb("g2", [128, 4]); t_gn = Tok()
        gs = sb("gs", [128, 12]); t_gs = Tok()
        wuq_f = sb("wuq_f", [128, 3, 768]); wuq = sb("wuq", [128, 3, 768], BF16); t_wuq = Tok()
        wukv_f = sb("wukv_f", [128, 2, 1024]); wukv = sb("wukv", [128, 2, 1024], BF16); t_wukv = Tok()
        cs = sb("cs", [128, 2, NT]); cs64 = sb("cs64", [64, 2, NT]); t_cs = Tok()
        cst = sb("cst", [128, 3, 128], BF16); t_cst = Tok()
        ones = sb("ones", [128, 128], BF16); t_ones = Tok()
        hT = sb("hT", [128, 16, NT], BF16); t_hT = [Tok() for _ in range(9)]
        P.dma("sp", mod[:], mod_d, writes=[t_mod])
        P.dma("sp", gn[:], gn_d, writes=[t_gn])
        P.dma("sp", g2[:], g2_d, writes=[t_gn])
        P.dma("sp", cst[:], cst_d, writes=[t_cst])
        P.dma("sp", wuq_f[:], wuq_d, writes=[t_wuq])
        P.dma("sp", wukv_f[:], wukv_d, writes=[t_wukv])
        P.dma("sp", cs[:], cs_d, writes=[t_cs])
        P.dma("sp", cs64[:], cs64_d, writes=[t_cs])
        P.op("pool", lambda e: e.memset(ones[:], 1.0), writes=[t_ones])
        P.op("pool", lambda e: e.tensor_copy(out=wuq[:], in_=wuq_f[:]), reads=[t_wuq], writes=[t_wuq])
        P.op("pool", lambda e: e.tensor_copy(out=wukv[:], in_=wukv_f[:]), reads=[t_wukv], writes=[t_wukv])
        for m in range(2):
            P.op("dve", lambda e, m=m: e.scalar_tensor_tensor(
                out=gain[:, m, :], in0=mod[:, 1 + 2 * m, :], scalar=1.0, in1=mod[:, 0, :],
                op0=ALU.add, op1=ALU.mult), reads=[t_mod], writes=[t_gain])
        for (o0, o1, src, s0, fac) in ((0, 2, gn, 0, 128.0 ** 0.5), (2, 5, gn, 2, 384.0 ** 0.5),
                                        (5, 7, gn, 5, 256.0 ** 0.5), (7, 11, g2, 0, 192.0 ** 0.5)):
            P.op("dve", lambda e, o0=o0, o1=o1, src=src, s0=s0, fac=fac: e.tensor_scalar(
                out=gs[:, o0:o1], in0=src[:, s0:s0 + (o1 - o0)], scalar1=fac, scalar2=None, op0=ALU.mult),
                reads=[t_gn], writes=[t_gs])

        xt = [sb("xt%d" % i, [128, D]) for i in range(2)]; t_xt = [Tok(), Tok()]
        xn = [sb("xn%d" % i, [128, D], BF16) for i in range(2)]; t_xn = [Tok(), Tok()]
        ssq = sb("ssq", [128, 16]); t_ssq = Tok()
        tp_all = ps("tp", [128, 2, 4, 128], BF16); tp = [tp_all[:, 0], tp_all[:, 1]]; t_tp = [Tok()] * 2
        tpi = 0
        for ti in range(9):
            r0 = ti * 128
            np_ = 128 if ti < 8 else CTK
            m = 0 if ti < 8 else 1
            b = ti % 2
            P.dma("sp", xt[b][0:np_, :], x_d[r0:r0 + np_, :], writes=[t_xt[b]])
            P.op("act", lambda e, b=b, np_=np_, ti=ti: e.activation(
                out=xn[b][0:np_, :], in_=xt[b][0:np_, :], func=AF.Square, accum_out=ssq[0:np_, ti:ti + 1]),
                reads=[t_xt[b]], writes=[t_xn[b], t_ssq])
            P.op("dve", lambda e, np_=np_, ti=ti: e.tensor_scalar(
                out=ssq[0:np_, ti:ti + 1], in0=ssq[0:np_, ti:ti + 1], scalar1=1.0 / D, scalar2=EPS,
                op0=ALU.mult, op1=ALU.add), reads=[t_ssq], writes=[t_ssq])
            P.op("act", lambda e, np_=np_, ti=ti: e.activation(
                out=ssq[0:np_, ti:ti + 1], in_=ssq[0:np_, ti:ti + 1], func=AF.Sqrt), reads=[t_ssq], writes=[t_ssq])
            P.op("dve", lambda e, np_=np_, ti=ti: e.reciprocal(
                out=ssq[0:np_, ti:ti + 1], in_=ssq[0:np_, ti:ti + 1]), reads=[t_ssq], writes=[t_ssq])
            P.op("dve", lambda e, b=b, np_=np_, ti=ti: e.tensor_scalar(
                out=xn[b][0:np_, :], in0=xt[b][0:np_, :], scalar1=ssq[0:np_, ti:ti + 1], scalar2=None,
                op0=ALU.mult), reads=[t_xt[b], t_ssq], writes=[t_xn[b]])
            for cg in range(4):
                pb = tpi % 2; tpi += 1
                for j in range(4):
                    c = cg * 4 + j
                    P.op("pe", lambda e, pb=pb, j=j, c=c, b=b, np_=np_: e.transpose(
                        tp[pb][:, j, 0:np_], xn[b][0:np_, c * 128:(c + 1) * 128], cst[0:np_, 0, 0:np_]),
                        reads=[t_xn[b], t_cst], writes=[t_tp[pb]])
                for j in range(4):
                    c = cg * 4 + j
                    eng = "act" if j % 2 == 0 else "dve"
                    if eng == "act":
                        P.op("act", lambda e, pb=pb, j=j, c=c, r0=r0, np_=np_, m=m: e.activation(
                            out=hT[:, c, r0:r0 + np_], in_=tp[pb][:, j, 0:np_], func=AF.Identity,
                            scale=gain[:, m, c:c + 1], bias=mod[:, 2 + 2 * m, c:c + 1]),
                            reads=[t_tp[pb], t_gain, t_mod], writes=[t_hT[ti]])
                    else:
                        P.op("dve", lambda e, pb=pb, j=j, c=c, r0=r0, np_=np_, m=m: e.tensor_scalar(
                            out=hT[:, c, r0:r0 + np_], in0=tp[pb][:, j, 0:np_], scalar1=gain[:, m, c:c + 1],
                            scalar2=mod[:, 2 + 2 * m, c:c + 1], op0=ALU.mult, op1=ALU.add),
                            reads=[t_tp[pb], t_gain, t_mod], writes=[t_hT[ti]])

        NWB = 3
        wst = [sb("wst%d" % i, [128, 16, 128]) for i in range(2)]; t_wst = [Tok() for _ in range(2)]
        wbf = [sb("wbf%d" % i, [128, 16, 128], BF16) for i in range(NWB)]; t_wbf = [Tok() for _ in range(NWB)]
        pj = [ps("pj%d" % i, [128, 3, 512]) for i in range(2)]; t_pj = [Tok(), Tok()]
        aux = ps("aux", [128, 1, 512]); t_aux = Tok()
        raw = [sb("raw%d" % i, [128, NT]) for i in range(5)]; t_raw = [Tok() for _ in range(5)]
        sqb = [sb("sqb%d" % i, [128, NT], BF16) for i in range(3)]; t_sqb = [Tok() for _ in range(3)]
        rstd = sb("rstd", [128, NT]); t_rstd = Tok()
        nbf = [sb("nbf%d" % i, [128, NT], BF16) for i in range(4)]; t_nbf = [Tok() for _ in range(4)]
        t1 = sb("t1", [128, NT]); t_t1 = Tok()
        t2 = sb("t2", [128, NT]); t_t2 = Tok()
        ob = [sb("ob%d" % i, [128, NT], BF16) for i in range(2)]; t_ob = [Tok(), Tok()]
        of = [sb("of%d" % i, [128, NT]) for i in range(2)]; t_of = [Tok(), Tok()]
        kr_raw = sb("kr_raw", [64, NT]); t_kr = Tok()
        kr_sq = sb("kr_sq", [64, NT], BF16); t_krsq = Tok()
        cnt = {"ob": 0, "of": 0}

        def store_bf(dst_ap, rows, producer):
            i = cnt["ob"] % 2; cnt["ob"] += 1
            producer(ob[i], t_ob[i])
            tk = Tok(); out_toks.append(tk)
            P.dma("sp", dst_ap, ob[i][0:rows, :], reads=[t_ob[i]], writes=[tk])

        def store_f32(dst_ap, rows, producer):
            i = cnt["of"] % 2; cnt["of"] += 1
            producer(of[i], t_of[i])
            tk = Tok(); out_toks.append(tk)
            P.dma("sp", dst_ap, of[i][0:rows, :], reads=[t_of[i]], writes=[tk])

        def ss_to_rstd(pieces, n_eps):
            for bi, (a0, a1) in enumerate(TBLK):
                for pi, (sqt, tk, rows) in enumerate(pieces):
                    P.op("pe", lambda e, sqt=sqt, rows=rows, a0=a0, a1=a1, bi=bi, pi=pi: e.matmul(
                        aux[:, 0, 0:a1 - a0], lhsT=ones[0:rows, :], rhs=sqt[0:rows, a0:a1],
                        start=(pi == 0), stop=(pi == len(pieces) - 1)),
                        reads=[tk, t_ones], writes=[t_aux])
                P.op("dve", lambda e, a0=a0, a1=a1, bi=bi: e.tensor_scalar(
                    out=rstd[:, a0:a1], in0=aux[:, 0, 0:a1 - a0], scalar1=n_eps, scalar2=None,
                    op0=ALU.add), reads=[t_aux], writes=[t_rstd])
            P.op("act", lambda e: e.activation(out=rstd[:], in_=rstd[:], func=AF.Sqrt), reads=[t_rstd], writes=[t_rstd])
            P.op("dve", lambda e: e.reciprocal(out=rstd[:], in_=rstd[:]), reads=[t_rstd], writes=[t_rstd])

        def rope_store(dst_ap, rows, nb, tnb, cst_idx, cs_t):
            P.op("pool", lambda e: e.tensor_tensor(out=t1[0:rows, :], in0=nb[0:rows, :], in1=cs_t[0:rows, 0, :],
                                                    op=ALU.mult), reads=[tnb, t_cs], writes=[t_t1])
            for bi, (a0, a1) in enumerate(TBLK):
                P.op("pe", lambda e, a0=a0, a1=a1, bi=bi: e.matmul(
                    aux[0:rows, 0, 0:a1 - a0], lhsT=cst[0:rows, cst_idx, 0:rows], rhs=nb[0:rows, a0:a1],
                    start=True, stop=True), reads=[tnb, t_cst], writes=[t_aux])
                P.op("dve", lambda e, a0=a0, a1=a1, bi=bi: e.tensor_tensor(
                    out=t2[0:rows, a0:a1], in0=aux[0:rows, 0, 0:a1 - a0], in1=cs_t[0:rows, 1, a0:a1],
                    op=ALU.mult), reads=[t_aux, t_cs], writes=[t_t2])
            store_bf(dst_ap, rows, lambda o, to: P.op("pool", lambda e: e.tensor_tensor(
                out=o[0:rows, :], in0=t1[0:rows, :], in1=t2[0:rows, :], op=ALU.add),
                reads=[t_t1, t_t2], writes=[to]))

        zi = 0
        deferred = []
        ci = {"qa": 0, "ka": 0, "va": 0, "ux": 0, "uqkv": 0, "cq": 0, "ckv": 0}
        for ct, (name, c0, ncol) in enumerate(COL_TILES):
            wb = ct % NWB
            ws = ct % 2
            P.dma("sp" if ct % 2 == 0 else "act", wst[ws][:], w_d[ct], writes=[t_wst[ws]])
            P.op("pool", lambda e, wb=wb, ws=ws: e.tensor_copy(out=wbf[wb][:], in_=wst[ws][:]),
                 reads=[t_wst[ws]], writes=[t_wbf[wb]])
            pb = ct % 2
            for bi, (a0, a1) in enumerate(TBLK):
                tks = t_hT[0:4] if bi == 0 else (t_hT[4:8] if bi == 1 else t_hT[8:9])
                for c in range(16):
                    P.op("pe", lambda e, pb=pb, bi=bi, a0=a0, a1=a1, c=c, wb=wb, ncol=ncol: e.matmul(
                        pj[pb][0:ncol, bi, 0:a1 - a0], lhsT=wbf[wb][:, c, 0:ncol], rhs=hT[:, c, a0:a1],
                        start=(c == 0), stop=(c == 15)), reads=[t_wbf[wb]] + tks, writes=[t_pj[pb]])

            if deferred:
                deferred.pop()()

            def evac(dst, tdst, rows, func=AF.Identity, pb=pb):
                for bi, (a0, a1) in enumerate(TBLK):
                    P.op("act", lambda e, bi=bi, a0=a0, a1=a1: e.activation(
                        out=dst[0:rows, a0:a1], in_=pj[pb][0:rows, bi, 0:a1 - a0], func=func),
                        reads=[t_pj[pb]], writes=[tdst])

            if name in ("qa", "ka"):
                i = ci[name]; ci[name] += 1
                evac(raw[0], t_raw[0], 128)
                evac(sqb[0], t_sqb[0], 128, AF.Square)

                def part_b(i=i, name=name):
                    ss_to_rstd([(sqb[0], t_sqb[0], 128)], 128 * EPS)
                    gcol = 0 if name == "qa" else 1
                    P.op("dve", lambda e: e.scalar_tensor_tensor(
                        out=nbf[0][:], in0=raw[0][:], scalar=gs[:, gcol:gcol + 1], in1=rstd[:],
                        op0=ALU.mult, op1=ALU.mult), reads=[t_raw[0], t_gs, t_rstd], writes=[t_nbf[0]])
                    rope_store((qa_o if name == "qa" else ka_o)[i], 128, nbf[0], t_nbf[0], 1, cs)
                deferred.append(part_b)
            elif name == "va":
                i = ci[name]; ci[name] += 1
                store_bf(va_o[i], 128, lambda o, to: evac(o, to, 128))
            elif name == "z":
                store_f32(zs_o[zi], 128, lambda o, to: evac(o, to, 128, AF.Silu)); zi += 1
            elif name == "ux":
                i = ci[name]; ci[name] += 1
                store_f32(ux_o[i], 128, lambda o, to: evac(o, to, 128))
            elif name == "uqkv":
                i = ci[name]; ci[name] += 1
                store_f32(uqkv_o[i], 128, lambda o, to: evac(o, to, 128))
            elif name == "uab":
                store_f32(uab_o, 16, lambda o, to: evac(o, to, 16))
            elif name in ("cq", "ckv"):
                i = ci[name]; ci[name] += 1
                nch = 3 if name == "cq" else 2
                evac(raw[i], t_raw[i], 128)
                evac(sqb[i], t_sqb[i], 128, AF.Square)
                if i == nch - 1:
                    ss_to_rstd([(sqb[k], t_sqb[k], 128) for k in range(nch)], (384 if name == "cq" else 256) * EPS)
                    g0 = 2 if name == "cq" else 5
                    for k in range(nch):
                        P.op("dve", lambda e, k=k, g0=g0: e.scalar_tensor_tensor(
                            out=nbf[k][:], in0=raw[k][:], scalar=gs[:, g0 + k:g0 + k + 1], in1=rstd[:],
                            op0=ALU.mult, op1=ALU.mult), reads=[t_raw[k], t_gs, t_rstd], writes=[t_nbf[k]])
                    if name == "cq":
                        for h in range(4):
                            for (pbi, col0, rows) in ((0, h * 192, 128), (1, h * 192 + 128, 64)):
                                for bi, (a0, a1) in enumerate(TBLK):
                                    for k in range(3):
                                        P.op("pe", lambda e, pbi=pbi, col0=col0, rows=rows, bi=bi, a0=a0, a1=a1, k=k: e.matmul(
                                            pj[pbi][0:rows, bi, 0:a1 - a0], lhsT=wuq[:, k, col0:col0 + rows],
                                            rhs=nbf[k][:, a0:a1], start=(k == 0), stop=(k == 2)),
                                            reads=[t_wuq, t_nbf[k]], writes=[t_pj[pbi]])
                            evac(raw[3], t_raw[3], 128, pb=0)
                            evac(sqb[0], t_sqb[0], 128, AF.Square, pb=0)
                            evac(raw[4], t_raw[4], 64, pb=1)
                            evac(sqb[1], t_sqb[1], 64, AF.Square, pb=1)
                            ss_to_rstd([(sqb[0], t_sqb[0], 128), (sqb[1], t_sqb[1], 64)], 192 * EPS)
                            store_bf(qdn_o[h], 128, lambda o, to: P.op("dve", lambda e: e.scalar_tensor_tensor(
                                out=o[:], in0=raw[3][:], scalar=gs[:, 7:8], in1=rstd[:], op0=ALU.mult, op1=ALU.mult),
                                reads=[t_raw[3], t_gs, t_rstd], writes=[to]))
                            P.op("dve", lambda e: e.scalar_tensor_tensor(
                                out=nbf[3][0:64, :], in0=raw[4][0:64, :], scalar=gs[0:64, 8:9], in1=rstd[0:64, :],
                                op0=ALU.mult, op1=ALU.mult), reads=[t_raw[4], t_gs, t_rstd], writes=[t_nbf[3]])
                            rope_store(qdr_o[h], 64, nbf[3], t_nbf[3], 2, cs64)
            elif name == "kr":
                evac(kr_raw, t_kr, 64)
                evac(kr_sq, t_krsq, 64, AF.Square)
                for h in range(4):
                    for (pbi, col0) in ((0, h * 256), (1, h * 256 + 128)):
                        for bi, (a0, a1) in enumerate(TBLK):
                            for k in range(2):
                                P.op("pe", lambda e, pbi=pbi, col0=col0, bi=bi, a0=a0, a1=a1, k=k: e.matmul(
                                    pj[pbi][:, bi, 0:a1 - a0], lhsT=wukv[:, k, col0:col0 + 128],
                                    rhs=nbf[k][:, a0:a1], start=(k == 0), stop=(k == 1)),
                                    reads=[t_wukv, t_nbf[k]], writes=[t_pj[pbi]])
                    evac(raw[3], t_raw[3], 128, pb=0)
                    evac(sqb[0], t_sqb[0], 128, AF.Square, pb=0)
                    store_bf(vd_o[h], 128, lambda o, to: evac(o, to, 128, pb=1))
                    ss_to_rstd([(sqb[0], t_sqb[0], 128), (kr_sq, t_krsq, 64)], 192 * EPS)
                    store_bf(kdn_o[h], 128, lambda o, to: P.op("dve", lambda e: e.scalar_tensor_tensor(
                        out=o[:], in0=raw[3][:], scalar=gs[:, 9:10], in1=rstd[:], op0=ALU.mult, op1=ALU.mult),
                        reads=[t_raw[3], t_gs, t_rstd], writes=[to]))
                    P.op("dve", lambda e: e.scalar_tensor_tensor(
                        out=nbf[2][0:64, :], in0=kr_raw[0:64, :], scalar=gs[0:64, 10:11], in1=rstd[0:64, :],
                        op0=ALU.mult, op1=ALU.mult), reads=[t_kr, t_gs, t_rstd], writes=[t_nbf[2]])
                    rope_store(kdr_o[h], 64, nbf[2], t_nbf[2], 2, cs64)
        P.wait_all("sp", out_toks)
        P.emit()
    return nc


def _fm(v):
    v = np.asarray(v, np.float32)
    return np.ascontiguousarray(v.reshape(-1, 128).T)


def _rope_tables():
    t = np.arange(SEQ)
    rows = (t // 64).astype(np.float64)
    cols = (t % 64).astype(np.float64)

    def tab(half, reps):
        inv = 10000.0 ** (-np.arange(half, dtype=np.float64) / half)
        ar = rows[None, :] * inv[:, None]
        ac = cols[None, :] * inv[:, None]
        ang = np.concatenate([ar, ar, ac, ac], axis=0)
        return np.cos(ang), np.sin(ang)
    c128, s128 = tab(32, 4)
    c64, s64 = tab(16, 4)
    return (c128.astype(np.float32), s128.astype(np.float32), c64.astype(np.float32), s64.astype(np.float32))


def _rot_mats():
    def rot(n):
        q = n // 4
        R = np.zeros((n, n), np.float32)
        for base in (0, 2 * q):
            for i in range(q):
                R[base + i, base + q + i] = -1.0
                R[base + q + i, base + i] = 1.0
        return R
    cst = np.zeros((128, 3, 128), np.float32)
    cst[:, 0, :] = np.eye(128, dtype=np.float32)
    cst[:, 1, :] = rot(128).T
    cst[0:64, 2, 0:64] = rot(64).T
    return cst.astype(NPBF)


_CONST = {}


def _consts():
    if not _CONST:
        _CONST["rope"] = _rope_tables()
        _CONST["cst"] = _rot_mats()
    return _CONST


def prep_k1(inputs, l, x_full, ctx_full, ada):
    cs = _consts()
    c128, s128, c64, s64 = cs["rope"]
    w_in = np.asarray(inputs["w_in"][l], np.float32)
    wt = np.zeros((NCT, 128, 16, 128), np.float32)
    for ct, (name, c0, ncol) in enumerate(COL_TILES):
        wt[ct, :, :, 0:ncol] = w_in[:, c0:c0 + ncol].reshape(16, 128, ncol).transpose(1, 0, 2)
    mod = np.stack([_fm(inputs["norm_w"][l]), _fm(ada[l, 0, 2048:4096]), _fm(ada[l, 0, 0:2048]),
                    _fm(ada[l, 1, 2048:4096]), _fm(ada[l, 1, 0:2048])], axis=1)
    gn = np.zeros((128, 8), np.float32)
    gn[:, 0] = inputs["attn_q_norm"][l]
    gn[:, 1] = inputs["attn_k_norm"][l]
    gn[:, 2:5] = _fm(inputs["mla_q_norm"][l])
    gn[:, 5:7] = _fm(inputs["mla_kv_norm"][l])
    g2 = np.zeros((128, 4), np.float32)
    g2[:, 0] = inputs["mla_q_qk_norm"][l][0:128]
    g2[0:64, 1] = inputs["mla_q_qk_norm"][l][128:192]
    g2[:, 2] = inputs["mla_k_qk_norm"][l][0:128]
    g2[0:64, 3] = inputs["mla_k_qk_norm"][l][128:192]
    wuq = np.ascontiguousarray(np.asarray(inputs["mla_w_uq"][l], np.float32).reshape(3, 128, 768).transpose(1, 0, 2))
    wukv = np.ascontiguousarray(np.asarray(inputs["mla_w_ukv"][l], np.float32).reshape(2, 128, 1024).transpose(1, 0, 2))
    in_maps = []
    for i in range(NCORES):
        xs = np.concatenate([x_full[i * TOK:(i + 1) * TOK], ctx_full[i * CTK:(i + 1) * CTK]], axis=0)
        csa = np.zeros((128, 2, NT), np.float32)
        csa[:, 0, 0:TOK] = c128[:, i * TOK:(i + 1) * TOK]
        csa[:, 1, 0:TOK] = s128[:, i * TOK:(i + 1) * TOK]
        csa[:, 0, TOK:] = 1.0
        csb = np.zeros((64, 2, NT), np.float32)
        csb[:, 0, 0:TOK] = c64[:, i * TOK:(i + 1) * TOK]
        csb[:, 1, 0:TOK] = s64[:, i * TOK:(i + 1) * TOK]
        csb[:, 0, TOK:] = 1.0
        in_maps.append({"x": np.ascontiguousarray(xs, np.float32), "mod": mod, "w": wt, "gn": gn, "g2": g2,
                        "wuq": wuq, "wukv": wukv, "cs": csa, "cs64": csb, "cst": cs["cst"]})
    return in_maps


NKT = NALL // 128
VW = 136


def build_k2a():
    nc = bass.Bass("TRN2", target_bir_lowering=False)
    dt_in = lambda name, shape, dt=F32: nc.dram_tensor(name, shape, dt, kind="ExternalInput").ap()
    dt_out = lambda name, shape, dt=F32: nc.dram_tensor(name, shape, dt, kind="ExternalOutput").ap()
    qm_d = dt_in("qm", [128, 8, NT], BF16)
    qr_d = dt_in("qr", [64, 4, NT], BF16)
    km_d = dt_in("km", [6, 128, NALL], BF16)
    kr_d = dt_in("kr", [4, 64, NALL], BF16)
    va_d = dt_in("va", [6, 128, NKT, VW], BF16)
    o_d = dt_out("o", [8, NT, 128])
    out_toks = []
    with contextlib.ExitStack() as es:
        P = Prog(nc, es)
        sb = lambda name, shape, dt=F32: es.enter_context(nc.sbuf_tensor("s_" + name, shape, dt))
        ps = lambda name, shape, dt=F32: es.enter_context(nc.psum_tensor("p_" + name, shape, dt))
        qm = sb("qm", [128, 8, NT], BF16); qr = sb("qr", [64, 4, NT], BF16); t_q = Tok()
        km = [sb("km%d" % i, [128, NALL], BF16) for i in range(2)]; t_km = [Tok(), Tok()]
        kr = [sb("kr%d" % i, [64, NALL], BF16) for i in range(2)]; t_kr = [Tok(), Tok()]
        va = [sb("va%d" % i, [128, NKT, VW], BF16) for i in range(2)]; t_va = [Tok(), Tok()]
        NS = 3
        s_ps = [ps("s%d" % i, [128, 512]) for i in range(NS)]; t_s = [Tok() for _ in range(NS)]
        pT = [sb("pT%d" % i, [128, 512], BF16) for i in range(NS)]; t_p = [Tok() for _ in range(NS)]
        o_ps = [ps("o%d" % i, [128, 512]) for i in range(4)]; t_o = [Tok() for _ in range(4)]
        rcp = sb("rcp", [128, 4]); t_rcp = Tok()
        o_sb = [sb("osb%d" % i, [128, 4, 128]) for i in range(2)]; t_osb = [Tok(), Tok()]
        P.dma("sp", qm[:], qm_d, writes=[t_q])
        P.dma("sp", qr[:], qr_d, writes=[t_q])
        slot = -1
        kvi = 0
        nfin = 0
        for h in range(8):
            mla = h >= 4
            if mla or h % 2 == 0:
                slot += 1
                sl = slot % 2
                P.dma("sp", km[sl][:], km_d[kvi], writes=[t_km[sl]])
                if mla:
                    P.dma("sp", kr[sl][:], kr_d[h - 4], writes=[t_kr[sl]])
                P.dma("sp", va[sl][:], va_d[kvi], writes=[t_va[sl]])
                kvi += 1
            scale = (192.0 if mla else 128.0) ** -0.5
            def do_tile(h, sl, mla, scale, q0, q1, nkt, ob):
                nq = q1 - q0
                nj = (nq + 127) // 128

                def S(kt):
                    i = kt % NS
                    rd = [t_q, t_km[sl]] + ([t_kr[sl]] if mla else [])
                    P.op("pe", lambda e, i=i, kt=kt: e.matmul(
                        s_ps[i][:, 0:nq], lhsT=km[sl][:, kt * 128:(kt + 1) * 128], rhs=qm[:, h, q0:q1],
                        start=True, stop=not mla), reads=rd, writes=[t_s[i]])
                    if mla:
                        P.op("pe", lambda e, i=i, kt=kt: e.matmul(
                            s_ps[i][:, 0:nq], lhsT=kr[sl][:, kt * 128:(kt + 1) * 128], rhs=qr[:, h - 4, q0:q1],
                            start=False, stop=True), reads=rd, writes=[t_s[i]])

                def E(kt):
                    i = kt % NS
                    P.op("act", lambda e, i=i: e.activation(out=pT[i][:, 0:nq], in_=s_ps[i][:, 0:nq], func=AF.Exp,
                                                            scale=scale), reads=[t_s[i]], writes=[t_p[i]])

                def PV(kt):
                    i = kt % NS
                    for j in range(nj):
                        m = min(128, nq - j * 128)
                        P.op("pe", lambda e, i=i, j=j, m=m, kt=kt: e.matmul(
                            o_ps[j][0:m, 0:129], lhsT=pT[i][:, j * 128:j * 128 + m], rhs=va[sl][:, kt, 0:129],
                            start=(kt == 0), stop=(kt == nkt - 1)), reads=[t_p[i], t_va[sl]], writes=[t_o[j]])

                for kt in range(min(NS, nkt)):
                    S(kt)
                    E(kt)
                for kt in range(nkt):
                    PV(kt)
                    if kt + NS < nkt:
                        S(kt + NS)
                        E(kt + NS)
                for j in range(nj):
                    m = min(128, nq - j * 128)
                    P.op("dve", lambda e, j=j, m=m: e.reciprocal(out=rcp[0:m, j:j + 1], in_=o_ps[j][0:m, 128:129]),
                         reads=[t_o[j]], writes=[t_rcp])
                    P.op("dve", lambda e, j=j, m=m, ob=ob: e.tensor_scalar(
                        out=o_sb[ob][0:m, j, :], in0=o_ps[j][0:m, 0:128], scalar1=rcp[0:m, j:j + 1], scalar2=None,
                        op0=ALU.mult), reads=[t_o[j], t_rcp], writes=[t_osb[ob]])
                tk = Tok(); out_toks.append(tk)
                if nq == 512:
                    P.dma("sp", o_d[h, q0:q1, :].rearrange("(j p) d -> p j d", p=128), o_sb[ob][:],
                          reads=[t_osb[ob]], writes=[tk])
                else:
                    P.dma("sp", o_d[h, q0:q1, :], o_sb[ob][0:nq, 0, :], reads=[t_osb[ob]], writes=[tk])

            for (q0, q1, nkt) in ((0, 512, NKT), (512, 1024, NKT), (1024, NT, 2)):
                do_tile(h, sl, mla, scale, q0, q1, nkt, nfin % 2)
                nfin += 1
        P.wait_all("sp", out_toks)
        P.emit()
    return nc


SEGS = ((0, CTX), (CTX, NALL))
BLK512 = [(i, min(i + 512, NALL)) for i in range(0, NALL, 512)]


def emit_conv5(P, eng_a, eng_b, y, ty, x, tx, w, tw, wcol0, bias_ap=None):
    for (s0, s1) in SEGS:
        if bias_ap is None:
            P.op(eng_a, lambda e, s0=s0, s1=s1: e.tensor_scalar(
                out=y[:, s0:s1], in0=x[:, s0:s1], scalar1=w[:, wcol0 + 2:wcol0 + 3], scalar2=None, op0=ALU.mult),
                reads=[tx, tw], writes=[ty])
        else:
            P.op(eng_a, lambda e, s0=s0, s1=s1: e.tensor_scalar(
                out=y[:, s0:s1], in0=x[:, s0:s1], scalar1=w[:, wcol0 + 2:wcol0 + 3], scalar2=bias_ap,
                op0=ALU.mult, op1=ALU.add), reads=[tx, tw], writes=[ty])
        for o in (-2, -1, 1, 2):
            a0 = s0 + max(0, -o)
            a1 = s1 - max(0, o)
            P.op(eng_a, lambda e, a0=a0, a1=a1, o=o: e.scalar_tensor_tensor(
                out=y[:, a0:a1], in0=x[:, a0 + o:a1 + o], scalar=w[:, wcol0 + o + 2:wcol0 + o + 3], in1=y[:, a0:a1],
                op0=ALU.mult, op1=ALU.add), reads=[tx, tw, ty], writes=[ty])


def build_k2b():
    nc = bass.Bass("TRN2", target_bir_lowering=False)
    dt_in = lambda name, shape, dt=F32: nc.dram_tensor(name, shape, dt, kind="ExternalInput").ap()
    dt_out = lambda name, shape, dt=F32: nc.dram_tensor(name, shape, dt, kind="ExternalOutput").ap()
    ux_d = dt_in("ux", [128, NALL])
    prm_d = dt_in("prm", [128, 16])
    w_d = dt_in("w", [128, 2, 128])
    h_d = dt_out("h", [128, NALL])
    with contextlib.ExitStack() as es:
        P = Prog(nc, es)
        sb = lambda name, shape, dt=F32: es.enter_context(nc.sbuf_tensor("s_" + name, shape, dt))
        ps = lambda name, shape, dt=F32: es.enter_context(nc.psum_tensor("p_" + name, shape, dt))
        ux = sb("ux", [128, NALL]); t_ux = Tok()
        xs = sb("xs", [128, NALL]); t_xs = Tok()
        xb = sb("xb", [128, NALL], BF16); t_xb = Tok()
        av = sb("av", [128, NALL]); t_av = Tok()
        bv = sb("bv", [128, NALL]); t_bv = Tok()
        prm = sb("prm", [128, 16]); t_prm = Tok()
        c1 = sb("c1", [128, 2]); t_c1 = Tok()
        wf = sb("wf", [128, 2, 128]); wb = sb("wb", [128, 2, 128], BF16); t_w = Tok()
        rr = [sb("rr%d" % i, [128, 512]) for i in range(2)]; t_rr = [Tok(), Tok()]
        ii = [sb("ii%d" % i, [128, 512]) for i in range(2)]; t_ii = [Tok(), Tok()]
        pr = [ps("pr%d" % i, [128, 512]) for i in range(2)]; t_pr = [Tok(), Tok()]
        pi = [ps("pi%d" % i, [128, 512]) for i in range(2)]; t_pi = [Tok(), Tok()]
        P.dma("sp", ux[:], ux_d, writes=[t_ux])
        P.dma("sp", prm[:], prm_d, writes=[t_prm])
        P.dma("sp", wf[:], w_d, writes=[t_w])
        P.op("pool", lambda e: e.tensor_copy(out=wb[:], in_=wf[:]), reads=[t_w], writes=[t_w])
        P.op("act", lambda e: e.activation(out=c1[:, 0:1], in_=prm[:, 8:9], func=AF.Exp, scale=-1.0), reads=[t_prm], writes=[t_c1])
        P.op("act", lambda e: e.activation(out=c1[:, 0:1], in_=c1[:, 0:1], func=AF.Ln, bias=1.0), reads=[t_c1], writes=[t_c1])
        P.op("dve", lambda e: e.tensor_scalar(out=c1[:, 1:2], in0=c1[:, 0:1], scalar1=-16.0, scalar2=None, op0=ALU.mult), reads=[t_c1], writes=[t_c1])
        P.op("dve", lambda e: e.tensor_scalar(out=c1[:, 0:1], in0=c1[:, 0:1], scalar1=-8.0, scalar2=None, op0=ALU.mult), reads=[t_c1], writes=[t_c1])
        emit_conv5(P, "dve", "pool", xs, t_xs, ux, t_ux, prm, t_prm, 0, bias_ap=prm[:, 5:6])
        P.op("pool", lambda e: e.tensor_copy(out=xb[:], in_=xs[:]), reads=[t_xs], writes=[t_xb])
        for bi, (a0, a1) in enumerate(BLK512):
            n = a1 - a0
            b = bi % 2
            P.op("pe", lambda e, b=b, a0=a0, a1=a1, n=n: e.matmul(pr[b][:, 0:n], lhsT=wb[:, 0, :], rhs=xb[:, a0:a1], start=True, stop=True),
                 reads=[t_w, t_xb], writes=[t_pr[b]])
            P.op("pe", lambda e, b=b, a0=a0, a1=a1, n=n: e.matmul(pi[b][:, 0:n], lhsT=wb[:, 1, :], rhs=xb[:, a0:a1], start=True, stop=True),
                 reads=[t_w, t_xb], writes=[t_pi[b]])
            P.op("act", lambda e, b=b, n=n: e.activation(out=rr[b][:, 0:n], in_=pr[b][:, 0:n], func=AF.Sigmoid, bias=prm[:, 6:7]),
                 reads=[t_pr[b], t_prm], writes=[t_rr[b]])
            P.op("act", lambda e, b=b, n=n: e.activation(out=ii[b][:, 0:n], in_=pi[b][:, 0:n], func=AF.Sigmoid, bias=prm[:, 7:8]),
                 reads=[t_pi[b], t_prm], writes=[t_ii[b]])
            P.op("act", lambda e, b=b, a0=a0, a1=a1, n=n: e.activation(out=av[:, a0:a1], in_=rr[b][:, 0:n], func=AF.Exp, scale=c1[:, 0:1]),
                 reads=[t_rr[b], t_c1], writes=[t_av])
            P.op("act", lambda e, b=b, n=n: e.activation(out=rr[b][:, 0:n], in_=rr[b][:, 0:n], func=AF.Exp, scale=c1[:, 1:2]),
                 reads=[t_rr[b], t_c1], writes=[t_rr[b]])
            P.op("act", lambda e, b=b, n=n: e.activation(out=rr[b][:, 0:n], in_=rr[b][:, 0:n], func=AF.Sqrt, scale=-1.0, bias=1.0),
                 reads=[t_rr[b]], writes=[t_rr[b]])
            P.op("dve", lambda e, b=b, a0=a0, a1=a1, n=n: e.tensor_tensor(out=ii[b][:, 0:n], in0=ii[b][:, 0:n], in1=xs[:, a0:a1], op=ALU.mult),
                 reads=[t_ii[b], t_xs], writes=[t_ii[b]])
            P.op("dve", lambda e, b=b, a0=a0, a1=a1, n=n: e.tensor_tensor(out=bv[:, a0:a1], in0=ii[b][:, 0:n], in1=rr[b][:, 0:n], op=ALU.mult),
                 reads=[t_ii[b], t_rr[b]], writes=[t_bv])
        SC = 2112
        for i, s0 in enumerate(range(0, NALL, SC)):
            init = 0.0 if i == 0 else ux[:, s0 - 1:s0]
            P.op("dve", lambda e, s0=s0, init=init: e.tensor_tensor_scan(
                out=ux[:, s0:s0 + SC], data0=av[:, s0:s0 + SC], data1=bv[:, s0:s0 + SC], initial=init,
                op0=ALU.mult, op1=ALU.add), reads=[t_av, t_bv, t_ux], writes=[t_ux])
        tk = Tok()
        P.dma("sp", h_d, ux[:], reads=[t_ux], writes=[tk])
        P.wait_all("sp", [tk])
        P.emit()
    return nc


NCH = NALL // 64


def build_c1():
    nc = bass.Bass("TRN2", target_bir_lowering=False)
    dt_in = lambda name, shape, dt=F32: nc.dram_tensor(name, shape, dt, kind="ExternalInput").ap()
    dt_out = lambda name, shape, dt=F32: nc.dram_tensor(name, shape, dt, kind="ExternalOutput").ap()
    u_d = dt_in("u", [3, 128, NALL])
    cw_d = dt_in("cw", [128, 16])
    ab_d = dt_in("ab", [64, 2, NCH])
    sc_d = dt_in("sc", [64, 2])
    o_d = dt_out("qkv", [3, 128, NALL])
    bg_d = dt_out("bg", [64, 2, NCH])
    with contextlib.ExitStack() as es:
        P = Prog(nc, es)
        sb = lambda name, shape, dt=F32: es.enter_context(nc.sbuf_tensor("s_" + name, shape, dt))
        ps = lambda name, shape, dt=F32: es.enter_context(nc.psum_tensor("p_" + name, shape, dt))
        x = [sb("x%d" % i, [128, NALL]) for i in range(2)]; t_x = [Tok(), Tok()]
        y = [sb("y%d" % i, [128, NALL]) for i in range(2)]; t_y = [Tok(), Tok()]
        sq = sb("sq", [128, NALL]); t_sq = Tok()
        cw = sb("cw", [128, 16]); t_cw = Tok()
        ones = sb("ones", [128, 128]); t_ones = Tok()
        ab = sb("ab", [64, 2, NCH]); t_ab = Tok()
        sc = sb("sc", [64, 2]); t_sc = Tok()
        bg = sb("bg", [64, 2, NCH]); t_bg = Tok()
        pp = [ps("pp%d" % i, [128, 512]) for i in range(2)]; t_pp = [Tok(), Tok()]
        P.dma("sp", cw[:], cw_d, writes=[t_cw])
        P.dma("sp", ab[:], ab_d, writes=[t_ab])
        P.dma("sp", sc[:], sc_d, writes=[t_sc])
        P.op("pool", lambda e: e.memset(ones[:], 1.0), writes=[t_ones])
        P.op("act", lambda e: e.activation(out=bg[:, 0, :], in_=ab[:, 0, :], func=AF.Sigmoid), reads=[t_ab], writes=[t_bg])
        P.op("act", lambda e: e.activation(out=bg[:, 1, :], in_=ab[:, 1, :], func=AF.Exp, bias=sc[:, 1:2]), reads=[t_ab, t_sc], writes=[t_bg])
        P.op("act", lambda e: e.activation(out=bg[:, 1, :], in_=bg[:, 1, :], func=AF.Ln, bias=1.0), reads=[t_bg], writes=[t_bg])
        P.op("act", lambda e: e.activation(out=sc[:, 0:1], in_=sc[:, 0:1], func=AF.Exp), reads=[t_sc], writes=[t_sc])
        P.op("dve", lambda e: e.tensor_scalar(out=bg[:, 1, :], in0=bg[:, 1, :], scalar1=sc[:, 0:1], scalar2=-1.0,
                                              op0=ALU.mult, op1=ALU.mult), reads=[t_bg, t_sc], writes=[t_bg])
        tk_bg = Tok()
        P.dma("sp", bg_d, bg[:], reads=[t_bg], writes=[tk_bg])
        outs = [tk_bg]
        for i in range(3):
            b = i % 2
            P.dma("sp", x[b][:], u_d[i], writes=[t_x[b]])
            emit_conv5(P, "dve", "pool", y[b], t_y[b], x[b], t_x[b], cw, t_cw, 5 * i)
            P.op("act", lambda e, b=b: e.activation(out=y[b][:], in_=y[b][:], func=AF.Silu), reads=[t_y[b]], writes=[t_y[b]])
            if i < 2:
                P.op("pool", lambda e, b=b: e.tensor_tensor(out=sq[:], in0=y[b][:], in1=y[b][:], op=ALU.mult),
                     reads=[t_y[b]], writes=[t_sq])
                for bi, (a0, a1) in enumerate(BLK512):
                    n = a1 - a0
                    pb = bi % 2
                    P.op("pe", lambda e, pb=pb, a0=a0, a1=a1, n=n: e.matmul(pp[pb][:, 0:n], lhsT=ones[:], rhs=sq[:, a0:a1],
                                                                      start=True, stop=True), reads=[t_ones, t_sq], writes=[t_pp[pb]])
                    P.op("act", lambda e, pb=pb, a0=a0, a1=a1, n=n, b=b: e.activation(out=x[b][:, a0:a1], in_=pp[pb][:, 0:n], func=AF.Sqrt, bias=EPS),
                         reads=[t_pp[pb]], writes=[t_x[b]])
                P.op("dve", lambda e, b=b: e.reciprocal(out=x[b][:], in_=x[b][:]), reads=[t_x[b]], writes=[t_x[b]])
                fac = (128.0 ** -0.5) if i == 0 else 1.0
                P.op("dve", lambda e, b=b, fac=fac: e.scalar_tensor_tensor(out=y[b][:], in0=y[b][:], scalar=fac, in1=x[b][:],
                                                                       op0=ALU.mult, op1=ALU.mult), reads=[t_y[b], t_x[b]], writes=[t_y[b]])
            tk = Tok(); outs.append(tk)
            P.dma("sp", o_d[i], y[b][:], reads=[t_y[b]], writes=[tk])
        P.wait_all("sp", outs)
        P.emit()
    return nc


GRP = 8


def _dn_consts():
    c = np.zeros((64, 6, 64), np.float32)
    p = np.arange(64)[:, None]
    f = np.arange(64)[None, :]
    c[:, 0] = (p <= f)
    c[:, 1] = -(p <= f).astype(np.float32)
    c[:, 2] = -(p > f).astype(np.float32)
    c[:, 3] = -(f > p).astype(np.float32)
    c[:, 4] = (f >= p)
    c[:, 5] = (f == p)
    return c


def build_c2():
    nc = bass.Bass("TRN2", target_bir_lowering=False)
    dt_in = lambda name, shape, dt=F32: nc.dram_tensor(name, shape, dt, kind="ExternalInput").ap()
    dt_out = lambda name, shape, dt=F32: nc.dram_tensor(name, shape, dt, kind="ExternalOutput").ap()
    qT_d = dt_in("qT", [128, NALL]); kT_d = dt_in("kT", [128, NALL])
    ktm_d = dt_in("ktm", [64, NCH, 128]); vtm_d = dt_in("vtm", [64, NCH, 128])
    bg_d = dt_in("bg", [64, 2, NCH]); bbc_d = dt_in("bbc", [64, NALL]); cst_d = dt_in("cst", [64, 6, 64]); id_d = dt_in("ident", [128, 128])
    o_d = dt_out("o", [64, NCH, 128])
    with contextlib.ExitStack() as es:
        P = Prog(nc, es)
        sb = lambda name, shape, dt=F32: es.enter_context(nc.sbuf_tensor("s_" + name, shape, dt))
        ps = lambda name, shape, dt=F32: es.enter_context(nc.psum_tensor("p_" + name, shape, dt))
        cst = sb("cst", [64, 6, 64]); t_cst = Tok()
        bg = sb("bg", [64, 2, NCH]); t_bg = Tok()
        bbc = sb("bbc", [64, NALL]); t_bbc = Tok()
        ones = sb("ones", [64, 128]); t_ones = Tok()
        gb = sb("gb", [64, NCH, 64]); t_gb = Tok()
        gc = sb("gc", [64, NCH]); eg = sb("eg", [64, NCH]); beg = sb("beg", [64, NCH]); dk = sb("dk", [64, NCH]); t_gc = Tok()
        egl = sb("egl", [128, NCH]); t_egl = Tok()
        S = sb("S", [128, 128]); t_S = Tok()
        qT = [sb("qT%d" % i, [128, GRP * 64]) for i in range(2)]; kT = [sb("kT%d" % i, [128, GRP * 64]) for i in range(2)]
        ktm = [sb("ktm%d" % i, [64, GRP, 128]) for i in range(2)]; vtm = [sb("vtm%d" % i, [64, GRP, 128]) for i in range(2)]
        t_in = [Tok(), Tok()]
        E8 = sb("E8", [64, GRP, 64]); ET8 = sb("ET8", [64, GRP, 64]); t_E = Tok(); t_ET = Tok(); t_QK = [Tok(), Tok()]
        Pa = sb("Pa", [64, GRP, 64]); Pb = sb("Pb", [64, GRP, 64]); Pta = sb("Pta", [64, GRP, 64]); Ptb = sb("Ptb", [64, GRP, 64])
        t_P = [Tok(), Tok()]; t_Pt = [Tok(), Tok()]
        Tt = sb("Tt", [64, GRP, 64]); t_Tt = Tok()
        Rv = sb("Rv", [64, GRP, 128]); Rw = sb("Rw", [64, GRP, 128]); t_R = Tok()
        QK = [sb("QK%d" % i, [64, GRP, 64]) for i in range(2)]; kd = [sb("kd%d" % i, [64, GRP, 128]) for i in range(2)]
        U8 = [sb("U8%d" % i, [64, GRP, 128]) for i in range(2)]; W8 = [sb("W8%d" % i, [128, GRP, 64]) for i in range(2)]
        t_prep = [Tok(), Tok()]
        o_sb = [sb("osb%d" % i, [64, GRP, 128]) for i in range(2)]; t_osb = [Tok(), Tok()]
        vn = [sb("vn%d" % i, [64, 128]) for i in range(2)]; t_vn = [Tok(), Tok()]
        tmp = [sb("tmp%d" % i, [64, 128]) for i in range(2)]; t_tmp = [Tok(), Tok()]
        gb0 = ps("g0", [128, 512]); gb1 = ps("g1", [128, 512]); gb2 = ps("g2", [128, 512]); t_g = [Tok(), Tok(), Tok()]
        g0, g1, g2 = [t[0:64, :].rearrange("p (g c) -> p g c", c=64) for t in (gb0, gb1, gb2)]
        gm = [t[:, :].rearrange("p (g c) -> p g c", c=128) for t in (gb0, gb1, gb2)]
        gU = ps("gU", [64, 4, 128]); t_gU = Tok()
        gW = ps("gW", [128, GRP, 64]); t_gW = Tok()
        q3 = ps("q3", [64, 128]); t_q3 = Tok()
        q12 = ps("q12", [64, 2, 128]); t_q12 = Tok(); t_q12a = t_q12; t_q12b = t_q12
        q4 = ps("q4", [128, 128]); t_q4 = Tok()
        P.dma("sp", cst[:], cst_d, writes=[t_cst])
        P.dma("sp", bg[:], bg_d, writes=[t_bg])
        P.dma("sp", bbc[:], bbc_d, writes=[t_bbc])
        P.op("pool", lambda e: e.memset(ones[:], 1.0), writes=[t_ones])
        P.op("pool", lambda e: e.memset(S[:], 0.0), writes=[t_S])
        P.op("dve", lambda e: e.tensor_copy(out=gb[:], in_=bg[:, 1, :].unsqueeze(2).broadcast_to([64, NCH, 64])),
             reads=[t_bg], writes=[t_gb])
        P.op("pe", lambda e: e.matmul(g0[:, 0:3, :].rearrange("p a b -> p (a b)")[:, 0:NCH], lhsT=cst[:, 0, :], rhs=bg[:, 1, :],
                                      start=True, stop=True), reads=[t_cst, t_bg], writes=[t_g[0]])
        P.op("pe", lambda e: e.matmul(gW[:, 0:3, :].rearrange("p a b -> p (a b)")[:, 0:NCH], lhsT=ones[:, :], rhs=bg[:, 1, :],
                                      start=True, stop=True), reads=[t_ones, t_bg], writes=[t_gW])
        g0f = g0[:, 0:3, :].rearrange("p a b -> p (a b)")[:, 0:NCH]
        gWf = gW[:, 0:3, :].rearrange("p a b -> p (a b)")[:, 0:NCH]
        P.op("dve", lambda e: e.tensor_copy(out=gc[:], in_=g0f), reads=[t_g[0]], writes=[t_gc])
        P.op("act", lambda e: e.activation(out=eg[:], in_=g0f, func=AF.Exp), reads=[t_g[0]], writes=[t_gc])
        P.op("act", lambda e: e.activation(out=egl[:], in_=gWf, func=AF.Exp), reads=[t_gW], writes=[t_egl])
        P.op("dve", lambda e: e.tensor_tensor(out=dk[:], in0=gWf[0:64, :], in1=gc[:], op=ALU.subtract), reads=[t_gW, t_gc], writes=[t_gc])
        P.op("act", lambda e: e.activation(out=dk[:], in_=dk[:], func=AF.Exp), reads=[t_gc], writes=[t_gc])
        P.op("dve", lambda e: e.tensor_tensor(out=beg[:], in0=eg[:], in1=bg[:, 0, :], op=ALU.mult), reads=[t_gc, t_bg], writes=[t_gc])

        def bc_last(ap2d, n):
            return ap2d.unsqueeze(2).broadcast_to([64, ap2d.shape[1], n])

        def bc_mid(ap2d, g):
            return ap2d.unsqueeze(1).broadcast_to([64, g, ap2d.shape[1]])

        def prep_stages(gi, n0, G):
            st = []
            b = gi % 2
            c0, c1 = n0 * 64, (n0 + G) * 64

            def s_dma():
                P.dma("sp", qT[b][:, 0:G * 64], qT_d[:, c0:c1], writes=[t_in[b]])
                P.dma("sp", kT[b][:, 0:G * 64], kT_d[:, c0:c1], writes=[t_in[b]])
                P.dma("sp", ktm[b][:, 0:G, :], ktm_d[:, n0:n0 + G, :], writes=[t_in[b]])
                P.dma("sp", vtm[b][:, 0:G, :], vtm_d[:, n0:n0 + G, :], writes=[t_in[b]])
            st.append(s_dma)

            def s_mm0():
                for j in range(G):
                    n = n0 + j
                    ks = kT[b][:, j * 64:(j + 1) * 64]
                    P.op("pe", lambda e, j=j, n=n: e.matmul(g0[:, j, :], lhsT=cst[:, 0, :], rhs=gb[:, n, :], start=True, stop=False),
                         reads=[t_cst, t_gb], writes=[t_g[0]])
                    P.op("pe", lambda e, j=j, n=n: e.matmul(g0[:, j, :], lhsT=gb[:, n, :], rhs=cst[:, 1, :], start=False, stop=True),
                         reads=[t_cst, t_gb], writes=[t_g[0]])
                    P.op("pe", lambda e, j=j, ks=ks: e.matmul(g1[:, j, :], lhsT=ks, rhs=ks, start=True, stop=True),
                         reads=[t_in[b]], writes=[t_g[1]])
                    P.op("pe", lambda e, j=j, ks=ks: e.matmul(g2[:, j, :], lhsT=ks, rhs=qT[b][:, j * 64:(j + 1) * 64], start=True, stop=True),
                         reads=[t_in[b]], writes=[t_g[2]])
            st.append(s_mm0)

            def s_min():
                P.op("dve", lambda e: e.tensor_scalar(out=E8[:, 0:G, :], in0=g0[:, 0:G, :], scalar1=0.0, scalar2=None, op0=ALU.min),
                     reads=[t_g[0]], writes=[t_E])
                P.op("dve", lambda e: e.tensor_scalar(out=ET8[:, 0:G, :], in0=g0[:, 0:G, :], scalar1=-1.0, scalar2=0.0, op0=ALU.mult, op1=ALU.min),
                     reads=[t_g[0]], writes=[t_ET])
                P.op("pool", lambda e: e.tensor_tensor(out=Rv[:, 0:G, :], in0=vtm[b][:, 0:G, :], in1=bc_last(bg[:, 0, n0:n0 + G], 128), op=ALU.mult), reads=[t_in[b], t_bg], writes=[t_R])
                P.op("pool", lambda e: e.tensor_tensor(out=Rw[:, 0:G, :], in0=ktm[b][:, 0:G, :], in1=bc_last(beg[:, n0:n0 + G], 128), op=ALU.mult), reads=[t_in[b], t_gc], writes=[t_R])
                P.op("pool", lambda e: e.tensor_tensor(out=kd[b][:, 0:G, :], in0=ktm[b][:, 0:G, :], in1=bc_last(dk[:, n0:n0 + G], 128), op=ALU.mult), reads=[t_in[b], t_gc], writes=[t_prep[b]])
            st.append(s_min)

            def s_exp():
                P.op("act", lambda e: e.activation(out=E8[:, 0:G, :], in_=E8[:, 0:G, :], func=AF.Exp), reads=[t_E], writes=[t_E])
                P.op("act", lambda e: e.activation(out=ET8[:, 0:G, :], in_=ET8[:, 0:G, :], func=AF.Exp), reads=[t_ET], writes=[t_ET])
            st.append(s_exp)

            def s_mul1():
                P.op("dve", lambda e: e.tensor_tensor(out=Pa[:, 0:G, :], in0=g1[:, 0:G, :], in1=E8[:, 0:G, :], op=ALU.mult), reads=[t_g[1], t_E], writes=[t_P[0]])
                P.op("dve", lambda e: e.tensor_tensor(out=Pta[:, 0:G, :], in0=g1[:, 0:G, :], in1=ET8[:, 0:G, :], op=ALU.mult), reads=[t_g[1], t_ET], writes=[t_Pt[0]])
                P.op("dve", lambda e: e.tensor_tensor(out=QK[b][:, 0:G, :], in0=g2[:, 0:G, :], in1=ET8[:, 0:G, :], op=ALU.mult), reads=[t_g[2], t_ET], writes=[t_QK[b]])
            st.append(s_mul1)

            def s_mul2():
                P.op("pool", lambda e: e.tensor_tensor(out=Pa[:, 0:G, :], in0=Pa[:, 0:G, :], in1=bc_mid(cst[:, 2, :], G), op=ALU.mult), reads=[t_cst, t_P[0]], writes=[t_P[0]])
                P.op("pool", lambda e: e.tensor_tensor(out=Pta[:, 0:G, :], in0=Pta[:, 0:G, :], in1=bc_mid(cst[:, 3, :], G), op=ALU.mult), reads=[t_cst, t_Pt[0]], writes=[t_Pt[0]])
                P.op("pool", lambda e: e.tensor_tensor(out=QK[b][:, 0:G, :], in0=QK[b][:, 0:G, :], in1=bc_mid(cst[:, 4, :], G), op=ALU.mult), reads=[t_cst, t_QK[b]], writes=[t_QK[b]])
            st.append(s_mul2)

            def s_mul3():
                P.op("pool", lambda e: e.tensor_tensor(out=Pa[:, 0:G, :], in0=Pa[:, 0:G, :], in1=bc_last(bg[:, 0, n0:n0 + G], 64), op=ALU.mult), reads=[t_bg, t_P[0]], writes=[t_P[0]])
                P.op("pool", lambda e: e.tensor_tensor(out=Pta[:, 0:G, :], in0=Pta[:, 0:G, :], in1=bbc[:, c0:c1].rearrange("p (g c) -> p g c", c=64), op=ALU.mult),
                     reads=[t_bbc, t_Pt[0]], writes=[t_Pt[0]])
            st.append(s_mul3)

            def s_tt0():
                P.op("pool", lambda e: e.tensor_tensor(out=Tt[:, 0:G, :], in0=Pta[:, 0:G, :], in1=bc_mid(cst[:, 5, :], G), op=ALU.add), reads=[t_cst, t_Pt[0]], writes=[t_Tt])
            st.append(s_tt0)
            Pk, Ptk = [Pa, Pb], [Pta, Ptb]
            for k in range(5):
                a, nx = k % 2, (k + 1) % 2

                def s_pw(a=a):
                    for j in range(G):
                        P.op("pe", lambda e, j=j: e.matmul(g0[:, j, :], lhsT=Ptk[a][:, j, :], rhs=Pk[a][:, j, :], start=True, stop=True),
                             reads=[t_P[a], t_Pt[a]], writes=[t_g[0]])
                        P.op("pe", lambda e, j=j: e.matmul(g1[:, j, :], lhsT=Pk[a][:, j, :], rhs=Ptk[a][:, j, :], start=True, stop=True),
                             reads=[t_P[a], t_Pt[a]], writes=[t_g[1]])
                st.append(s_pw)

                def s_cp(nx=nx):
                    P.op("act", lambda e: e.activation(out=Pk[nx][:, 0:G, :], in_=g0[:, 0:G, :], func=AF.Identity), reads=[t_g[0]], writes=[t_P[nx]])
                    P.op("dve", lambda e: e.tensor_copy(out=Ptk[nx][:, 0:G, :], in_=g1[:, 0:G, :]), reads=[t_g[1]], writes=[t_Pt[nx]])
                st.append(s_cp)

                def s_tm(nx=nx):
                    for j in range(G):
                        P.op("pe", lambda e, j=j: e.matmul(g2[:, j, :], lhsT=Pk[nx][:, j, :], rhs=Tt[:, j, :], start=True, stop=True),
                             reads=[t_P[nx], t_Tt], writes=[t_g[2]])
                st.append(s_tm)

                def s_ta():
                    P.op("dve", lambda e: e.tensor_tensor(out=Tt[:, 0:G, :], in0=Tt[:, 0:G, :], in1=g2[:, 0:G, :], op=ALU.add), reads=[t_g[2], t_Tt], writes=[t_Tt])
                st.append(s_ta)

            def s_w():
                for j in range(G):
                    P.op("pe", lambda e, j=j: e.matmul(gW[:, j, :], lhsT=Rw[:, j, :], rhs=Tt[:, j, :], start=True, stop=True), reads=[t_Tt, t_R], writes=[t_gW])
            st.append(s_w)

            def s_wc():
                P.op("dve", lambda e: e.tensor_copy(out=W8[b][:, 0:G, :], in_=gW[:, 0:G, :]), reads=[t_gW], writes=[t_prep[b]])
            st.append(s_wc)
            for h0 in range(0, G, 4):
                h1 = min(G, h0 + 4)

                def s_u(h0=h0, h1=h1):
                    for j in range(h0, h1):
                        P.op("pe", lambda e, j=j: e.matmul(gU[:, j - h0, :], lhsT=Tt[:, j, :], rhs=Rv[:, j, :], start=True, stop=True), reads=[t_Tt, t_R], writes=[t_gU])
                st.append(s_u)

                def s_uc(h0=h0, h1=h1):
                    P.op("act", lambda e: e.activation(out=U8[b][:, h0:h1, :], in_=gU[:, 0:h1 - h0, :], func=AF.Identity), reads=[t_gU], writes=[t_prep[b]])
                st.append(s_uc)
            return st

        def seq_group(gi, n0, G, filler):
            b = gi % 2
            per = (len(filler) + G - 1) // G if filler else 0
            for j in range(G):
                n = n0 + j
                v = n % 2
                P.op("pe", lambda e, j=j: e.matmul(q3[:, :], lhsT=W8[b][:, j, :], rhs=S[:, :], start=True, stop=True), reads=[t_prep[b], t_S], writes=[t_q3])
                P.op("pe", lambda e, j=j: e.matmul(q12[:, 0, :], lhsT=qT[b][:, j * 64:(j + 1) * 64], rhs=S[:, :], start=True, stop=True), reads=[t_in[b], t_S], writes=[t_q12])
                P.op("dve", lambda e, j=j, v=v: e.tensor_tensor(out=vn[v][:], in0=U8[b][:, j, :], in1=q3[:, :], op=ALU.subtract), reads=[t_prep[b], t_q3], writes=[t_vn[v]])
                P.op("pe", lambda e, j=j, v=v: e.matmul(q4[:, :], lhsT=kd[b][:, j, :], rhs=vn[v][:], start=True, stop=True), reads=[t_prep[b], t_vn[v]], writes=[t_q4])
                P.op("pe", lambda e, j=j, v=v: e.matmul(q12[:, 1, :], lhsT=QK[b][:, j, :], rhs=vn[v][:], start=True, stop=True), reads=[t_QK[b], t_vn[v]], writes=[t_q12])
                P.op("dve", lambda e, n=n: e.scalar_tensor_tensor(out=S[:], in0=S[:], scalar=egl[:, n:n + 1], in1=q4[:, :], op0=ALU.mult, op1=ALU.add),
                     reads=[t_S, t_egl, t_q4], writes=[t_S])
                P.op("act", lambda e, n=n, v=v: e.activation(out=tmp[v][:], in_=q12[:, 0, :], func=AF.Identity, scale=eg[:, n:n + 1]), reads=[t_q12, t_gc], writes=[t_tmp[v]])
                P.op("dve", lambda e, j=j, v=v: e.tensor_tensor(out=o_sb[b][:, j, :], in0=tmp[v][:], in1=q12[:, 1, :], op=ALU.add), reads=[t_tmp[v], t_q12], writes=[t_osb[b]])
                for f in filler[j * per:(j + 1) * per]:
                    f()
            tk = Tok(); outs.append(tk)
            P.dma("sp", o_d[:, n0:n0 + G, :], o_sb[b][:, 0:G, :], reads=[t_osb[b]], writes=[tk])

        outs = []
        groups = [(gi, n0, min(GRP, NCH - n0)) for gi, n0 in enumerate(range(0, NCH, GRP))]
        for f in prep_stages(*groups[0]):
            f()
        for idx, grp in enumerate(groups):
            filler = prep_stages(*groups[idx + 1]) if idx + 1 < len(groups) else []
            seq_group(grp[0], grp[1], grp[2], filler)
        P.wait_all("sp", outs)
        P.emit()
    return nc


def build_k3():
    nc = bass.Bass("TRN2", target_bir_lowering=False)
    dt_in = lambda name, shape, dt=F32: nc.dram_tensor(name, shape, dt, kind="ExternalInput").ap()
    dt_out = lambda name, shape, dt=F32: nc.dram_tensor(name, shape, dt, kind="ExternalOutput").ap()
    m1_d = dt_in("m1", [16, 128, NT])
    m2_d = dt_in("m2", [8, 128, NT])
    zs_d = dt_in("zs", [16, 128, NT])
    nw_d = dt_in("nw", [128, 1])
    gt_d = dt_in("gt", [128, 2, D])
    x_d = dt_in("x", [NT, D])
    w_d = dt_in("w", [4, 128, 16, 512])
    xo_d = dt_out("xo", [NT, D])
    with contextlib.ExitStack() as es:
        P = Prog(nc, es)
        sb = lambda name, shape, dt=F32: es.enter_context(nc.sbuf_tensor("s_" + name, shape, dt))
        ps = lambda name, shape, dt=F32: es.enter_context(nc.psum_tensor("p_" + name, shape, dt))
        yT = sb("yT", [128, 16, NT], BF16); t_y = [Tok() for _ in range(16)]
        a_sb = [sb("a%d" % i, [128, NT]) for i in range(2)]; t_a = [Tok(), Tok()]
        b_sb = [sb("b%d" % i, [128, NT]) for i in range(2)]; t_b = [Tok(), Tok()]
        z_sb = [sb("z%d" % i, [128, NT]) for i in range(2)]; t_z = [Tok(), Tok()]
        sq = sb("sq", [128, NT]); t_sq = Tok()
        rs = sb("rs", [128, NT]); t_rs = Tok()
        nw = sb("nw", [128, 1]); t_nw = Tok()
        gt = sb("gt", [128, 2, D]); t_gt = Tok()
        ones = sb("ones", [128, 128]); t_ones = Tok()
        wst = [sb("wst%d" % i, [128, 16, 512]) for i in range(2)]; t_wst = [Tok(), Tok()]
        wbf = [sb("wbf%d" % i, [128, 16, 512], BF16) for i in range(2)]; t_wbf = [Tok(), Tok()]
        xt = [sb("xt%d" % i, [128, 512]) for i in range(2)]; t_xt = [Tok(), Tok()]
        xo = [sb("xo%d" % i, [128, 512]) for i in range(2)]; t_xo = [Tok(), Tok()]
        pn = ps("pn", [128, 512]); t_pn = Tok()
        po = [ps("po%d" % i, [128, 512]) for i in range(2)]; t_po = [Tok(), Tok()]
        P.dma("sp", nw[:], nw_d, writes=[t_nw])
        P.dma("sp", gt[:], gt_d, writes=[t_gt])
        P.op("pool", lambda e: e.memset(ones[:], 1.0), writes=[t_ones])
        P.op("dve", lambda e: e.tensor_scalar(out=nw[:], in0=nw[:], scalar1=128.0 ** 0.5, scalar2=None, op0=ALU.mult), reads=[t_nw], writes=[t_nw])

        def do_chunk(c):
            b = c % 2
            P.dma("sp", a_sb[b][:], m1_d[c], writes=[t_a[b]])
            P.dma("act", z_sb[b][:], zs_d[c], writes=[t_z[b]])
            if 4 <= c < 12:
                P.dma("act", b_sb[b][:], m2_d[c - 4], writes=[t_b[b]])
                P.op("pool", lambda e: e.tensor_tensor(out=a_sb[b][:], in0=a_sb[b][:], in1=b_sb[b][:], op=ALU.add),
                     reads=[t_a[b], t_b[b]], writes=[t_a[b]])
            if 8 <= c < 12:
                P.op("pool", lambda e: e.tensor_tensor(out=sq[:], in0=a_sb[b][:], in1=a_sb[b][:], op=ALU.mult), reads=[t_a[b]], writes=[t_sq])
                for (a0, a1) in TBLK:
                    P.op("pe", lambda e, a0=a0, a1=a1: e.matmul(pn[:, 0:a1 - a0], lhsT=ones[:], rhs=sq[:, a0:a1], start=True, stop=True),
                         reads=[t_ones, t_sq], writes=[t_pn])
                    P.op("act", lambda e, a0=a0, a1=a1: e.activation(out=rs[:, a0:a1], in_=pn[:, 0:a1 - a0], func=AF.Sqrt, bias=128 * EPS),
                         reads=[t_pn], writes=[t_rs])
                P.op("dve", lambda e: e.reciprocal(out=rs[:], in_=rs[:]), reads=[t_rs], writes=[t_rs])
                P.op("dve", lambda e: e.scalar_tensor_tensor(out=a_sb[b][:], in0=a_sb[b][:], scalar=nw[:, 0:1], in1=rs[:], op0=ALU.mult, op1=ALU.mult),
                     reads=[t_a[b], t_nw, t_rs], writes=[t_a[b]])
            P.op("dve", lambda e: e.tensor_tensor(out=yT[:, c, :], in0=a_sb[b][:], in1=z_sb[b][:], op=ALU.mult),
                 reads=[t_a[b], t_z[b]], writes=[t_y[c]])
        for c in range(16):
            do_chunk(c)
        outs = []
        cnt = 0
        for cb in range(4):
            wbi = cb % 2
            P.dma("act", wst[wbi][:], w_d[cb], writes=[t_wst[wbi]])
            P.op("pool", lambda e, wbi=wbi: e.tensor_copy(out=wbf[wbi][:], in_=wst[wbi][:]), reads=[t_wst[wbi]], writes=[t_wbf[wbi]])

            def do_tile(ti, cb, wbi, i):
                r0 = ti * 128
                np_ = 128 if ti < 8 else CTK
                m = 0 if ti < 8 else 1
                P.dma("act", xt[i][0:np_, :], x_d[r0:r0 + np_, cb * 512:(cb + 1) * 512], writes=[t_xt[i]])
                for c in range(16):
                    P.op("pe", lambda e, c=c: e.matmul(po[i][0:np_, :], lhsT=yT[:, c, r0:r0 + np_], rhs=wbf[wbi][:, c, :],
                                                      start=(c == 0), stop=(c == 15)), reads=[t_y[c], t_wbf[wbi]], writes=[t_po[i]])
                P.op("dve", lambda e: e.tensor_tensor(out=xo[i][0:np_, :], in0=po[i][0:np_, :], in1=gt[0:np_, m, cb * 512:(cb + 1) * 512], op=ALU.mult),
                     reads=[t_po[i], t_gt], writes=[t_xo[i]])
                P.op("pool", lambda e: e.tensor_tensor(out=xo[i][0:np_, :], in0=xo[i][0:np_, :], in1=xt[i][0:np_, :], op=ALU.add),
                     reads=[t_xo[i], t_xt[i]], writes=[t_xo[i]])
                tk = Tok(); outs.append(tk)
                P.dma("sp", xo_d[r0:r0 + np_, cb * 512:(cb + 1) * 512], xo[i][0:np_, :], reads=[t_xo[i]], writes=[tk])
            for ti in range(9):
                do_tile(ti, cb, wbi, cnt % 2)
                cnt += 1
        P.wait_all("sp", outs)
        P.emit()
    return nc


def _gather(res, name):
    ctx = np.concatenate([np.asarray(r[name])[..., TOK:] for r in res], axis=-1)
    lat = np.concatenate([np.asarray(r[name])[..., :TOK] for r in res], axis=-1)
    return np.concatenate([ctx, lat], axis=-1)


def _flipseg(a):
    return np.concatenate([a[..., :CTX][..., ::-1], a[..., CTX:][..., ::-1]], axis=-1)


def _core_slice(a, i):
    return np.concatenate([a[..., CTX + i * TOK:CTX + (i + 1) * TOK], a[..., i * CTK:(i + 1) * CTK]], axis=-1)


def _taps5(cw4, d):
    out = np.zeros((cw4.shape[1], 5), np.float32)
    if d == 0:
        out[:, 0:4] = cw4.T
    else:
        out[:, 1] = cw4[3]; out[:, 2] = cw4[2]; out[:, 3] = cw4[1]; out[:, 4] = cw4[0]
    return out


_PROGS = {}


def _prog(name, builder):
    if name not in _PROGS:
        _PROGS[name] = builder()
    return _PROGS[name]


def _forward(inputs, depth=DEPTH):
    f32 = lambda a: np.ascontiguousarray(np.asarray(a, np.float32))
    x_full = f32(inputs["x"][0]).copy()
    ctx_full = f32(inputs["ctx"][0]).copy()
    ada = run_k0(inputs)
    for l in range(depth):
        im1 = prep_k1(inputs, l, x_full, ctx_full, ada)
        r1 = _run(_prog("k1", build_k1), im1, "k1")
        ka = _gather(r1, "ka"); kdn = _gather(r1, "kdn"); kdr = _gather(r1, "kdr")
        va = _gather(r1, "va"); vd = _gather(r1, "vd")
        km = np.ascontiguousarray(np.concatenate([ka, kdn], axis=0))
        vall = np.concatenate([va, vd], axis=0)
        vaug = np.zeros((6, 128, NKT, VW), NPBF)
        vaug[:, :, :, 0:128] = vall.transpose(0, 2, 1).reshape(6, NKT, 128, 128).transpose(0, 2, 1, 3)
        vaug[:, :, :, 128] = 1.0
        im2 = []
        for i in range(NCORES):
            qm = np.ascontiguousarray(np.concatenate([np.asarray(r1[i]["qa"]), np.asarray(r1[i]["qdn"])], axis=0).transpose(1, 0, 2))
            qr = np.ascontiguousarray(np.asarray(r1[i]["qdr"]).transpose(1, 0, 2))
            im2.append({"qm": qm, "qr": qr, "km": km, "kr": np.ascontiguousarray(kdr), "va": vaug})
        r2a = _run(_prog("k2a", build_k2a), im2, "k2a")
        ux = _gather(r1, "ux")
        cwl = f32(inputs["lru_conv_w"][l])
        im3 = []
        for i in range(NCORES):
            d, blk = i // 4, i % 4
            sl = slice(blk * 128, (blk + 1) * 128)
            u = ux[blk] if d == 0 else _flipseg(ux[blk])
            prm = np.zeros((128, 16), np.float32)
            prm[:, 0:5] = _taps5(cwl[:, sl], d)
            prm[:, 5] = inputs["lru_conv_b"][l][sl]
            prm[:, 6] = inputs["lru_b_a"][l][d][sl]
            prm[:, 7] = inputs["lru_b_x"][l][d][sl]
            prm[:, 8] = inputs["lru_lambda"][l][d][sl]
            w = np.ascontiguousarray(np.stack([f32(inputs["lru_w_a"][l][d][blk]), f32(inputs["lru_w_x"][l][d][blk])], axis=1))
            im3.append({"ux": f32(u), "prm": prm, "w": w})
        r2b = _run(_prog("k2b", build_k2b), im3, "k2b")
        hdir = [[None] * 4, [None] * 4]
        for i in range(NCORES):
            d, blk = i // 4, i % 4
            h = np.asarray(r2b[i]["h"])
            hdir[d][blk] = h if d == 0 else _flipseg(h)
        uqkv = _gather(r1, "uqkv")
        uab = _gather(r1, "uab")
        cwd = f32(inputs["dn_conv_w"][l])
        im4 = []
        for i in range(NCORES):
            d, hd = i // 4, i % 4
            u = np.stack([uqkv[hd], uqkv[4 + hd], uqkv[8 + hd]], axis=0)
            br, ar = uab[d * 8 + hd], uab[d * 8 + 4 + hd]
            if d == 1:
                u = _flipseg(u); br = _flipseg(br); ar = _flipseg(ar)
            cw = np.zeros((128, 16), np.float32)
            for j in range(3):
                c = j * 4 + hd
                cw[:, 5 * j:5 * j + 5] = _taps5(cwd[:, c * 128:(c + 1) * 128], d)
            ab = np.ascontiguousarray(np.stack([br.reshape(NCH, 64).T, ar.reshape(NCH, 64).T], axis=1))
            sc = np.zeros((64, 2), np.float32)
            sc[:, 0] = inputs["dn_a_log"][l][d][hd]
            sc[:, 1] = inputs["dn_dt_bias"][l][d][hd]
            im4.append({"u": f32(u), "cw": cw, "ab": f32(ab), "sc": sc})
        rc1 = _run(_prog("c1", build_c1), im4, "c1")
        dnc = _dn_consts()
        im5 = []
        for i in range(NCORES):
            qkv = np.asarray(rc1[i]["qkv"]); bg = np.asarray(rc1[i]["bg"])
            tm = lambda a: np.ascontiguousarray(a.T.reshape(NCH, 64, 128).transpose(1, 0, 2))
            brow = bg[:, 0, :].T.reshape(-1)
            im5.append({"qT": f32(qkv[0]), "kT": f32(qkv[1]), "ktm": tm(qkv[1]), "vtm": tm(qkv[2]), "bg": f32(bg),
                        "bbc": np.ascontiguousarray(np.broadcast_to(brow[None, :], (64, NALL))), "cst": dnc,
                        "ident": np.eye(128, dtype=np.float32)})
        rc2 = _run(_prog("c2", build_c2), im5, "c2")
        odir = [[None] * 4, [None] * 4]
        for i in range(NCORES):
            d, hd = i // 4, i % 4
            o = np.asarray(rc2[i]["o"]).transpose(1, 0, 2).reshape(NALL, 128).T
            odir[d][hd] = o if d == 0 else _flipseg(o)
        w_out = f32(inputs["w_out"][l])
        wl = np.ascontiguousarray(w_out.reshape(16, 128, 4, 512).transpose(2, 1, 0, 3))
        gt = np.ascontiguousarray(np.broadcast_to(np.stack([ada[l, 0, 4096:], ada[l, 1, 4096:]], axis=0)[None], (128, 2, D)))
        nw = f32(inputs["dn_norm_w"][l]).reshape(128, 1)
        im6 = []
        for i in range(NCORES):
            o2 = np.asarray(r2a[i]["o"])
            m1 = np.empty((16, 128, NT), np.float32)
            m2 = np.empty((8, 128, NT), np.float32)
            for j in range(4):
                m1[j] = o2[j].T
                m1[12 + j] = o2[4 + j].T
                m1[4 + j] = _core_slice(hdir[0][j], i)
                m2[j] = _core_slice(hdir[1][j], i)
                m1[8 + j] = _core_slice(odir[0][j], i)
                m2[4 + j] = _core_slice(odir[1][j], i)
            im6.append({"m1": m1, "m2": m2, "zs": f32(r1[i]["zs"]), "nw": nw, "gt": gt, "x": im1[i]["x"], "w": wl})
        r3 = _run(_prog("k3", build_k3), im6, "k3")
        for i in range(NCORES):
            xo = np.asarray(r3[i]["xo"])
            x_full[i * TOK:(i + 1) * TOK] = xo[:TOK]
            ctx_full[i * CTK:(i + 1) * CTK] = xo[TOK:]
    return x_full, ctx_full


def kernel(**inputs):
    x_full, _ = _forward(inputs, DEPTH)
    return np.ascontiguousarray(x_full[None].astype(np.float32))
```

```python
import contextlib
import numpy as np
import ml_dtypes
import concourse.bass as bass
import concourse.mybir as mybir
from concourse.bass_utils import run_bass_kernel_spmd

F32 = mybir.dt.float32
BF16 = mybir.dt.bfloat16
AF = mybir.ActivationFunctionType
ALU = mybir.AluOpType
AX = mybir.AxisListType
NPBF = ml_dtypes.bfloat16

NCORES = 8
D = 2048
SEQ = 8192
CTX = 256
DEPTH = 4
TOK = SEQ // NCORES
CTK = CTX // NCORES
NT = TOK + CTK
NALL = SEQ + CTX
EPS = 1e-6
IN_W = 5840


class Tok:
    __slots__ = ("lw", "rd", "name")

    def __init__(self, name=""):
        self.lw = None
        self.rd = {}
        self.name = name


class Prog:
    ENG = ("pe", "act", "dve", "pool", "sp")

    def __init__(self, nc, es, n_dma_sems=8, same_engine_sync=True):
        self.nc = nc
        self.es = es
        self.eng = {"pe": nc.tensor, "act": nc.scalar, "dve": nc.vector,
                    "pool": nc.gpsimd, "sp": nc.sync}
        self.streams = {e: [] for e in self.ENG}
        self.sems = {}
        self.count = {}
        self.waited = {e: {} for e in self.ENG}
        for e in self.ENG:
            self.sems[e] = es.enter_context(nc.semaphore("sem_" + e))
            self.count[e] = 0
        self.dma_pool = {}
        self.dma_rr = {}
        for q in ("sp", "pool", "act"):
            lst = []
            for i in range(n_dma_sems):
                k = "dma_%s_%d" % (q, i)
                self.sems[k] = es.enter_context(nc.semaphore(k))
                self.count[k] = 0
                lst.append(k)
            self.dma_pool[q] = lst
            self.dma_rr[q] = 0
        self.same_engine_sync = same_engine_sync
        self.n_ops = 0

    def _deps(self, eng, reads, writes):
        deps = {}

        def add(k, v):
            if deps.get(k, 0) < v:
                deps[k] = v
        for t in reads:
            if t.lw is not None:
                add(*t.lw)
        for t in writes:
            if t.lw is not None:
                add(*t.lw)
            for k, v in t.rd.items():
                add(k, v)
        return deps

    def _emit_waits(self, eng, deps):
        for k, v in deps.items():
            if k == eng:
                if eng == "pe" or not self.same_engine_sync:
                    continue
            if self.waited[eng].get(k, 0) >= v:
                continue
            self.waited[eng][k] = v
            self.streams[eng].append(("wait", k, v))

    def _mark(self, key, val, reads, writes):
        for t in writes:
            t.lw = (key, val)
            t.rd = {}
        for t in reads:
            if t.rd.get(key, 0) < val:
                t.rd[key] = val

    def op(self, eng, fn, reads=(), writes=()):
        deps = self._deps(eng, reads, writes)
        self._emit_waits(eng, deps)
        self.count[eng] += 1
        self.streams[eng].append(("op", fn, eng, 1))
        self._mark(eng, self.count[eng], reads, writes)
        self.n_ops += 1

    def dma(self, q, out, in_, reads=(), writes=()):
        deps = self._deps(q, reads, writes)
        pool = self.dma_pool[q]
        k = pool[self.dma_rr[q] % len(pool)]
        self.dma_rr[q] += 1
        if self.count[k] > 0:
            deps[k] = max(deps.get(k, 0), self.count[k])
        self._emit_waits(q, deps)
        self.count[k] += 16
        self.streams[q].append(("op", lambda e: e.dma_start(out=out, in_=in_), k, 16))
        self._mark(k, self.count[k], reads, writes)
        self.n_ops += 1

    def wait_all(self, eng, toks):
        deps = {}
        for t in toks:
            if t.lw is not None and deps.get(t.lw[0], 0) < t.lw[1]:
                deps[t.lw[0]] = t.lw[1]
        for k, v in deps.items():
            self.streams[eng].append(("wait", k, v))

    def emit(self):
        nc = self.nc
        with nc.Block() as block:
            def run(ename):
                def body(e):
                    for item in self.streams[ename]:
                        if item[0] == "wait":
                            e.wait_ge(self.sems[item[1]], item[2])
                        else:
                            _, fn, k, inc = item
                            fn(e).then_inc(self.sems[k], inc)
                return body
            block.tensor(run("pe"))
            block.scalar(run("act"))
            block.vector(run("dve"))
            block.gpsimd(run("pool"))
            block.sync(run("sp"))


def _run(nc, in_maps, tag=""):
    import sys, time
    t0 = time.time()
    res = run_bass_kernel_spmd(nc, in_maps, core_ids=list(range(NCORES)))
    print("[launch %s] %.1fs" % (tag, time.time() - t0), file=sys.stderr, flush=True)
    return res.results


ADA_N = 3 * D
ADA_PC = ADA_N // NCORES


def build_k0():
    nc = bass.Bass("TRN2", target_bir_lowering=False)
    cT = nc.dram_tensor("cT", [128, 16, 2], F32, kind="ExternalInput").ap()
    w = nc.dram_tensor("w", [DEPTH, D, ADA_PC], F32, kind="ExternalInput").ap()
    b = nc.dram_tensor("b", [DEPTH, 2, ADA_PC], F32, kind="ExternalInput").ap()
    o = nc.dram_tensor("o", [DEPTH, 2, ADA_PC], F32, kind="ExternalOutput").ap()
    with contextlib.ExitStack() as es:
        P = Prog(nc, es)
        sb = lambda name, shape, dt: es.enter_context(nc.sbuf_tensor(name, shape, dt))
        ps = lambda name, shape, dt: es.enter_context(nc.psum_tensor(name, shape, dt))
        c_sb = sb("c_sb", [128, 16, 2], F32)
        s_sb = sb("s_sb", [128, 16, 2], F32)
        b_sb = sb("b_sb", [2, DEPTH, ADA_PC], F32)
        o_sb = sb("o_sb", [2, DEPTH, ADA_PC], F32)
        NB = 4
        w_sb = [sb("w_sb%d" % i, [128, 4, ADA_PC], F32) for i in range(NB)]
        acc = [ps("acc%d" % i, [2, 512], F32) for i in range(4)]
        t_c, t_s, t_b, t_o = Tok(), Tok(), Tok(), Tok()
        t_w = [Tok() for _ in range(NB)]
        t_acc = [Tok() for _ in range(4)]
        P.dma("sp", c_sb[:], cT, writes=[t_c])
        P.dma("sp", b_sb[:], b.rearrange("l m n -> m l n"), writes=[t_b])
        P.op("act", lambda e: e.activation(out=s_sb[:], in_=c_sb[:], func=AF.Silu), reads=[t_c], writes=[t_s])
        wi = 0
        for l in range(DEPTH):
            wv = w[l].rearrange("(c p) n -> p c n", p=128)
            for g in range(4):
                bi = wi % NB
                wi += 1
                q = "sp" if (wi % 2) else "pool"
                P.dma(q, w_sb[bi][:], wv[:, g * 4:(g + 1) * 4, :], writes=[t_w[bi]])
                for kk in range(4):
                    kc = g * 4 + kk
                    for j, (n0, n1) in enumerate(((0, 512), (512, 768))):
                        a = acc[(l % 2) * 2 + j]
                        P.op("pe", lambda e, a=a, bi=bi, kk=kk, kc=kc, n0=n0, n1=n1: e.matmul(
                            a[:, 0:n1 - n0], lhsT=s_sb[:, kc, :], rhs=w_sb[bi][:, kk, n0:n1],
                            start=(kc == 0), stop=(kc == 15)),
                            reads=[t_s, t_w[bi]], writes=[t_acc[(l % 2) * 2 + j]])
            for j, (n0, n1) in enumerate(((0, 512), (512, 768))):
                a = acc[(l % 2) * 2 + j]
                P.op("dve", lambda e, a=a, l=l, n0=n0, n1=n1: e.tensor_tensor(
                    out=o_sb[:, l, n0:n1], in0=a[:, 0:n1 - n0], in1=b_sb[:, l, n0:n1], op=ALU.add),
                    reads=[t_acc[(l % 2) * 2 + j], t_b], writes=[t_o])
        P.dma("sp", o.rearrange("l m n -> m l n"), o_sb[:], reads=[t_o], writes=[t_o])
        P.wait_all("sp", [t_o])
        P.emit()
    return nc


def run_k0(inputs):
    c = np.asarray(inputs["c"], np.float32).reshape(D)
    cc = np.asarray(inputs["c_ctx"], np.float32).reshape(D)
    cm = np.stack([c, cc], axis=1)
    cT = np.ascontiguousarray(cm.reshape(16, 128, 2).transpose(1, 0, 2))
    w_ada = np.asarray(inputs["w_ada"], np.float32)
    b_ada = np.asarray(inputs["b_ada"], np.float32)
    nc = build_k0()
    in_maps = []
    for i in range(NCORES):
        sl = slice(i * ADA_PC, (i + 1) * ADA_PC)
        bb = np.ascontiguousarray(np.broadcast_to(b_ada[:, None, sl], (DEPTH, 2, ADA_PC)))
        in_maps.append({"cT": cT, "w": np.ascontiguousarray(w_ada[:, :, sl]), "b": bb})
    res = _run(nc, in_maps, "k0")
    ada = np.concatenate([r["o"] for r in res], axis=2)
    return ada


def _col_tiles():
    tiles = []
    def add(name, c0, n, step=128):
        for i in range(0, n, step):
            tiles.append((name, c0 + i, min(step, n - i)))
    add("qa", 0, 512); add("ka", 512, 256); add("va", 768, 256); add("z", 1024, 512)
    add("ux", 1536, 512); add("z", 2048, 512)
    add("uqkv", 2560, 1536); add("z", 4096, 512); add("uab", 4608, 16)
    add("cq", 4624, 384); add("ckv", 5008, 256); add("kr", 5264, 64); add("z", 5328, 512)
    return tiles
COL_TILES = _col_tiles()
NCT = len(COL_TILES)
TBLK = ((0, 512), (512, 1024), (1024, NT))


def build_k1():
    nc = bass.Bass("TRN2", target_bir_lowering=False)
    dt_in = lambda name, shape, dt=F32: nc.dram_tensor(name, shape, dt, kind="ExternalInput").ap()
    dt_out = lambda name, shape, dt=F32: nc.dram_tensor(name, shape, dt, kind="ExternalOutput").ap()
    x_d = dt_in("x", [NT, D])
    mod_d = dt_in("mod", [128, 5, 16])
    w_d = dt_in("w", [NCT, 128, 16, 128])
    gn_d = dt_in("gn", [128, 8])
    g2_d = dt_in("g2", [128, 4])
    wuq_d = dt_in("wuq", [128, 3, 768])
    wukv_d = dt_in("wukv", [128, 2, 1024])
    cs_d = dt_in("cs", [128, 2, NT])
    cs64_d = dt_in("cs64", [64, 2, NT])
    cst_d = dt_in("cst", [128, 3, 128], BF16)
    qa_o = dt_out("qa", [4, 128, NT], BF16)
    ka_o = dt_out("ka", [2, 128, NT], BF16)
    va_o = dt_out("va", [2, 128, NT], BF16)
    zs_o = dt_out("zs", [16, 128, NT])
    ux_o = dt_out("ux", [4, 128, NT])
    uqkv_o = dt_out("uqkv", [12, 128, NT])
    uab_o = dt_out("uab", [16, NT])
    qdn_o = dt_out("qdn", [4, 128, NT], BF16)
    qdr_o = dt_out("qdr", [4, 64, NT], BF16)
    kdn_o = dt_out("kdn", [4, 128, NT], BF16)
    kdr_o = dt_out("kdr", [4, 64, NT], BF16)
    vd_o = dt_out("vd", [4, 128, NT], BF16)
    out_toks = []
    with contextlib.ExitStack() as es:
        P = Prog(nc, es)
        sb = lambda name, shape, dt=F32: es.enter_context(nc.sbuf_tensor("s_" + name, shape, dt))
        ps = lambda name, shape, dt=F32: es.enter_context(nc.psum_tensor("p_" + name, shape, dt))
        mod = sb("mod", [128, 5, 16]); t_mod = Tok()
        gain = sb("gain", [128, 2, 16]); t_gain = Tok()
        gn = sb("gn", [128, 8]); g2 = sb("g2", [128, 4]); t_gn = Tok()
        gs = sb("gs", [128, 12]); t_gs = Tok()
        wuq_f = sb("wuq_f", [128, 3, 768]); wuq = sb("wuq", [128, 3, 768], BF16); t_wuq = Tok()
        wukv_f = sb("wukv_f", [128, 2, 1024]); wukv = sb("wukv", [128, 2, 1024], BF16); t_wukv = Tok()
        cs = sb("cs", [128, 2, NT]); cs64 = sb("cs64", [64, 2, NT]); t_cs = Tok()
        cst = sb("cst", [128, 3, 128], BF16); t_cst = Tok()
        ones = sb("ones", [128, 128], BF16); t_ones = Tok()
        hT = sb("hT", [128, 16, NT], BF16); t_hT = [Tok() for _ in range(9)]
        P.dma("sp", mod[:], mod_d, writes=[t_mod])
        P.dma("sp", gn[:], gn_d, writes=[t_gn])
        P.dma("sp", g2[:], g2_d, writes=[t_gn])
        P.dma("sp", cst[:], cst_d, writes=[t_cst])
        P.dma("sp", wuq_f[:], wuq_d, writes=[t_wuq])
        P.dma("sp", wukv_f[:], wukv_d, writes=[t_wukv])
        P.dma("sp", cs[:], cs_d, writes=[t_cs])
        P.dma("sp", cs64[:], cs64_d, writes=[t_cs])
        P.op("pool", lambda e: e.memset(ones[:], 1.0), writes=[t_ones])
        P.op("pool", lambda e: e.tensor_copy(out=wuq[:], in_=wuq_f[:]), reads=[t_wuq], writes=[t_wuq])
        P.op("pool", lambda e: e.tensor_copy(out=wukv[:], in_=wukv_f[:]), reads=[t_wukv], writes=[t_wukv])
        for m in range(2):
            P.op("dve", lambda e, m=m: e.scalar_tensor_tensor(
                out=gain[:, m, :], in0=mod[:, 1 + 2 * m, :], scalar=1.0, in1=mod[:, 0, :],
                op0=ALU.add, op1=ALU.mult), reads=[t_mod], writes=[t_gain])
        for (o0, o1, src, s0, fac) in ((0, 2, gn, 0, 128.0 ** 0.5), (2, 5, gn, 2, 384.0 ** 0.5),
                                        (5, 7, gn, 5, 256.0 ** 0.5), (7, 11, g2, 0, 192.0 ** 0.5)):
            P.op("dve", lambda e, o0=o0, o1=o1, src=src, s0=s0, fac=fac: e.tensor_scalar(
                out=gs[:, o0:o1], in0=src[:, s0:s0 + (o1 - o0)], scalar1=fac, scalar2=None, op0=ALU.mult),
                reads=[t_gn], writes=[t_gs])

        xt = [sb("xt%d" % i, [128, D]) for i in range(2)]; t_xt = [Tok(), Tok()]
        xn = [sb("xn%d" % i, [128, D], BF16) for i in range(2)]; t_xn = [Tok(), Tok()]
        ssq = sb("ssq", [128, 16]); t_ssq = Tok()
        tp_all = ps("tp", [128, 2, 4, 128], BF16); tp = [tp_all[:, 0], tp_all[:, 1]]; t_tp = [Tok()] * 2
        tpi = 0
        for ti in range(9):
            r0 = ti * 128
            np_ = 128 if ti < 8 else CTK
            m = 0 if ti < 8 else 1
            b = ti % 2
            P.dma("sp", xt[b][0:np_, :], x_d[r0:r0 + np_, :], writes=[t_xt[b]])
            P.op("act", lambda e, b=b, np_=np_, ti=ti: e.activation(
                out=xn[b][0:np_, :], in_=xt[b][0:np_, :], func=AF.Square, accum_out=ssq[0:np_, ti:ti + 1]),
                reads=[t_xt[b]], writes=[t_xn[b], t_ssq])
            P.op("dve", lambda e, np_=np_, ti=ti: e.tensor_scalar(
                out=ssq[0:np_, ti:ti + 1], in0=ssq[0:np_, ti:ti + 1], scalar1=1.0 / D, scalar2=EPS,
                op0=ALU.mult, op1=ALU.add), reads=[t_ssq], writes=[t_ssq])
            P.op("act", lambda e, np_=np_, ti=ti: e.activation(
                out=ssq[0:np_, ti:ti + 1], in_=ssq[0:np_, ti:ti + 1], func=AF.Sqrt), reads=[t_ssq], writes=[t_ssq])
            P.op("dve", lambda e, np_=np_, ti=ti: e.reciprocal(
                out=ssq[0:np_, ti:ti + 1], in_=ssq[0:np_, ti:ti + 1]), reads=[t_ssq], writes=[t_ssq])
            P.op("dve", lambda e, b=b, np_=np_, ti=ti: e.tensor_scalar(
                out=xn[b][0:np_, :], in0=xt[b][0:np_, :], scalar1=ssq[0:np_, ti:ti + 1], scalar2=None,
                op0=ALU.mult), reads=[t_xt[b], t_ssq], writes=[t_xn[b]])
            for cg in range(4):
                pb = tpi % 2; tpi += 1
                for j in range(4):
                    c = cg * 4 + j
                    P.op("pe", lambda e, pb=pb, j=j, c=c, b=b, np_=np_: e.transpose(
                        tp[pb][:, j, 0:np_], xn[b][0:np_, c * 128:(c + 1) * 128], cst[0:np_, 0, 0:np_]),
                        reads=[t_xn[b], t_cst], writes=[t_tp[pb]])
                for j in range(4):
                    c = cg * 4 + j
                    eng = "act" if j % 2 == 0 else "dve"
                    if eng == "act":
                        P.op("act", lambda e, pb=pb, j=j, c=c, r0=r0, np_=np_, m=m: e.activation(
                            out=hT[:, c, r0:r0 + np_], in_=tp[pb][:, j, 0:np_], func=AF.Identity,
                            scale=gain[:, m, c:c + 1], bias=mod[:, 2 + 2 * m, c:c + 1]),
                            reads=[t_tp[pb], t_gain, t_mod], writes=[t_hT[ti]])
                    else:
                        P.op("dve", lambda e, pb=pb, j=j, c=c, r0=r0, np_=np_, m=m: e.tensor_scalar(
                            out=hT[:, c, r0:r0 + np_], in0=tp[pb][:, j, 0:np_], scalar1=gain[:, m, c:c + 1],
                            scalar2=mod[:, 2 + 2 * m, c:c + 1], op0=ALU.mult, op1=ALU.add),
                            reads=[t_tp[pb], t_gain, t_mod], writes=[t_hT[ti]])

        NWB = 3
        wst = [sb("wst%d" % i, [128, 16, 128]) for i in range(2)]; t_wst = [Tok() for _ in range(2)]
        wbf = [sb("wbf%d" % i, [128, 16, 128], BF16) for i in range(NWB)]; t_wbf = [Tok() for _ in range(NWB)]
        pj = [ps("pj%d" % i, [128, 3, 512]) for i in range(2)]; t_pj = [Tok(), Tok()]
        aux = ps("aux", [128, 1, 512]); t_aux = Tok()
        raw = [sb("raw%d" % i, [128, NT]) for i in range(5)]; t_raw = [Tok() for _ in range(5)]
        sqb = [sb("sqb%d" % i, [128, NT], BF16) for i in range(3)]; t_sqb = [Tok() for _ in range(3)]
        rstd = sb("rstd", [128, NT]); t_rstd = Tok()
        nbf = [sb("nbf%d" % i, [128, NT], BF16) for i in range(4)]; t_nbf = [Tok() for _ in range(4)]
        t1 = sb("t1", [128, NT]); t_t1 = Tok()
        t2 = sb("t2", [128, NT]); t_t2 = Tok()
        ob = [sb("ob%d" % i, [128, NT], BF16) for i in range(2)]; t_ob = [Tok(), Tok()]
        of = [sb("of%d" % i, [128, NT]) for i in range(2)]; t_of = [Tok(), Tok()]
        kr_raw = sb("kr_raw", [64, NT]); t_kr = Tok()
        kr_sq = sb("kr_sq", [64, NT], BF16); t_krsq = Tok()
        cnt = {"ob": 0, "of": 0}

        def store_bf(dst_ap, rows, producer):
            i = cnt["ob"] % 2; cnt["ob"] += 1
            producer(ob[i], t_ob[i])
            tk = Tok(); out_toks.append(tk)
            P.dma("sp", dst_ap, ob[i][0:rows, :], reads=[t_ob[i]], writes=[tk])

        def store_f32(dst_ap, rows, producer):
            i = cnt["of"] % 2; cnt["of"] += 1
            producer(of[i], t_of[i])
            tk = Tok(); out_toks.append(tk)
            P.dma("sp", dst_ap, of[i][0:rows, :], reads=[t_of[i]], writes=[tk])

        def ss_to_rstd(pieces, n_eps):
            for bi, (a0, a1) in enumerate(TBLK):
                for pi, (sqt, tk, rows) in enumerate(pieces):
                    P.op("pe", lambda e, sqt=sqt, rows=rows, a0=a0, a1=a1, bi=bi, pi=pi: e.matmul(
                        aux[:, 0, 0:a1 - a0], lhsT=ones[0:rows, :], rhs=sqt[0:rows, a0:a1],
                        start=(pi == 0), stop=(pi == len(pieces) - 1)),
                        reads=[tk, t_ones], writes=[t_aux])
                P.op("dve", lambda e, a0=a0, a1=a1, bi=bi: e.tensor_scalar(
                    out=rstd[:, a0:a1], in0=aux[:, 0, 0:a1 - a0], scalar1=n_eps, scalar2=None,
                    op0=ALU.add), reads=[t_aux], writes=[t_rstd])
            P.op("act", lambda e: e.activation(out=rstd[:], in_=rstd[:], func=AF.Sqrt), reads=[t_rstd], writes=[t_rstd])
            P.op("dve", lambda e: e.reciprocal(out=rstd[:], in_=rstd[:]), reads=[t_rstd], writes=[t_rstd])

        def rope_store(dst_ap, rows, nb, tnb, cst_idx, cs_t):
            P.op("pool", lambda e: e.tensor_tensor(out=t1[0:rows, :], in0=nb[0:rows, :], in1=cs_t[0:rows, 0, :],
                                                    op=ALU.mult), reads=[tnb, t_cs], writes=[t_t1])
            for bi, (a0, a1) in enumerate(TBLK):
                P.op("pe", lambda e, a0=a0, a1=a1, bi=bi: e.matmul(
                    aux[0:rows, 0, 0:a1 - a0], lhsT=cst[0:rows, cst_idx, 0:rows], rhs=nb[0:rows, a0:a1],
                    start=True, stop=True), reads=[tnb, t_cst], writes=[t_aux])
                P.op("dve", lambda e, a0=a0, a1=a1, bi=bi: e.tensor_tensor(
                    out=t2[0:rows, a0:a1], in0=aux[0:rows, 0, 0:a1 - a0], in1=cs_t[0:rows, 1, a0:a1],
                    op=ALU.mult), reads=[t_aux, t_cs], writes=[t_t2])
            store_bf(dst_ap, rows, lambda o, to: P.op("pool", lambda e: e.tensor_tensor(
                out=o[0:rows, :], in0=t1[0:rows, :], in1=t2[0:rows, :], op=ALU.add),
                reads=[t_t1, t_t2], writes=[to]))

        def load_w(ct):
            wb, ws = ct % NWB, ct % 2
            P.dma("act", wst[ws][:], w_d[ct], writes=[t_wst[ws]])
            P.op("dve", lambda e: e.tensor_copy(out=wbf[wb][:], in_=wst[ws][:]), reads=[t_wst[ws]], writes=[t_wbf[wb]])

        zi = 0
        deferred = []
        ci = {"qa": 0, "ka": 0, "va": 0, "ux": 0, "uqkv": 0, "cq": 0, "ckv": 0}
        for ct, (name, c0, ncol) in enumerate(COL_TILES):
            wb = ct % NWB
            if ct == 0:
                load_w(0)
            if ct + 1 < NCT:
                load_w(ct + 1)
            pb = ct % 2
            for bi, (a0, a1) in enumerate(TBLK):
                tks = t_hT[0:4] if bi == 0 else (t_hT[4:8] if bi == 1 else t_hT[8:9])
                for c in range(16):
                    P.op("pe", lambda e, pb=pb, bi=bi, a0=a0, a1=a1, c=c, wb=wb, ncol=ncol: e.matmul(
                        pj[pb][0:ncol, bi, 0:a1 - a0], lhsT=wbf[wb][:, c, 0:ncol], rhs=hT[:, c, a0:a1],
                        start=(c == 0), stop=(c == 15)), reads=[t_wbf[wb]] + tks, writes=[t_pj[pb]])

            if deferred:
                deferred.pop()()

            def evac(dst, tdst, rows, func=AF.Identity, pb=pb):
                for bi, (a0, a1) in enumerate(TBLK):
                    P.op("act", lambda e, bi=bi, a0=a0, a1=a1: e.activation(
                        out=dst[0:rows, a0:a1], in_=pj[pb][0:rows, bi, 0:a1 - a0], func=func),
                        reads=[t_pj[pb]], writes=[tdst])

            if name in ("qa", "ka"):
                i = ci[name]; ci[name] += 1
                evac(raw[0], t_raw[0], 128)
                evac(sqb[0], t_sqb[0], 128, AF.Square)

                def part_b(i=i, name=name):
                    ss_to_rstd([(sqb[0], t_sqb[0], 128)], 128 * EPS)
                    gcol = 0 if name == "qa" else 1
                    P.op("dve", lambda e: e.scalar_tensor_tensor(
                        out=nbf[0][:], in0=raw[0][:], scalar=gs[:, gcol:gcol + 1], in1=rstd[:],
                        op0=ALU.mult, op1=ALU.mult), reads=[t_raw[0], t_gs, t_rstd], writes=[t_nbf[0]])
                    rope_store((qa_o if name == "qa" else ka_o)[i], 128, nbf[0], t_nbf[0], 1, cs)
                deferred.append(part_b)
            elif name == "va":
                i = ci[name]; ci[name] += 1
                store_bf(va_o[i], 128, lambda o, to: evac(o, to, 128))
            elif name == "z":
                store_f32(zs_o[zi], 128, lambda o, to: evac(o, to, 128, AF.Silu)); zi += 1
            elif name == "ux":
                i = ci[name]; ci[name] += 1
                store_f32(ux_o[i], 128, lambda o, to: evac(o, to, 128))
            elif name == "uqkv":
                i = ci[name]; ci[name] += 1
                store_f32(uqkv_o[i], 128, lambda o, to: evac(o, to, 128))
            elif name == "uab":
                store_f32(uab_o, 16, lambda o, to: evac(o, to, 16))
            elif name in ("cq", "ckv"):
                i = ci[name]; ci[name] += 1
                nch = 3 if name == "cq" else 2
                evac(raw[i], t_raw[i], 128)
                evac(sqb[i], t_sqb[i], 128, AF.Square)
                if i == nch - 1:
                    ss_to_rstd([(sqb[k], t_sqb[k], 128) for k in range(nch)], (384 if name == "cq" else 256) * EPS)
                    g0 = 2 if name == "cq" else 5
                    for k in range(nch):
                        P.op("dve", lambda e, k=k, g0=g0: e.scalar_tensor_tensor(
                            out=nbf[k][:], in0=raw[k][:], scalar=gs[:, g0 + k:g0 + k + 1], in1=rstd[:],
                            op0=ALU.mult, op1=ALU.mult), reads=[t_raw[k], t_gs, t_rstd], writes=[t_nbf[k]])
                    if name == "cq":
                        for h in range(4):
                            for (pbi, col0, rows) in ((0, h * 192, 128), (1, h * 192 + 128, 64)):
                                for bi, (a0, a1) in enumerate(TBLK):
                                    for k in range(3):
                                        P.op("pe", lambda e, pbi=pbi, col0=col0, rows=rows, bi=bi, a0=a0, a1=a1, k=k: e.matmul(
                                            pj[pbi][0:rows, bi, 0:a1 - a0], lhsT=wuq[:, k, col0:col0 + rows],
                                            rhs=nbf[k][:, a0:a1], start=(k == 0), stop=(k == 2)),
                                            reads=[t_wuq, t_nbf[k]], writes=[t_pj[pbi]])
                            evac(raw[3], t_raw[3], 128, pb=0)
                            evac(sqb[0], t_sqb[0], 128, AF.Square, pb=0)
                            evac(raw[4], t_raw[4], 64, pb=1)
                            evac(sqb[1], t_sqb[1], 64, AF.Square, pb=1)
                            ss_to_rstd([(sqb[0], t_sqb[0], 128), (sqb[1], t_sqb[1], 64)], 192 * EPS)
                            store_bf(qdn_o[h], 128, lambda o, to: P.op("dve", lambda e: e.scalar_tensor_tensor(
                                out=o[:], in0=raw[3][:], scalar=gs[:, 7:8], in1=rstd[:], op0=ALU.mult, op1=ALU.mult),
                                reads=[t_raw[3], t_gs, t_rstd], writes=[to]))
                            P.op("dve", lambda e: e.scalar_tensor_tensor(
                                out=nbf[3][0:64, :], in0=raw[4][0:64, :], scalar=gs[0:64, 8:9], in1=rstd[0:64, :],
                                op0=ALU.mult, op1=ALU.mult), reads=[t_raw[4], t_gs, t_rstd], writes=[t_nbf[3]])
                            rope_store(qdr_o[h], 64, nbf[3], t_nbf[3], 2, cs64)
            elif name == "kr":
                evac(kr_raw, t_kr, 64)
                evac(kr_sq, t_krsq, 64, AF.Square)
                for h in range(4):
                    for (pbi, col0) in ((0, h * 256), (1, h * 256 + 128)):
                        for bi, (a0, a1) in enumerate(TBLK):
                            for k in range(2):
                                P.op("pe", lambda e, pbi=pbi, col0=col0, bi=bi, a0=a0, a1=a1, k=k: e.matmul(
                                    pj[pbi][:, bi, 0:a1 - a0], lhsT=wukv[:, k, col0:col0 + 128],
                                    rhs=nbf[k][:, a0:a1], start=(k == 0), stop=(k == 1)),
                                    reads=[t_wukv, t_nbf[k]], writes=[t_pj[pbi]])
                    evac(raw[3], t_raw[3], 128, pb=0)
                    evac(sqb[0], t_sqb[0], 128, AF.Square, pb=0)
                    store_bf(vd_o[h], 128, lambda o, to: evac(o, to, 128, pb=1))
                    ss_to_rstd([(sqb[0], t_sqb[0], 128), (kr_sq, t_krsq, 64)], 192 * EPS)
                    store_bf(kdn_o[h], 128, lambda o, to: P.op("dve", lambda e: e.scalar_tensor_tensor(
                        out=o[:], in0=raw[3][:], scalar=gs[:, 9:10], in1=rstd[:], op0=ALU.mult, op1=ALU.mult),
                        reads=[t_raw[3], t_gs, t_rstd], writes=[to]))
                    P.op("dve", lambda e: e.scalar_tensor_tensor(
                        out=nbf[2][0:64, :], in0=kr_raw[0:64, :], scalar=gs[0:64, 10:11], in1=rstd[0:64, :],
                        op0=ALU.mult, op1=ALU.mult), reads=[t_kr, t_gs, t_rstd], writes=[t_nbf[2]])
                    rope_store(kdr_o[h], 64, nbf[2], t_nbf[2], 2, cs64)
        P.wait_all("sp", out_toks)
        P.emit()
    return nc


def _fm(v):
    v = np.asarray(v, np.float32)
    return np.ascontiguousarray(v.reshape(-1, 128).T)


def _rope_tables():
    t = np.arange(SEQ)
    rows = (t // 64).astype(np.float64)
    cols = (t % 64).astype(np.float64)

    def tab(half, reps):
        inv = 10000.0 ** (-np.arange(half, dtype=np.float64) / half)
        ar = rows[None, :] * inv[:, None]
        ac = cols[None, :] * inv[:, None]
        ang = np.concatenate([ar, ar, ac, ac], axis=0)
        return np.cos(ang), np.sin(ang)
    c128, s128 = tab(32, 4)
    c64, s64 = tab(16, 4)
    return (c128.astype(np.float32), s128.astype(np.float32), c64.astype(np.float32), s64.astype(np.float32))


def _rot_mats():
    def rot(n):
        q = n // 4
        R = np.zeros((n, n), np.float32)
        for base in (0, 2 * q):
            for i in range(q):
                R[base + i, base + q + i] = -1.0
                R[base + q + i, base + i] = 1.0
        return R
    cst = np.zeros((128, 3, 128), np.float32)
    cst[:, 0, :] = np.eye(128, dtype=np.float32)
    cst[:, 1, :] = rot(128).T
    cst[0:64, 2, 0:64] = rot(64).T
    return cst.astype(NPBF)


_CONST = {}


def _consts():
    if not _CONST:
        _CONST["rope"] = _rope_tables()
        _CONST["cst"] = _rot_mats()
    return _CONST


def prep_k1(inputs, l, x_full, ctx_full, ada):
    cs = _consts()
    c128, s128, c64, s64 = cs["rope"]
    w_in = np.asarray(inputs["w_in"][l], np.float32)
    wt = np.zeros((NCT, 128, 16, 128), np.float32)
    for ct, (name, c0, ncol) in enumerate(COL_TILES):
        wt[ct, :, :, 0:ncol] = w_in[:, c0:c0 + ncol].reshape(16, 128, ncol).transpose(1, 0, 2)
    mod = np.stack([_fm(inputs["norm_w"][l]), _fm(ada[l, 0, 2048:4096]), _fm(ada[l, 0, 0:2048]),
                    _fm(ada[l, 1, 2048:4096]), _fm(ada[l, 1, 0:2048])], axis=1)
    gn = np.zeros((128, 8), np.float32)
    gn[:, 0] = inputs["attn_q_norm"][l]
    gn[:, 1] = inputs["attn_k_norm"][l]
    gn[:, 2:5] = _fm(inputs["mla_q_norm"][l])
    gn[:, 5:7] = _fm(inputs["mla_kv_norm"][l])
    g2 = np.zeros((128, 4), np.float32)
    g2[:, 0] = inputs["mla_q_qk_norm"][l][0:128]
    g2[0:64, 1] = inputs["mla_q_qk_norm"][l][128:192]
    g2[:, 2] = inputs["mla_k_qk_norm"][l][0:128]
    g2[0:64, 3] = inputs["mla_k_qk_norm"][l][128:192]
    wuq = np.ascontiguousarray(np.asarray(inputs["mla_w_uq"][l], np.float32).reshape(3, 128, 768).transpose(1, 0, 2))
    wukv = np.ascontiguousarray(np.asarray(inputs["mla_w_ukv"][l], np.float32).reshape(2, 128, 1024).transpose(1, 0, 2))
    in_maps = []
    for i in range(NCORES):
        xs = np.concatenate([x_full[i * TOK:(i + 1) * TOK], ctx_full[i * CTK:(i + 1) * CTK]], axis=0)
        csa = np.zeros((128, 2, NT), np.float32)
        csa[:, 0, 0:TOK] = c128[:, i * TOK:(i + 1) * TOK]
        csa[:, 1, 0:TOK] = s128[:, i * TOK:(i + 1) * TOK]
        csa[:, 0, TOK:] = 1.0
        csb = np.zeros((64, 2, NT), np.float32)
        csb[:, 0, 0:TOK] = c64[:, i * TOK:(i + 1) * TOK]
        csb[:, 1, 0:TOK] = s64[:, i * TOK:(i + 1) * TOK]
        csb[:, 0, TOK:] = 1.0
        in_maps.append({"x": np.ascontiguousarray(xs, np.float32), "mod": mod, "w": wt, "gn": gn, "g2": g2,
                        "wuq": wuq, "wukv": wukv, "cs": csa, "cs64": csb, "cst": cs["cst"]})
    return in_maps


NKT = NALL // 128
VW = 136


def build_k2a():
    nc = bass.Bass("TRN2", target_bir_lowering=False)
    dt_in = lambda name, shape, dt=F32: nc.dram_tensor(name, shape, dt, kind="ExternalInput").ap()
    dt_out = lambda name, shape, dt=F32: nc.dram_tensor(name, shape, dt, kind="ExternalOutput").ap()
    qm_d = dt_in("qm", [128, 8, NT], BF16)
    qr_d = dt_in("qr", [64, 4, NT], BF16)
    km_d = dt_in("km", [6, 128, NALL], BF16)
    kr_d = dt_in("kr", [4, 64, NALL], BF16)
    va_d = dt_in("va", [6, 128, NKT, VW], BF16)
    o_d = dt_out("o", [8, NT, 128])
    out_toks = []
    with contextlib.ExitStack() as es:
        P = Prog(nc, es)
        sb = lambda name, shape, dt=F32: es.enter_context(nc.sbuf_tensor("s_" + name, shape, dt))
        ps = lambda name, shape, dt=F32: es.enter_context(nc.psum_tensor("p_" + name, shape, dt))
        qm = sb("qm", [128, 8, NT], BF16); qr = sb("qr", [64, 4, NT], BF16); t_q = Tok()
        km = [sb("km%d" % i, [128, NALL], BF16) for i in range(2)]; t_km = [Tok(), Tok()]
        kr = [sb("kr%d" % i, [64, NALL], BF16) for i in range(2)]; t_kr = [Tok(), Tok()]
        va = [sb("va%d" % i, [128, NKT, VW], BF16) for i in range(2)]; t_va = [Tok(), Tok()]
        NS = 3
        s_ps = [ps("s%d" % i, [128, 512]) for i in range(NS)]; t_s = [Tok() for _ in range(NS)]
        pT = [sb("pT%d" % i, [128, 512], BF16) for i in range(NS)]; t_p = [Tok() for _ in range(NS)]
        o_ps = [ps("o%d" % i, [128, 512]) for i in range(4)]; t_o = [Tok() for _ in range(4)]
        rcp = sb("rcp", [128, 4]); t_rcp = Tok()
        o_sb = [sb("osb%d" % i, [128, 4, 128]) for i in range(2)]; t_osb = [Tok(), Tok()]
        P.dma("sp", qm[:], qm_d, writes=[t_q])
        P.dma("sp", qr[:], qr_d, writes=[t_q])
        slot = -1
        kvi = 0
        nfin = 0
        def load_kv(h, sl, kvi):
            P.dma("sp", km[sl][:], km_d[kvi], writes=[t_km[sl]])
            if h >= 4:
                P.dma("sp", kr[sl][:], kr_d[h - 4], writes=[t_kr[sl]])
            P.dma("sp", va[sl][:], va_d[kvi], writes=[t_va[sl]])
        slot_of = {0: 0, 1: 0, 2: 1, 3: 1, 4: 0, 5: 1, 6: 0, 7: 1}
        issue_at = {0: [(0, 0, 0), (2, 1, 1)], 2: [(4, 0, 2)], 4: [(5, 1, 3)], 5: [(6, 0, 4)], 6: [(7, 1, 5)]}
        for h in range(8):
            mla = h >= 4
            sl = slot_of[h]
            for args in issue_at.get(h, []):
                load_kv(*args)
            scale = (192.0 if mla else 128.0) ** -0.5
            def do_tile(h, sl, mla, scale, q0, q1, nkt, ob):
                nq = q1 - q0
                nj = (nq + 127) // 128

                def S(kt):
                    i = kt % NS
                    rd = [t_q, t_km[sl]] + ([t_kr[sl]] if mla else [])
                    P.op("pe", lambda e, i=i, kt=kt: e.matmul(
                        s_ps[i][:, 0:nq], lhsT=km[sl][:, kt * 128:(kt + 1) * 128], rhs=qm[:, h, q0:q1],
                        start=True, stop=not mla), reads=rd, writes=[t_s[i]])
                    if mla:
                        P.op("pe", lambda e, i=i, kt=kt: e.matmul(
                            s_ps[i][:, 0:nq], lhsT=kr[sl][:, kt * 128:(kt + 1) * 128], rhs=qr[:, h - 4, q0:q1],
                            start=False, stop=True), reads=rd, writes=[t_s[i]])

                def E(kt):
                    i = kt % NS
                    P.op("act", lambda e, i=i: e.activation(out=pT[i][:, 0:nq], in_=s_ps[i][:, 0:nq], func=AF.Exp,
                                                            scale=scale), reads=[t_s[i]], writes=[t_p[i]])

                def PV(kt):
                    i = kt % NS
                    for j in range(nj):
                        m = min(128, nq - j * 128)
                        P.op("pe", lambda e, i=i, j=j, m=m, kt=kt: e.matmul(
                            o_ps[j][0:m, 0:129], lhsT=pT[i][:, j * 128:j * 128 + m], rhs=va[sl][:, kt, 0:129],
                            start=(kt == 0), stop=(kt == nkt - 1)), reads=[t_p[i], t_va[sl]], writes=[t_o[j]])

                for kt in range(min(NS, nkt)):
                    S(kt)
                    E(kt)
                for kt in range(nkt):
                    PV(kt)
                    if kt + NS < nkt:
                        S(kt + NS)
                        E(kt + NS)
                for j in range(nj):
                    m = min(128, nq - j * 128)
                    P.op("dve", lambda e, j=j, m=m: e.reciprocal(out=rcp[0:m, j:j + 1], in_=o_ps[j][0:m, 128:129]),
                         reads=[t_o[j]], writes=[t_rcp])
                    P.op("dve", lambda e, j=j, m=m, ob=ob: e.tensor_scalar(
                        out=o_sb[ob][0:m, j, :], in0=o_ps[j][0:m, 0:128], scalar1=rcp[0:m, j:j + 1], scalar2=None,
                        op0=ALU.mult), reads=[t_o[j], t_rcp], writes=[t_osb[ob]])
                tk = Tok(); out_toks.append(tk)
                if nq == 512:
                    P.dma("sp", o_d[h, q0:q1, :].rearrange("(j p) d -> p j d", p=128), o_sb[ob][:],
                          reads=[t_osb[ob]], writes=[tk])
                else:
                    P.dma("sp", o_d[h, q0:q1, :], o_sb[ob][0:nq, 0, :], reads=[t_osb[ob]], writes=[tk])

            for (q0, q1, nkt) in ((0, 512, NKT), (512, 1024, NKT), (1024, NT, 2)):
                do_tile(h, sl, mla, scale, q0, q1, nkt, nfin % 2)
                nfin += 1
        P.wait_all("sp", out_toks)
        P.emit()
    return nc


SEGS = ((0, CTX), (CTX, NALL))
BLK512 = [(i, min(i + 512, NALL)) for i in range(0, NALL, 512)]


def emit_conv5(P, eng_a, eng_b, y, ty, x, tx, w, tw, wcol0, bias_ap=None):
    for (s0, s1) in SEGS:
        if bias_ap is None:
            P.op(eng_a, lambda e, s0=s0, s1=s1: e.tensor_scalar(
                out=y[:, s0:s1], in0=x[:, s0:s1], scalar1=w[:, wcol0 + 2:wcol0 + 3], scalar2=None, op0=ALU.mult),
                reads=[tx, tw], writes=[ty])
        else:
            P.op(eng_a, lambda e, s0=s0, s1=s1: e.tensor_scalar(
                out=y[:, s0:s1], in0=x[:, s0:s1], scalar1=w[:, wcol0 + 2:wcol0 + 3], scalar2=bias_ap,
                op0=ALU.mult, op1=ALU.add), reads=[tx, tw], writes=[ty])
        for o in (-2, -1, 1, 2):
            a0 = s0 + max(0, -o)
            a1 = s1 - max(0, o)
            P.op(eng_a, lambda e, a0=a0, a1=a1, o=o: e.scalar_tensor_tensor(
                out=y[:, a0:a1], in0=x[:, a0 + o:a1 + o], scalar=w[:, wcol0 + o + 2:wcol0 + o + 3], in1=y[:, a0:a1],
                op0=ALU.mult, op1=ALU.add), reads=[tx, tw, ty], writes=[ty])


def build_k2b():
    nc = bass.Bass("TRN2", target_bir_lowering=False)
    dt_in = lambda name, shape, dt=F32: nc.dram_tensor(name, shape, dt, kind="ExternalInput").ap()
    dt_out = lambda name, shape, dt=F32: nc.dram_tensor(name, shape, dt, kind="ExternalOutput").ap()
    ux_d = dt_in("ux", [128, NALL])
    prm_d = dt_in("prm", [128, 16])
    w_d = dt_in("w", [128, 2, 128])
    h_d = dt_out("h", [128, NALL])
    with contextlib.ExitStack() as es:
        P = Prog(nc, es)
        sb = lambda name, shape, dt=F32: es.enter_context(nc.sbuf_tensor("s_" + name, shape, dt))
        ps = lambda name, shape, dt=F32: es.enter_context(nc.psum_tensor("p_" + name, shape, dt))
        ux = sb("ux", [128, NALL]); t_ux = Tok()
        xs = sb("xs", [128, NALL]); t_xs = Tok()
        xb = sb("xb", [128, NALL], BF16); t_xb = Tok()
        av = sb("av", [128, NALL]); t_av = Tok()
        bv = sb("bv", [128, NALL]); t_bv = Tok()
        prm = sb("prm", [128, 16]); t_prm = Tok()
        c1 = sb("c1", [128, 2]); t_c1 = Tok()
        wf = sb("wf", [128, 2, 128]); wb = sb("wb", [128, 2, 128], BF16); t_w = Tok()
        rr = [sb("rr%d" % i, [128, 512]) for i in range(2)]; t_rr = [Tok(), Tok()]
        ii = [sb("ii%d" % i, [128, 512]) for i in range(2)]; t_ii = [Tok(), Tok()]
        pr = [ps("pr%d" % i, [128, 512]) for i in range(2)]; t_pr = [Tok(), Tok()]
        pi = [ps("pi%d" % i, [128, 512]) for i in range(2)]; t_pi = [Tok(), Tok()]
        P.dma("sp", ux[:], ux_d, writes=[t_ux])
        P.dma("sp", prm[:], prm_d, writes=[t_prm])
        P.dma("sp", wf[:], w_d, writes=[t_w])
        P.op("pool", lambda e: e.tensor_copy(out=wb[:], in_=wf[:]), reads=[t_w], writes=[t_w])
        P.op("act", lambda e: e.activation(out=c1[:, 0:1], in_=prm[:, 8:9], func=AF.Exp, scale=-1.0), reads=[t_prm], writes=[t_c1])
        P.op("act", lambda e: e.activation(out=c1[:, 0:1], in_=c1[:, 0:1], func=AF.Ln, bias=1.0), reads=[t_c1], writes=[t_c1])
        P.op("dve", lambda e: e.tensor_scalar(out=c1[:, 1:2], in0=c1[:, 0:1], scalar1=-16.0, scalar2=None, op0=ALU.mult), reads=[t_c1], writes=[t_c1])
        P.op("dve", lambda e: e.tensor_scalar(out=c1[:, 0:1], in0=c1[:, 0:1], scalar1=-8.0, scalar2=None, op0=ALU.mult), reads=[t_c1], writes=[t_c1])
        emit_conv5(P, "dve", "pool", xs, t_xs, ux, t_ux, prm, t_prm, 0, bias_ap=prm[:, 5:6])
        P.op("pool", lambda e: e.tensor_copy(out=xb[:], in_=xs[:]), reads=[t_xs], writes=[t_xb])
        for bi, (a0, a1) in enumerate(BLK512):
            n = a1 - a0
            b = bi % 2
            P.op("pe", lambda e, b=b, a0=a0, a1=a1, n=n: e.matmul(pr[b][:, 0:n], lhsT=wb[:, 0, :], rhs=xb[:, a0:a1], start=True, stop=True),
                 reads=[t_w, t_xb], writes=[t_pr[b]])
            P.op("pe", lambda e, b=b, a0=a0, a1=a1, n=n: e.matmul(pi[b][:, 0:n], lhsT=wb[:, 1, :], rhs=xb[:, a0:a1], start=True, stop=True),
                 reads=[t_w, t_xb], writes=[t_pi[b]])
            P.op("act", lambda e, b=b, n=n: e.activation(out=rr[b][:, 0:n], in_=pr[b][:, 0:n], func=AF.Sigmoid, bias=prm[:, 6:7]),
                 reads=[t_pr[b], t_prm], writes=[t_rr[b]])
            P.op("act", lambda e, b=b, n=n: e.activation(out=ii[b][:, 0:n], in_=pi[b][:, 0:n], func=AF.Sigmoid, bias=prm[:, 7:8]),
                 reads=[t_pi[b], t_prm], writes=[t_ii[b]])
            P.op("act", lambda e, b=b, a0=a0, a1=a1, n=n: e.activation(out=av[:, a0:a1], in_=rr[b][:, 0:n], func=AF.Exp, scale=c1[:, 0:1]),
                 reads=[t_rr[b], t_c1], writes=[t_av])
            P.op("act", lambda e, b=b, n=n: e.activation(out=rr[b][:, 0:n], in_=rr[b][:, 0:n], func=AF.Exp, scale=c1[:, 1:2]),
                 reads=[t_rr[b], t_c1], writes=[t_rr[b]])
            P.op("act", lambda e, b=b, n=n: e.activation(out=rr[b][:, 0:n], in_=rr[b][:, 0:n], func=AF.Sqrt, scale=-1.0, bias=1.0),
                 reads=[t_rr[b]], writes=[t_rr[b]])
            P.op("dve", lambda e, b=b, a0=a0, a1=a1, n=n: e.tensor_tensor(out=ii[b][:, 0:n], in0=ii[b][:, 0:n], in1=xs[:, a0:a1], op=ALU.mult),
                 reads=[t_ii[b], t_xs], writes=[t_ii[b]])
            P.op("dve", lambda e, b=b, a0=a0, a1=a1, n=n: e.tensor_tensor(out=bv[:, a0:a1], in0=ii[b][:, 0:n], in1=rr[b][:, 0:n], op=ALU.mult),
                 reads=[t_ii[b], t_rr[b]], writes=[t_bv])
        SC = 2112
        for i, s0 in enumerate(range(0, NALL, SC)):
            init = 0.0 if i == 0 else ux[:, s0 - 1:s0]
            P.op("dve", lambda e, s0=s0, init=init: e.tensor_tensor_scan(
                out=ux[:, s0:s0 + SC], data0=av[:, s0:s0 + SC], data1=bv[:, s0:s0 + SC], initial=init,
                op0=ALU.mult, op1=ALU.add), reads=[t_av, t_bv, t_ux], writes=[t_ux])
        tk = Tok()
        P.dma("sp", h_d, ux[:], reads=[t_ux], writes=[tk])
        P.wait_all("sp", [tk])
        P.emit()
    return nc


NCH = NALL // 64


def build_c1():
    nc = bass.Bass("TRN2", target_bir_lowering=False)
    dt_in = lambda name, shape, dt=F32: nc.dram_tensor(name, shape, dt, kind="ExternalInput").ap()
    dt_out = lambda name, shape, dt=F32: nc.dram_tensor(name, shape, dt, kind="ExternalOutput").ap()
    u_d = dt_in("u", [3, 128, NALL])
    cw_d = dt_in("cw", [128, 16])
    ab_d = dt_in("ab", [64, 2, NCH])
    sc_d = dt_in("sc", [64, 2])
    o_d = dt_out("qkv", [3, 128, NALL])
    bg_d = dt_out("bg", [64, 2, NCH])
    with contextlib.ExitStack() as es:
        P = Prog(nc, es)
        sb = lambda name, shape, dt=F32: es.enter_context(nc.sbuf_tensor("s_" + name, shape, dt))
        ps = lambda name, shape, dt=F32: es.enter_context(nc.psum_tensor("p_" + name, shape, dt))
        x = [sb("x%d" % i, [128, NALL]) for i in range(2)]; t_x = [Tok(), Tok()]
        y = [sb("y%d" % i, [128, NALL]) for i in range(2)]; t_y = [Tok(), Tok()]
        sq = sb("sq", [128, NALL]); t_sq = Tok()
        cw = sb("cw", [128, 16]); t_cw = Tok()
        ones = sb("ones", [128, 128]); t_ones = Tok()
        ab = sb("ab", [64, 2, NCH]); t_ab = Tok()
        sc = sb("sc", [64, 2]); t_sc = Tok()
        bg = sb("bg", [64, 2, NCH]); t_bg = Tok()
        pp = [ps("pp%d" % i, [128, 512]) for i in range(2)]; t_pp = [Tok(), Tok()]
        P.dma("sp", cw[:], cw_d, writes=[t_cw])
        P.dma("sp", ab[:], ab_d, writes=[t_ab])
        P.dma("sp", sc[:], sc_d, writes=[t_sc])
        P.op("pool", lambda e: e.memset(ones[:], 1.0), writes=[t_ones])
        P.op("act", lambda e: e.activation(out=bg[:, 0, :], in_=ab[:, 0, :], func=AF.Sigmoid), reads=[t_ab], writes=[t_bg])
        P.op("act", lambda e: e.activation(out=bg[:, 1, :], in_=ab[:, 1, :], func=AF.Exp, bias=sc[:, 1:2]), reads=[t_ab, t_sc], writes=[t_bg])
        P.op("act", lambda e: e.activation(out=bg[:, 1, :], in_=bg[:, 1, :], func=AF.Ln, bias=1.0), reads=[t_bg], writes=[t_bg])
        P.op("act", lambda e: e.activation(out=sc[:, 0:1], in_=sc[:, 0:1], func=AF.Exp), reads=[t_sc], writes=[t_sc])
        P.op("dve", lambda e: e.tensor_scalar(out=bg[:, 1, :], in0=bg[:, 1, :], scalar1=sc[:, 0:1], scalar2=-1.0,
                                              op0=ALU.mult, op1=ALU.mult), reads=[t_bg, t_sc], writes=[t_bg])
        tk_bg = Tok()
        P.dma("sp", bg_d, bg[:], reads=[t_bg], writes=[tk_bg])
        outs = [tk_bg]
        for i in range(3):
            b = i % 2
            P.dma("sp", x[b][:], u_d[i], writes=[t_x[b]])
            emit_conv5(P, "dve", "pool", y[b], t_y[b], x[b], t_x[b], cw, t_cw, 5 * i)
            P.op("act", lambda e, b=b: e.activation(out=y[b][:], in_=y[b][:], func=AF.Silu), reads=[t_y[b]], writes=[t_y[b]])
            if i < 2:
                P.op("pool", lambda e, b=b: e.tensor_tensor(out=sq[:], in0=y[b][:], in1=y[b][:], op=ALU.mult),
                     reads=[t_y[b]], writes=[t_sq])
                for bi, (a0, a1) in enumerate(BLK512):
                    n = a1 - a0
                    pb = bi % 2
                    P.op("pe", lambda e, pb=pb, a0=a0, a1=a1, n=n: e.matmul(pp[pb][:, 0:n], lhsT=ones[:], rhs=sq[:, a0:a1],
                                                                      start=True, stop=True), reads=[t_ones, t_sq], writes=[t_pp[pb]])
                    P.op("act", lambda e, pb=pb, a0=a0, a1=a1, n=n, b=b: e.activation(out=x[b][:, a0:a1], in_=pp[pb][:, 0:n], func=AF.Sqrt, bias=EPS),
                         reads=[t_pp[pb]], writes=[t_x[b]])
                P.op("dve", lambda e, b=b: e.reciprocal(out=x[b][:], in_=x[b][:]), reads=[t_x[b]], writes=[t_x[b]])
                fac = (128.0 ** -0.5) if i == 0 else 1.0
                P.op("dve", lambda e, b=b, fac=fac: e.scalar_tensor_tensor(out=y[b][:], in0=y[b][:], scalar=fac, in1=x[b][:],
                                                                       op0=ALU.mult, op1=ALU.mult), reads=[t_y[b], t_x[b]], writes=[t_y[b]])
            tk = Tok(); outs.append(tk)
            P.dma("sp", o_d[i], y[b][:], reads=[t_y[b]], writes=[tk])
        P.wait_all("sp", outs)
        P.emit()
    return nc


GRP = 8


def _dn_consts():
    c = np.zeros((64, 6, 64), np.float32)
    p = np.arange(64)[:, None]
    f = np.arange(64)[None, :]
    c[:, 0] = (p <= f)
    c[:, 1] = -(p <= f).astype(np.float32)
    c[:, 2] = -(p > f).astype(np.float32)
    c[:, 3] = -(f > p).astype(np.float32)
    c[:, 4] = (f >= p)
    c[:, 5] = (f == p)
    return c


def build_c2():
    nc = bass.Bass("TRN2", target_bir_lowering=False)
    dt_in = lambda name, shape, dt=F32: nc.dram_tensor(name, shape, dt, kind="ExternalInput").ap()
    dt_out = lambda name, shape, dt=F32: nc.dram_tensor(name, shape, dt, kind="ExternalOutput").ap()
    qT_d = dt_in("qT", [128, NALL]); kT_d = dt_in("kT", [128, NALL])
    ktm_d = dt_in("ktm", [64, NCH, 128]); vtm_d = dt_in("vtm", [64, NCH, 128])
    bg_d = dt_in("bg", [64, 2, NCH]); bbc_d = dt_in("bbc", [64, NALL]); cst_d = dt_in("cst", [64, 6, 64]); id_d = dt_in("ident", [128, 128])
    o_d = dt_out("o", [64, NCH, 128])
    with contextlib.ExitStack() as es:
        P = Prog(nc, es)
        sb = lambda name, shape, dt=F32: es.enter_context(nc.sbuf_tensor("s_" + name, shape, dt))
        ps = lambda name, shape, dt=F32: es.enter_context(nc.psum_tensor("p_" + name, shape, dt))
        cst = sb("cst", [64, 6, 64]); t_cst = Tok()
        bg = sb("bg", [64, 2, NCH]); t_bg = Tok()
        bbc = sb("bbc", [64, NALL]); t_bbc = Tok()
        ones = sb("ones", [64, 128]); t_ones = Tok()
        gb = sb("gb", [64, NCH, 64]); t_gb = Tok()
        gc = sb("gc", [64, NCH]); eg = sb("eg", [64, NCH]); beg = sb("beg", [64, NCH]); dk = sb("dk", [64, NCH]); t_gc = Tok()
        egl = sb("egl", [128, NCH]); t_egl = Tok()
        S = sb("S", [128, 128]); t_S = Tok()
        qT = [sb("qT%d" % i, [128, GRP * 64]) for i in range(2)]; kT = [sb("kT%d" % i, [128, GRP * 64]) for i in range(2)]
        ktm = [sb("ktm%d" % i, [64, GRP, 128]) for i in range(2)]; vtm = [sb("vtm%d" % i, [64, GRP, 128]) for i in range(2)]
        t_in = [Tok(), Tok()]
        E8 = sb("E8", [64, GRP, 64]); ET8 = sb("ET8", [64, GRP, 64]); t_E = Tok(); t_ET = Tok(); t_QK = [Tok(), Tok()]
        Pa = sb("Pa", [64, GRP, 64]); Pb = sb("Pb", [64, GRP, 64]); Pta = sb("Pta", [64, GRP, 64]); Ptb = sb("Ptb", [64, GRP, 64])
        t_P = [Tok(), Tok()]; t_Pt = [Tok(), Tok()]
        Tt = sb("Tt", [64, GRP, 64]); t_Tt = Tok()
        Rv = sb("Rv", [64, GRP, 128]); Rw = sb("Rw", [64, GRP, 128]); t_R = Tok()
        QK = [sb("QK%d" % i, [64, GRP, 64]) for i in range(2)]; kd = [sb("kd%d" % i, [64, GRP, 128]) for i in range(2)]
        U8 = [sb("U8%d" % i, [64, GRP, 128]) for i in range(2)]; W8 = [sb("W8%d" % i, [128, GRP, 64]) for i in range(2)]
        t_prep = [Tok(), Tok()]
        o_sb = [sb("osb%d" % i, [64, GRP, 128]) for i in range(2)]; t_osb = [Tok(), Tok()]
        vn = [sb("vn%d" % i, [64, 128]) for i in range(2)]; t_vn = [Tok(), Tok()]
        tmp = [sb("tmp%d" % i, [64, 128]) for i in range(2)]; t_tmp = [Tok(), Tok()]
        gb0 = ps("g0", [128, 512]); gb1 = ps("g1", [128, 512]); gb2 = ps("g2", [128, 512]); t_g = [Tok(), Tok(), Tok()]
        g0, g1, g2 = [t[0:64, :].rearrange("p (g c) -> p g c", c=64) for t in (gb0, gb1, gb2)]
        gm = [t[:, :].rearrange("p (g c) -> p g c", c=128) for t in (gb0, gb1, gb2)]
        gU = ps("gU", [64, 4, 128]); t_gU = Tok()
        gW = ps("gW", [128, GRP, 64]); t_gW = Tok()
        q3 = ps("q3", [64, 128]); t_q3 = Tok()
        q12 = ps("q12", [64, 2, 128]); t_q12 = Tok(); t_q12a = t_q12; t_q12b = t_q12
        q4 = ps("q4", [128, 128]); t_q4 = Tok()
        P.dma("sp", cst[:], cst_d, writes=[t_cst])
        P.dma("sp", bg[:], bg_d, writes=[t_bg])
        P.dma("sp", bbc[:], bbc_d, writes=[t_bbc])
        P.op("pool", lambda e: e.memset(ones[:], 1.0), writes=[t_ones])
        P.op("pool", lambda e: e.memset(S[:], 0.0), writes=[t_S])
        P.op("dve", lambda e: e.tensor_copy(out=gb[:], in_=bg[:, 1, :].unsqueeze(2).broadcast_to([64, NCH, 64])),
             reads=[t_bg], writes=[t_gb])
        P.op("pe", lambda e: e.matmul(g0[:, 0:3, :].rearrange("p a b -> p (a b)")[:, 0:NCH], lhsT=cst[:, 0, :], rhs=bg[:, 1, :],
                                      start=True, stop=True), reads=[t_cst, t_bg], writes=[t_g[0]])
        P.op("pe", lambda e: e.matmul(gW[:, 0:3, :].rearrange("p a b -> p (a b)")[:, 0:NCH], lhsT=ones[:, :], rhs=bg[:, 1, :],
                                      start=True, stop=True), reads=[t_ones, t_bg], writes=[t_gW])
        g0f = g0[:, 0:3, :].rearrange("p a b -> p (a b)")[:, 0:NCH]
        gWf = gW[:, 0:3, :].rearrange("p a b -> p (a b)")[:, 0:NCH]
        P.op("dve", lambda e: e.tensor_copy(out=gc[:], in_=g0f), reads=[t_g[0]], writes=[t_gc])
        P.op("act", lambda e: e.activation(out=eg[:], in_=g0f, func=AF.Exp), reads=[t_g[0]], writes=[t_gc])
        P.op("act", lambda e: e.activation(out=egl[:], in_=gWf, func=AF.Exp), reads=[t_gW], writes=[t_egl])
        P.op("dve", lambda e: e.tensor_tensor(out=dk[:], in0=gWf[0:64, :], in1=gc[:], op=ALU.subtract), reads=[t_gW, t_gc], writes=[t_gc])
        P.op("act", lambda e: e.activation(out=dk[:], in_=dk[:], func=AF.Exp), reads=[t_gc], writes=[t_gc])
        P.op("dve", lambda e: e.tensor_tensor(out=beg[:], in0=eg[:], in1=bg[:, 0, :], op=ALU.mult), reads=[t_gc, t_bg], writes=[t_gc])

        def bc_last(ap2d, n):
            return ap2d.unsqueeze(2).broadcast_to([64, ap2d.shape[1], n])

        def bc_mid(ap2d, g):
            return ap2d.unsqueeze(1).broadcast_to([64, g, ap2d.shape[1]])

        def prep_stages(gi, n0, G):
            st = []
            b = gi % 2
            c0, c1 = n0 * 64, (n0 + G) * 64

            def s_dma():
                P.dma("sp", qT[b][:, 0:G * 64], qT_d[:, c0:c1], writes=[t_in[b]])
                P.dma("sp", kT[b][:, 0:G * 64], kT_d[:, c0:c1], writes=[t_in[b]])
                P.dma("sp", ktm[b][:, 0:G, :], ktm_d[:, n0:n0 + G, :], writes=[t_in[b]])
                P.dma("sp", vtm[b][:, 0:G, :], vtm_d[:, n0:n0 + G, :], writes=[t_in[b]])
            st.append(s_dma)

            def s_mm0():
                for j in range(G):
                    n = n0 + j
                    ks = kT[b][:, j * 64:(j + 1) * 64]
                    P.op("pe", lambda e, j=j, n=n: e.matmul(g0[:, j, :], lhsT=cst[:, 0, :], rhs=gb[:, n, :], start=True, stop=False),
                         reads=[t_cst, t_gb], writes=[t_g[0]])
                    P.op("pe", lambda e, j=j, n=n: e.matmul(g0[:, j, :], lhsT=gb[:, n, :], rhs=cst[:, 1, :], start=False, stop=True),
                         reads=[t_cst, t_gb], writes=[t_g[0]])
                    P.op("pe", lambda e, j=j, ks=ks: e.matmul(g1[:, j, :], lhsT=ks, rhs=ks, start=True, stop=True),
                         reads=[t_in[b]], writes=[t_g[1]])
                    P.op("pe", lambda e, j=j, ks=ks: e.matmul(g2[:, j, :], lhsT=ks, rhs=qT[b][:, j * 64:(j + 1) * 64], start=True, stop=True),
                         reads=[t_in[b]], writes=[t_g[2]])
            st.append(s_mm0)

            def s_min():
                P.op("dve", lambda e: e.tensor_scalar(out=E8[:, 0:G, :], in0=g0[:, 0:G, :], scalar1=0.0, scalar2=None, op0=ALU.min),
                     reads=[t_g[0]], writes=[t_E])
                P.op("dve", lambda e: e.tensor_scalar(out=ET8[:, 0:G, :], in0=g0[:, 0:G, :], scalar1=-1.0, scalar2=0.0, op0=ALU.mult, op1=ALU.min),
                     reads=[t_g[0]], writes=[t_ET])
                P.op("pool", lambda e: e.tensor_tensor(out=Rv[:, 0:G, :], in0=vtm[b][:, 0:G, :], in1=bc_last(bg[:, 0, n0:n0 + G], 128), op=ALU.mult), reads=[t_in[b], t_bg], writes=[t_R])
                P.op("pool", lambda e: e.tensor_tensor(out=Rw[:, 0:G, :], in0=ktm[b][:, 0:G, :], in1=bc_last(beg[:, n0:n0 + G], 128), op=ALU.mult), reads=[t_in[b], t_gc], writes=[t_R])
                P.op("pool", lambda e: e.tensor_tensor(out=kd[b][:, 0:G, :], in0=ktm[b][:, 0:G, :], in1=bc_last(dk[:, n0:n0 + G], 128), op=ALU.mult), reads=[t_in[b], t_gc], writes=[t_prep[b]])
            st.append(s_min)

            def s_exp():
                P.op("act", lambda e: e.activation(out=E8[:, 0:G, :], in_=E8[:, 0:G, :], func=AF.Exp), reads=[t_E], writes=[t_E])
                P.op("act", lambda e: e.activation(out=ET8[:, 0:G, :], in_=ET8[:, 0:G, :], func=AF.Exp), reads=[t_ET], writes=[t_ET])
            st.append(s_exp)

            def s_mul1():
                P.op("dve", lambda e: e.tensor_tensor(out=Pa[:, 0:G, :], in0=g1[:, 0:G, :], in1=E8[:, 0:G, :], op=ALU.mult), reads=[t_g[1], t_E], writes=[t_P[0]])
                P.op("dve", lambda e: e.tensor_tensor(out=Pta[:, 0:G, :], in0=g1[:, 0:G, :], in1=ET8[:, 0:G, :], op=ALU.mult), reads=[t_g[1], t_ET], writes=[t_Pt[0]])
                P.op("dve", lambda e: e.tensor_tensor(out=QK[b][:, 0:G, :], in0=g2[:, 0:G, :], in1=ET8[:, 0:G, :], op=ALU.mult), reads=[t_g[2], t_ET], writes=[t_QK[b]])
            st.append(s_mul1)

            def s_mul2():
                P.op("pool", lambda e: e.tensor_tensor(out=Pa[:, 0:G, :], in0=Pa[:, 0:G, :], in1=bc_mid(cst[:, 2, :], G), op=ALU.mult), reads=[t_cst, t_P[0]], writes=[t_P[0]])
                P.op("pool", lambda e: e.tensor_tensor(out=Pta[:, 0:G, :], in0=Pta[:, 0:G, :], in1=bc_mid(cst[:, 3, :], G), op=ALU.mult), reads=[t_cst, t_Pt[0]], writes=[t_Pt[0]])
                P.op("pool", lambda e: e.tensor_tensor(out=QK[b][:, 0:G, :], in0=QK[b][:, 0:G, :], in1=bc_mid(cst[:, 4, :], G), op=ALU.mult), reads=[t_cst, t_QK[b]], writes=[t_QK[b]])
            st.append(s_mul2)

            def s_mul3():
                P.op("pool", lambda e: e.tensor_tensor(out=Pa[:, 0:G, :], in0=Pa[:, 0:G, :], in1=bc_last(bg[:, 0, n0:n0 + G], 64), op=ALU.mult), reads=[t_bg, t_P[0]], writes=[t_P[0]])
                P.op("pool", lambda e: e.tensor_tensor(out=Pta[:, 0:G, :], in0=Pta[:, 0:G, :], in1=bbc[:, c0:c1].rearrange("p (g c) -> p g c", c=64), op=ALU.mult),
                     reads=[t_bbc, t_Pt[0]], writes=[t_Pt[0]])
            st.append(s_mul3)

            def s_tt0():
                P.op("pool", lambda e: e.tensor_tensor(out=Tt[:, 0:G, :], in0=Pta[:, 0:G, :], in1=bc_mid(cst[:, 5, :], G), op=ALU.add), reads=[t_cst, t_Pt[0]], writes=[t_Tt])
            st.append(s_tt0)
            Pk, Ptk = [Pa, Pb], [Pta, Ptb]
            for k in range(5):
                a, nx = k % 2, (k + 1) % 2

                def s_pw(a=a):
                    for j in range(G):
                        P.op("pe", lambda e, j=j: e.matmul(g0[:, j, :], lhsT=Ptk[a][:, j, :], rhs=Pk[a][:, j, :], start=True, stop=True),
                             reads=[t_P[a], t_Pt[a]], writes=[t_g[0]])
                        P.op("pe", lambda e, j=j: e.matmul(g1[:, j, :], lhsT=Pk[a][:, j, :], rhs=Ptk[a][:, j, :], start=True, stop=True),
                             reads=[t_P[a], t_Pt[a]], writes=[t_g[1]])
                st.append(s_pw)

                def s_cp(nx=nx):
                    P.op("act", lambda e: e.activation(out=Pk[nx][:, 0:G, :], in_=g0[:, 0:G, :], func=AF.Identity), reads=[t_g[0]], writes=[t_P[nx]])
                    P.op("dve", lambda e: e.tensor_copy(out=Ptk[nx][:, 0:G, :], in_=g1[:, 0:G, :]), reads=[t_g[1]], writes=[t_Pt[nx]])
                st.append(s_cp)

                def s_tm(nx=nx):
                    for j in range(G):
                        P.op("pe", lambda e, j=j: e.matmul(g2[:, j, :], lhsT=Pk[nx][:, j, :], rhs=Tt[:, j, :], start=True, stop=True),
                             reads=[t_P[nx], t_Tt], writes=[t_g[2]])
                st.append(s_tm)

                def s_ta():
                    P.op("dve", lambda e: e.tensor_tensor(out=Tt[:, 0:G, :], in0=Tt[:, 0:G, :], in1=g2[:, 0:G, :], op=ALU.add), reads=[t_g[2], t_Tt], writes=[t_Tt])
                st.append(s_ta)

            def s_w():
                for j in range(G):
                    P.op("pe", lambda e, j=j: e.matmul(gW[:, j, :], lhsT=Rw[:, j, :], rhs=Tt[:, j, :], start=True, stop=True), reads=[t_Tt, t_R], writes=[t_gW])
            st.append(s_w)

            def s_wc():
                P.op("dve", lambda e: e.tensor_copy(out=W8[b][:, 0:G, :], in_=gW[:, 0:G, :]), reads=[t_gW], writes=[t_prep[b]])
            st.append(s_wc)
            for h0 in range(0, G, 4):
                h1 = min(G, h0 + 4)

                def s_u(h0=h0, h1=h1):
                    for j in range(h0, h1):
                        P.op("pe", lambda e, j=j: e.matmul(gU[:, j - h0, :], lhsT=Tt[:, j, :], rhs=Rv[:, j, :], start=True, stop=True), reads=[t_Tt, t_R], writes=[t_gU])
                st.append(s_u)

                def s_uc(h0=h0, h1=h1):
                    P.op("act", lambda e: e.activation(out=U8[b][:, h0:h1, :], in_=gU[:, 0:h1 - h0, :], func=AF.Identity), reads=[t_gU], writes=[t_prep[b]])
                st.append(s_uc)
            return st

        def seq_group(gi, n0, G, filler):
            b = gi % 2
            per = (len(filler) + G - 1) // G if filler else 0
            for j in range(G):
                n = n0 + j
                v = n % 2
                P.op("pe", lambda e, j=j: e.matmul(q3[:, :], lhsT=W8[b][:, j, :], rhs=S[:, :], start=True, stop=True), reads=[t_prep[b], t_S], writes=[t_q3])
                P.op("pe", lambda e, j=j: e.matmul(q12[:, 0, :], lhsT=qT[b][:, j * 64:(j + 1) * 64], rhs=S[:, :], start=True, stop=True), reads=[t_in[b], t_S], writes=[t_q12])
                P.op("dve", lambda e, j=j, v=v: e.tensor_tensor(out=vn[v][:], in0=U8[b][:, j, :], in1=q3[:, :], op=ALU.subtract), reads=[t_prep[b], t_q3], writes=[t_vn[v]])
                P.op("pe", lambda e, j=j, v=v: e.matmul(q4[:, :], lhsT=kd[b][:, j, :], rhs=vn[v][:], start=True, stop=True), reads=[t_prep[b], t_vn[v]], writes=[t_q4])
                P.op("pe", lambda e, j=j, v=v: e.matmul(q12[:, 1, :], lhsT=QK[b][:, j, :], rhs=vn[v][:], start=True, stop=True), reads=[t_QK[b], t_vn[v]], writes=[t_q12])
                P.op("dve", lambda e, n=n: e.scalar_tensor_tensor(out=S[:], in0=S[:], scalar=egl[:, n:n + 1], in1=q4[:, :], op0=ALU.mult, op1=ALU.add),
                     reads=[t_S, t_egl, t_q4], writes=[t_S])
                P.op("act", lambda e, n=n, v=v: e.activation(out=tmp[v][:], in_=q12[:, 0, :], func=AF.Identity, scale=eg[:, n:n + 1]), reads=[t_q12, t_gc], writes=[t_tmp[v]])
                P.op("dve", lambda e, j=j, v=v: e.tensor_tensor(out=o_sb[b][:, j, :], in0=tmp[v][:], in1=q12[:, 1, :], op=ALU.add), reads=[t_tmp[v], t_q12], writes=[t_osb[b]])
                for f in filler[j * per:(j + 1) * per]:
                    f()
            tk = Tok(); outs.append(tk)
            P.dma("sp", o_d[:, n0:n0 + G, :], o_sb[b][:, 0:G, :], reads=[t_osb[b]], writes=[tk])

        outs = []
        groups = [(gi, n0, min(GRP, NCH - n0)) for gi, n0 in enumerate(range(0, NCH, GRP))]
        for f in prep_stages(*groups[0]):
            f()
        for idx, grp in enumerate(groups):
            filler = prep_stages(*groups[idx + 1]) if idx + 1 < len(groups) else []
            seq_group(grp[0], grp[1], grp[2], filler)
        P.wait_all("sp", outs)
        P.emit()
    return nc


def build_k3():
    nc = bass.Bass("TRN2", target_bir_lowering=False)
    dt_in = lambda name, shape, dt=F32: nc.dram_tensor(name, shape, dt, kind="ExternalInput").ap()
    dt_out = lambda name, shape, dt=F32: nc.dram_tensor(name, shape, dt, kind="ExternalOutput").ap()
    m1_d = dt_in("m1", [16, 128, NT])
    m2_d = dt_in("m2", [8, 128, NT])
    zs_d = dt_in("zs", [16, 128, NT])
    nw_d = dt_in("nw", [128, 1])
    gt_d = dt_in("gt", [128, 2, D])
    x_d = dt_in("x", [NT, D])
    w_d = dt_in("w", [4, 128, 16, 512])
    xo_d = dt_out("xo", [NT, D])
    with contextlib.ExitStack() as es:
        P = Prog(nc, es)
        sb = lambda name, shape, dt=F32: es.enter_context(nc.sbuf_tensor("s_" + name, shape, dt))
        ps = lambda name, shape, dt=F32: es.enter_context(nc.psum_tensor("p_" + name, shape, dt))
        yT = sb("yT", [128, 16, NT], BF16); t_y = [Tok() for _ in range(16)]
        a_sb = [sb("a%d" % i, [128, NT]) for i in range(2)]; t_a = [Tok(), Tok()]
        b_sb = [sb("b%d" % i, [128, NT]) for i in range(2)]; t_b = [Tok(), Tok()]
        z_sb = [sb("z%d" % i, [128, NT]) for i in range(2)]; t_z = [Tok(), Tok()]
        sq = sb("sq", [128, NT]); t_sq = Tok()
        rs = sb("rs", [128, NT]); t_rs = Tok()
        nw = sb("nw", [128, 1]); t_nw = Tok()
        gt = sb("gt", [128, 2, D]); t_gt = Tok()
        ones = sb("ones", [128, 128]); t_ones = Tok()
        wst = [sb("wst%d" % i, [128, 16, 512]) for i in range(2)]; t_wst = [Tok(), Tok()]
        wbf = [sb("wbf%d" % i, [128, 16, 512], BF16) for i in range(2)]; t_wbf = [Tok(), Tok()]
        xt = [sb("xt%d" % i, [128, 512]) for i in range(2)]; t_xt = [Tok(), Tok()]
        xo = [sb("xo%d" % i, [128, 512]) for i in range(2)]; t_xo = [Tok(), Tok()]
        pn = ps("pn", [128, 512]); t_pn = Tok()
        po = [ps("po%d" % i, [128, 512]) for i in range(2)]; t_po = [Tok(), Tok()]
        P.dma("sp", nw[:], nw_d, writes=[t_nw])
        P.dma("sp", gt[:], gt_d, writes=[t_gt])
        P.op("pool", lambda e: e.memset(ones[:], 1.0), writes=[t_ones])
        P.op("dve", lambda e: e.tensor_scalar(out=nw[:], in0=nw[:], scalar1=128.0 ** 0.5, scalar2=None, op0=ALU.mult), reads=[t_nw], writes=[t_nw])

        def do_chunk(c):
            b = c % 2
            P.dma("sp", a_sb[b][:], m1_d[c], writes=[t_a[b]])
            P.dma("act", z_sb[b][:], zs_d[c], writes=[t_z[b]])
            if 4 <= c < 12:
                P.dma("act", b_sb[b][:], m2_d[c - 4], writes=[t_b[b]])
                P.op("pool", lambda e: e.tensor_tensor(out=a_sb[b][:], in0=a_sb[b][:], in1=b_sb[b][:], op=ALU.add),
                     reads=[t_a[b], t_b[b]], writes=[t_a[b]])
            if 8 <= c < 12:
                P.op("pool", lambda e: e.tensor_tensor(out=sq[:], in0=a_sb[b][:], in1=a_sb[b][:], op=ALU.mult), reads=[t_a[b]], writes=[t_sq])
                for (a0, a1) in TBLK:
                    P.op("pe", lambda e, a0=a0, a1=a1: e.matmul(pn[:, 0:a1 - a0], lhsT=ones[:], rhs=sq[:, a0:a1], start=True, stop=True),
                         reads=[t_ones, t_sq], writes=[t_pn])
                    P.op("act", lambda e, a0=a0, a1=a1: e.activation(out=rs[:, a0:a1], in_=pn[:, 0:a1 - a0], func=AF.Sqrt, bias=128 * EPS),
                         reads=[t_pn], writes=[t_rs])
                P.op("dve", lambda e: e.reciprocal(out=rs[:], in_=rs[:]), reads=[t_rs], writes=[t_rs])
                P.op("dve", lambda e: e.scalar_tensor_tensor(out=a_sb[b][:], in0=a_sb[b][:], scalar=nw[:, 0:1], in1=rs[:], op0=ALU.mult, op1=ALU.mult),
                     reads=[t_a[b], t_nw, t_rs], writes=[t_a[b]])
            P.op("dve", lambda e: e.tensor_tensor(out=yT[:, c, :], in0=a_sb[b][:], in1=z_sb[b][:], op=ALU.mult),
                 reads=[t_a[b], t_z[b]], writes=[t_y[c]])
        outs = []
        cnt = 0
        def load_wo(cb):
            wbi = cb % 2
            P.dma("act", wst[wbi][:], w_d[cb], writes=[t_wst[wbi]])
            for hh in range(2):
                P.op("dve" if hh == 0 else "pool", lambda e, hh=hh: e.tensor_copy(out=wbf[wbi][:, hh * 8:(hh + 1) * 8, :], in_=wst[wbi][:, hh * 8:(hh + 1) * 8, :]),
                     reads=[t_wst[wbi]], writes=[t_wbf[wbi]])
        load_wo(0)
        for c in range(16):
            do_chunk(c)
        for cb in range(4):
            wbi = cb % 2
            if cb + 1 < 4:
                load_wo(cb + 1)

            def do_tile(ti, cb, wbi, i):
                r0 = ti * 128
                np_ = 128 if ti < 8 else CTK
                m = 0 if ti < 8 else 1
                P.dma("act", xt[i][0:np_, :], x_d[r0:r0 + np_, cb * 512:(cb + 1) * 512], writes=[t_xt[i]])
                for c in range(16):
                    P.op("pe", lambda e, c=c: e.matmul(po[i][0:np_, :], lhsT=yT[:, c, r0:r0 + np_], rhs=wbf[wbi][:, c, :],
                                                      start=(c == 0), stop=(c == 15)), reads=[t_y[c], t_wbf[wbi]], writes=[t_po[i]])
                P.op("dve", lambda e: e.tensor_tensor(out=xo[i][0:np_, :], in0=po[i][0:np_, :], in1=gt[0:np_, m, cb * 512:(cb + 1) * 512], op=ALU.mult),
                     reads=[t_po[i], t_gt], writes=[t_xo[i]])
                P.op("pool", lambda e: e.tensor_tensor(out=xo[i][0:np_, :], in0=xo[i][0:np_, :], in1=xt[i][0:np_, :], op=ALU.add),
                     reads=[t_xo[i], t_xt[i]], writes=[t_xo[i]])
                tk = Tok(); outs.append(tk)
                P.dma("sp", xo_d[r0:r0 + np_, cb * 512:(cb + 1) * 512], xo[i][0:np_, :], reads=[t_xo[i]], writes=[tk])
            for ti in range(9):
                do_tile(ti, cb, wbi, cnt % 2)
                cnt += 1
        P.wait_all("sp", outs)
        P.emit()
    return nc


def _gather(res, name):
    ctx = np.concatenate([np.asarray(r[name])[..., TOK:] for r in res], axis=-1)
    lat = np.concatenate([np.asarray(r[name])[..., :TOK] for r in res], axis=-1)
    return np.concatenate([ctx, lat], axis=-1)


def _flipseg(a):
    return np.concatenate([a[..., :CTX][..., ::-1], a[..., CTX:][..., ::-1]], axis=-1)


def _core_slice(a, i):
    return np.concatenate([a[..., CTX + i * TOK:CTX + (i + 1) * TOK], a[..., i * CTK:(i + 1) * CTK]], axis=-1)


def _taps5(cw4, d):
    out = np.zeros((cw4.shape[1], 5), np.float32)
    if d == 0:
        out[:, 0:4] = cw4.T
    else:
        out[:, 1] = cw4[3]; out[:, 2] = cw4[2]; out[:, 3] = cw4[1]; out[:, 4] = cw4[0]
    return out


_PROGS = {}


def _prog(name, builder):
    if name not in _PROGS:
        _PROGS[name] = builder()
    return _PROGS[name]


def _forward(inputs, depth=DEPTH):
    f32 = lambda a: np.ascontiguousarray(np.asarray(a, np.float32))
    x_full = f32(inputs["x"][0]).copy()
    ctx_full = f32(inputs["ctx"][0]).copy()
    ada = run_k0(inputs)
    for l in range(depth):
        im1 = prep_k1(inputs, l, x_full, ctx_full, ada)
        r1 = _run(_prog("k1", build_k1), im1, "k1")
        ka = _gather(r1, "ka"); kdn = _gather(r1, "kdn"); kdr = _gather(r1, "kdr")
        va = _gather(r1, "va"); vd = _gather(r1, "vd")
        km = np.ascontiguousarray(np.concatenate([ka, kdn], axis=0))
        vall = np.concatenate([va, vd], axis=0)
        vaug = np.zeros((6, 128, NKT, VW), NPBF)
        vaug[:, :, :, 0:128] = vall.transpose(0, 2, 1).reshape(6, NKT, 128, 128).transpose(0, 2, 1, 3)
        vaug[:, :, :, 128] = 1.0
        im2 = []
        for i in range(NCORES):
            qm = np.ascontiguousarray(np.concatenate([np.asarray(r1[i]["qa"]), np.asarray(r1[i]["qdn"])], axis=0).transpose(1, 0, 2))
            qr = np.ascontiguousarray(np.asarray(r1[i]["qdr"]).transpose(1, 0, 2))
            im2.append({"qm": qm, "qr": qr, "km": km, "kr": np.ascontiguousarray(kdr), "va": vaug})
        r2a = _run(_prog("k2a", build_k2a), im2, "k2a")
        ux = _gather(r1, "ux")
        cwl = f32(inputs["lru_conv_w"][l])
        im3 = []
        for i in range(NCORES):
            d, blk = i // 4, i % 4
            sl = slice(blk * 128, (blk + 1) * 128)
            u = ux[blk] if d == 0 else _flipseg(ux[blk])
            prm = np.zeros((128, 16), np.float32)
            prm[:, 0:5] = _taps5(cwl[:, sl], d)
            prm[:, 5] = inputs["lru_conv_b"][l][sl]
            prm[:, 6] = inputs["lru_b_a"][l][d][sl]
            prm[:, 7] = inputs["lru_b_x"][l][d][sl]
            prm[:, 8] = inputs["lru_lambda"][l][d][sl]
            w = np.ascontiguousarray(np.stack([f32(inputs["lru_w_a"][l][d][blk]), f32(inputs["lru_w_x"][l][d][blk])], axis=1))
            im3.append({"ux": f32(u), "prm": prm, "w": w})
        r2b = _run(_prog("k2b", build_k2b), im3, "k2b")
        hdir = [[None] * 4, [None] * 4]
        for i in range(NCORES):
            d, blk = i // 4, i % 4
            h = np.asarray(r2b[i]["h"])
            hdir[d][blk] = h if d == 0 else _flipseg(h)
        uqkv = _gather(r1, "uqkv")
        uab = _gather(r1, "uab")
        cwd = f32(inputs["dn_conv_w"][l])
        im4 = []
        for i in range(NCORES):
            d, hd = i // 4, i % 4
            u = np.stack([uqkv[hd], uqkv[4 + hd], uqkv[8 + hd]], axis=0)
            br, ar = uab[d * 8 + hd], uab[d * 8 + 4 + hd]
            if d == 1:
                u = _flipseg(u); br = _flipseg(br); ar = _flipseg(ar)
            cw = np.zeros((128, 16), np.float32)
            for j in range(3):
                c = j * 4 + hd
                cw[:, 5 * j:5 * j + 5] = _taps5(cwd[:, c * 128:(c + 1) * 128], d)
            ab = np.ascontiguousarray(np.stack([br.reshape(NCH, 64).T, ar.reshape(NCH, 64).T], axis=1))
            sc = np.zeros((64, 2), np.float32)
            sc[:, 0] = inputs["dn_a_log"][l][d][hd]
            sc[:, 1] = inputs["dn_dt_bias"][l][d][hd]
            im4.append({"u": f32(u), "cw": cw, "ab": f32(ab), "sc": sc})
        rc1 = _run(_prog("c1", build_c1), im4, "c1")
        dnc = _dn_consts()
        im5 = []
        for i in range(NCORES):
            qkv = np.asarray(rc1[i]["qkv"]); bg = np.asarray(rc1[i]["bg"])
            tm = lambda a: np.ascontiguousarray(a.T.reshape(NCH, 64, 128).transpose(1, 0, 2))
            brow = bg[:, 0, :].T.reshape(-1)
            im5.append({"qT": f32(qkv[0]), "kT": f32(qkv[1]), "ktm": tm(qkv[1]), "vtm": tm(qkv[2]), "bg": f32(bg),
                        "bbc": np.ascontiguousarray(np.broadcast_to(brow[None, :], (64, NALL))), "cst": dnc,
                        "ident": np.eye(128, dtype=np.float32)})
        rc2 = _run(_prog("c2", build_c2), im5, "c2")
        odir = [[None] * 4, [None] * 4]
        for i in range(NCORES):
            d, hd = i // 4, i % 4
            o = np.asarray(rc2[i]["o"]).transpose(1, 0, 2).reshape(NALL, 128).T
            odir[d][hd] = o if d == 0 else _flipseg(o)
        w_out = f32(inputs["w_out"][l])
        wl = np.ascontiguousarray(w_out.reshape(16, 128, 4, 512).transpose(2, 1, 0, 3))
        gt = np.ascontiguousarray(np.broadcast_to(np.stack([ada[l, 0, 4096:], ada[l, 1, 4096:]], axis=0)[None], (128, 2, D)))
        nw = f32(inputs["dn_norm_w"][l]).reshape(128, 1)
        im6 = []
        for i in range(NCORES):
            o2 = np.asarray(r2a[i]["o"])
            m1 = np.empty((16, 128, NT), np.float32)
            m2 = np.empty((8, 128, NT), np.float32)
            for j in range(4):
                m1[j] = o2[j].T
                m1[12 + j] = o2[4 + j].T
                m1[4 + j] = _core_slice(hdir[0][j], i)
                m2[j] = _core_slice(hdir[1][j], i)
                m1[8 + j] = _core_slice(odir[0][j], i)
                m2[4 + j] = _core_slice(odir[1][j], i)
            im6.append({"m1": m1, "m2": m2, "zs": f32(r1[i]["zs"]), "nw": nw, "gt": gt, "x": im1[i]["x"], "w": wl})
        r3 = _run(_prog("k3", build_k3), im6, "k3")
        for i in range(NCORES):
            xo = np.asarray(r3[i]["xo"])
            x_full[i * TOK:(i + 1) * TOK] = xo[:TOK]
            ctx_full[i * CTK:(i + 1) * CTK] = xo[TOK:]
    return x_full, ctx_full


def kernel(**inputs):
    x_full, _ = _forward(inputs, DEPTH)
    return np.ascontiguousarray(x_full[None].astype(np.float32))
```

```python
import contextlib
import numpy as np
import ml_dtypes
import concourse.bass as bass
import concourse.mybir as mybir
from concourse.bass_utils import run_bass_kernel_spmd

F32 = mybir.dt.float32
BF16 = mybir.dt.bfloat16
AF = mybir.ActivationFunctionType
ALU = mybir.AluOpType
AX = mybir.AxisListType
NPBF = ml_dtypes.bfloat16

NCORES = 8
D = 2048
SEQ = 8192
CTX = 256
DEPTH = 4
TOK = SEQ // NCORES
CTK = CTX // NCORES
NT = TOK + CTK
NALL = SEQ + CTX
EPS = 1e-6
IN_W = 5840


class Tok:
    __slots__ = ("lw", "rd", "name")

    def __init__(self, name=""):
        self.lw = None
        self.rd = {}
        self.name = name


class Prog:
    ENG = ("pe", "act", "dve", "pool", "sp")

    def __init__(self, nc, es, n_dma_sems=8, same_engine_sync=True):
        self.nc = nc
        self.es = es
        self.eng = {"pe": nc.tensor, "act": nc.scalar, "dve": nc.vector,
                    "pool": nc.gpsimd, "sp": nc.sync}
        self.streams = {e: [] for e in self.ENG}
        self.sems = {}
        self.count = {}
        self.waited = {e: {} for e in self.ENG}
        for e in self.ENG:
            self.sems[e] = es.enter_context(nc.semaphore("sem_" + e))
            self.count[e] = 0
        self.dma_pool = {}
        self.dma_rr = {}
        for q in ("sp", "pool", "act"):
            lst = []
            for i in range(n_dma_sems):
                k = "dma_%s_%d" % (q, i)
                self.sems[k] = es.enter_context(nc.semaphore(k))
                self.count[k] = 0
                lst.append(k)
            self.dma_pool[q] = lst
            self.dma_rr[q] = 0
        self.same_engine_sync = same_engine_sync
        self.n_ops = 0

    def _deps(self, eng, reads, writes):
        deps = {}

        def add(k, v):
            if deps.get(k, 0) < v:
                deps[k] = v
        for t in reads:
            if t.lw is not None:
                add(*t.lw)
        for t in writes:
            if t.lw is not None:
                add(*t.lw)
            for k, v in t.rd.items():
                add(k, v)
        return deps

    def _emit_waits(self, eng, deps):
        for k, v in deps.items():
            if k == eng:
                if eng == "pe" or not self.same_engine_sync:
                    continue
            if self.waited[eng].get(k, 0) >= v:
                continue
            self.waited[eng][k] = v
            self.streams[eng].append(("wait", k, v))

    def _mark(self, key, val, reads, writes):
        for t in writes:
            t.lw = (key, val)
            t.rd = {}
        for t in reads:
            if t.rd.get(key, 0) < val:
                t.rd[key] = val

    def op(self, eng, fn, reads=(), writes=()):
        deps = self._deps(eng, reads, writes)
        self._emit_waits(eng, deps)
        self.count[eng] += 1
        self.streams[eng].append(("op", fn, eng, 1))
        self._mark(eng, self.count[eng], reads, writes)
        self.n_ops += 1

    def dma(self, q, out, in_, reads=(), writes=()):
        deps = self._deps(q, reads, writes)
        pool = self.dma_pool[q]
        k = pool[self.dma_rr[q] % len(pool)]
        self.dma_rr[q] += 1
        if self.count[k] > 0:
            deps[k] = max(deps.get(k, 0), self.count[k])
        self._emit_waits(q, deps)
        self.count[k] += 16
        self.streams[q].append(("op", lambda e: e.dma_start(out=out, in_=in_), k, 16))
        self._mark(k, self.count[k], reads, writes)
        self.n_ops += 1

    def wait_all(self, eng, toks):
        deps = {}
        for t in toks:
            if t.lw is not None and deps.get(t.lw[0], 0) < t.lw[1]:
                deps[t.lw[0]] = t.lw[1]
        for k, v in deps.items():
            self.streams[eng].append(("wait", k, v))

    def emit(self):
        nc = self.nc
        with nc.Block() as block:
            def run(ename):
                def body(e):
                    for item in self.streams[ename]:
                        if item[0] == "wait":
                            e.wait_ge(self.sems[item[1]], item[2])
                        else:
                            _, fn, k, inc = item
                            fn(e).then_inc(self.sems[k], inc)
                return body
            block.tensor(run("pe"))
            block.scalar(run("act"))
            block.vector(run("dve"))
            block.gpsimd(run("pool"))
            block.sync(run("sp"))


def _run(nc, in_maps, tag=""):
    import sys, time
    t0 = time.time()
    res = run_bass_kernel_spmd(nc, in_maps, core_ids=list(range(NCORES)))
    print("[launch %s] %.1fs" % (tag, time.time() - t0), file=sys.stderr, flush=True)
    return res.results


ADA_N = 3 * D
ADA_PC = ADA_N // NCORES


def build_k0():
    nc = bass.Bass("TRN2", target_bir_lowering=False)
    cT = nc.dram_tensor("cT", [128, 16, 2], F32, kind="ExternalInput").ap()
    w = nc.dram_tensor("w", [DEPTH, D, ADA_PC], F32, kind="ExternalInput").ap()
    b = nc.dram_tensor("b", [DEPTH, 2, ADA_PC], F32, kind="ExternalInput").ap()
    o = nc.dram_tensor("o", [DEPTH, 2, ADA_PC], F32, kind="ExternalOutput").ap()
    with contextlib.ExitStack() as es:
        P = Prog(nc, es)
        sb = lambda name, shape, dt: es.enter_context(nc.sbuf_tensor(name, shape, dt))
        ps = lambda name, shape, dt: es.enter_context(nc.psum_tensor(name, shape, dt))
        c_sb = sb("c_sb", [128, 16, 2], F32)
        s_sb = sb("s_sb", [128, 16, 2], F32)
        b_sb = sb("b_sb", [2, DEPTH, ADA_PC], F32)
        o_sb = sb("o_sb", [2, DEPTH, ADA_PC], F32)
        NB = 4
        w_sb = [sb("w_sb%d" % i, [128, 4, ADA_PC], F32) for i in range(NB)]
        acc = [ps("acc%d" % i, [2, 512], F32) for i in range(4)]
        t_c, t_s, t_b, t_o = Tok(), Tok(), Tok(), Tok()
        t_w = [Tok() for _ in range(NB)]
        t_acc = [Tok() for _ in range(4)]
        P.dma("sp", c_sb[:], cT, writes=[t_c])
        P.dma("sp", b_sb[:], b.rearrange("l m n -> m l n"), writes=[t_b])
        P.op("act", lambda e: e.activation(out=s_sb[:], in_=c_sb[:], func=AF.Silu), reads=[t_c], writes=[t_s])
        wi = 0
        for l in range(DEPTH):
            wv = w[l].rearrange("(c p) n -> p c n", p=128)
            for g in range(4):
                bi = wi % NB
                wi += 1
                q = "sp" if (wi % 2) else "pool"
                P.dma(q, w_sb[bi][:], wv[:, g * 4:(g + 1) * 4, :], writes=[t_w[bi]])
                for kk in range(4):
                    kc = g * 4 + kk
                    for j, (n0, n1) in enumerate(((0, 512), (512, 768))):
                        a = acc[(l % 2) * 2 + j]
                        P.op("pe", lambda e, a=a, bi=bi, kk=kk, kc=kc, n0=n0, n1=n1: e.matmul(
                            a[:, 0:n1 - n0], lhsT=s_sb[:, kc, :], rhs=w_sb[bi][:, kk, n0:n1],
                            start=(kc == 0), stop=(kc == 15)),
                            reads=[t_s, t_w[bi]], writes=[t_acc[(l % 2) * 2 + j]])
            for j, (n0, n1) in enumerate(((0, 512), (512, 768))):
                a = acc[(l % 2) * 2 + j]
                P.op("dve", lambda e, a=a, l=l, n0=n0, n1=n1: e.tensor_tensor(
                    out=o_sb[:, l, n0:n1], in0=a[:, 0:n1 - n0], in1=b_sb[:, l, n0:n1], op=ALU.add),
                    reads=[t_acc[(l % 2) * 2 + j], t_b], writes=[t_o])
        P.dma("sp", o.rearrange("l m n -> m l n"), o_sb[:], reads=[t_o], writes=[t_o])
        P.wait_all("sp", [t_o])
        P.emit()
    return nc


def run_k0(inputs):
    c = np.asarray(inputs["c"], np.float32).reshape(D)
    cc = np.asarray(inputs["c_ctx"], np.float32).reshape(D)
    cm = np.stack([c, cc], axis=1)
    cT = np.ascontiguousarray(cm.reshape(16, 128, 2).transpose(1, 0, 2))
    w_ada = np.asarray(inputs["w_ada"], np.float32)
    b_ada = np.asarray(inputs["b_ada"], np.float32)
    nc = build_k0()
    in_maps = []
    for i in range(NCORES):
        sl = slice(i * ADA_PC, (i + 1) * ADA_PC)
        bb = np.ascontiguousarray(np.broadcast_to(b_ada[:, None, sl], (DEPTH, 2, ADA_PC)))
        in_maps.append({"cT": cT, "w": np.ascontiguousarray(w_ada[:, :, sl]), "b": bb})
    res = _run(nc, in_maps, "k0")
    ada = np.concatenate([r["o"] for r in res], axis=2)
    return ada


def _col_tiles():
    tiles = []
    def add(name, c0, n, step=128):
        for i in range(0, n, step):
            tiles.append((name, c0 + i, min(step, n - i)))
    add("qa", 0, 512); add("ka", 512, 256); add("va", 768, 256); add("z", 1024, 512)
    add("ux", 1536, 512); add("z", 2048, 512)
    add("uqkv", 2560, 1536); add("z", 4096, 512); add("uab", 4608, 16)
    add("cq", 4624, 384); add("ckv", 5008, 256); add("kr", 5264, 64); add("z", 5328, 512)
    return tiles
COL_TILES = _col_tiles()
NCT = len(COL_TILES)
TBLK = ((0, 512), (512, 1024), (1024, NT))


def build_k1():
    nc = bass.Bass("TRN2", target_bir_lowering=False)
    dt_in = lambda name, shape, dt=F32: nc.dram_tensor(name, shape, dt, kind="ExternalInput").ap()
    dt_out = lambda name, shape, dt=F32: nc.dram_tensor(name, shape, dt, kind="ExternalOutput").ap()
    x_d = dt_in("x", [NT, D])
    mod_d = dt_in("mod", [128, 5, 16])
    w_d = dt_in("w", [NCT, 128, 16, 128])
    gn_d = dt_in("gn", [128, 8])
    g2_d = dt_in("g2", [128, 4])
    wuq_d = dt_in("wuq", [128, 3, 768])
    wukv_d = dt_in("wukv", [128, 2, 1024])
    cs_d = dt_in("cs", [128, 2, NT])
    cs64_d = dt_in("cs64", [64, 2, NT])
    cst_d = dt_in("cst", [128, 3, 128], BF16)
    qa_o = dt_out("qa", [4, 128, NT], BF16)
    ka_o = dt_out("ka", [2, 128, NT], BF16)
    va_o = dt_out("va", [2, 128, NT], BF16)
    zs_o = dt_out("zs", [16, 128, NT])
    ux_o = dt_out("ux", [4, 128, NT])
    uqkv_o = dt_out("uqkv", [12, 128, NT])
    uab_o = dt_out("uab", [16, NT])
    qdn_o = dt_out("qdn", [4, 128, NT], BF16)
    qdr_o = dt_out("qdr", [4, 64, NT], BF16)
    kdn_o = dt_out("kdn", [4, 128, NT], BF16)
    kdr_o = dt_out("kdr", [4, 64, NT], BF16)
    vd_o = dt_out("vd", [4, 128, NT], BF16)
    out_toks = []
    with contextlib.ExitStack() as es:
        P = Prog(nc, es)
        sb = lambda name, shape, dt=F32: es.enter_context(nc.sbuf_tensor("s_" + name, shape, dt))
        ps = lambda name, shape, dt=F32: es.enter_context(nc.psum_tensor("p_" + name, shape, dt))
        mod = sb("mod", [128, 5, 16]); t_mod = Tok()
        gain = sb("gain", [128, 2, 16]); t_gain = Tok()
        gn = sb("gn", [128, 8]); g2 = sb("g2", [128, 4]); t_gn = Tok()
        gs = sb("gs", [128, 12]); t_gs = Tok()
        wuq_f = sb("wuq_f", [128, 3, 768]); wuq = sb("wuq", [128, 3, 768], BF16); t_wuq = Tok()
        wukv_f = sb("wukv_f", [128, 2, 1024]); wukv = sb("wukv", [128, 2, 1024], BF16); t_wukv = Tok()
        cs = sb("cs", [128, 2, NT]); cs64 = sb("cs64", [64, 2, NT]); t_cs = Tok()
        cst = sb("cst", [128, 3, 128], BF16); t_cst = Tok()
        ones = sb("ones", [128, 128], BF16); t_ones = Tok()
        hT = sb("hT", [128, 16, NT], BF16); t_hT = [Tok() for _ in range(9)]
        P.dma("sp", mod[:], mod_d, writes=[t_mod])
        P.dma("sp", gn[:], gn_d, writes=[t_gn])
        P.dma("sp", g2[:], g2_d, writes=[t_gn])
        P.dma("sp", cst[:], cst_d, writes=[t_cst])
        P.dma("sp", wuq_f[:], wuq_d, writes=[t_wuq])
        P.dma("sp", wukv_f[:], wukv_d, writes=[t_wukv])
        P.dma("sp", cs[:], cs_d, writes=[t_cs])
        P.dma("sp", cs64[:], cs64_d, writes=[t_cs])
        P.op("pool", lambda e: e.memset(ones[:], 1.0), writes=[t_ones])
        P.op("pool", lambda e: e.tensor_copy(out=wuq[:], in_=wuq_f[:]), reads=[t_wuq], writes=[t_wuq])
        P.op("pool", lambda e: e.tensor_copy(out=wukv[:], in_=wukv_f[:]), reads=[t_wukv], writes=[t_wukv])
        for m in range(2):
            P.op("dve", lambda e, m=m: e.scalar_tensor_tensor(
                out=gain[:, m, :], in0=mod[:, 1 + 2 * m, :], scalar=1.0, in1=mod[:, 0, :],
                op0=ALU.add, op1=ALU.mult), reads=[t_mod], writes=[t_gain])
        for (o0, o1, src, s0, fac) in ((0, 2, gn, 0, 128.0 ** 0.5), (2, 5, gn, 2, 384.0 ** 0.5),
                                        (5, 7, gn, 5, 256.0 ** 0.5), (7, 11, g2, 0, 192.0 ** 0.5)):
            P.op("dve", lambda e, o0=o0, o1=o1, src=src, s0=s0, fac=fac: e.tensor_scalar(
                out=gs[:, o0:o1], in0=src[:, s0:s0 + (o1 - o0)], scalar1=fac, scalar2=None, op0=ALU.mult),
                reads=[t_gn], writes=[t_gs])

        xt = [sb("xt%d" % i, [128, D]) for i in range(2)]; t_xt = [Tok(), Tok()]
        xn = [sb("xn%d" % i, [128, D], BF16) for i in range(2)]; t_xn = [Tok(), Tok()]
        ssq = sb("ssq", [128, 16]); t_ssq = Tok()
        tp_all = ps("tp", [128, 2, 4, 128], BF16); tp = [tp_all[:, 0], tp_all[:, 1]]; t_tp = [Tok()] * 2
        tpi = 0
        for ti in range(9):
            r0 = ti * 128
            np_ = 128 if ti < 8 else CTK
            m = 0 if ti < 8 else 1
            b = ti % 2
            P.dma("sp", xt[b][0:np_, :], x_d[r0:r0 + np_, :], writes=[t_xt[b]])
            P.op("act", lambda e, b=b, np_=np_, ti=ti: e.activation(
                out=xn[b][0:np_, :], in_=xt[b][0:np_, :], func=AF.Square, accum_out=ssq[0:np_, ti:ti + 1]),
                reads=[t_xt[b]], writes=[t_xn[b], t_ssq])
            P.op("dve", lambda e, np_=np_, ti=ti: e.tensor_scalar(
                out=ssq[0:np_, ti:ti + 1], in0=ssq[0:np_, ti:ti + 1], scalar1=1.0 / D, scalar2=EPS,
                op0=ALU.mult, op1=ALU.add), reads=[t_ssq], writes=[t_ssq])
            P.op("act", lambda e, np_=np_, ti=ti: e.activation(
                out=ssq[0:np_, ti:ti + 1], in_=ssq[0:np_, ti:ti + 1], func=AF.Sqrt), reads=[t_ssq], writes=[t_ssq])
            P.op("dve", lambda e, np_=np_, ti=ti: e.reciprocal(
                out=ssq[0:np_, ti:ti + 1], in_=ssq[0:np_, ti:ti + 1]), reads=[t_ssq], writes=[t_ssq])
            P.op("dve", lambda e, b=b, np_=np_, ti=ti: e.tensor_scalar(
                out=xn[b][0:np_, :], in0=xt[b][0:np_, :], scalar1=ssq[0:np_, ti:ti + 1], scalar2=None,
                op0=ALU.mult), reads=[t_xt[b], t_ssq], writes=[t_xn[b]])
            for cg in range(4):
                pb = tpi % 2; tpi += 1
                for j in range(4):
                    c = cg * 4 + j
                    P.op("pe", lambda e, pb=pb, j=j, c=c, b=b, np_=np_: e.transpose(
                        tp[pb][:, j, 0:np_], xn[b][0:np_, c * 128:(c + 1) * 128], cst[0:np_, 0, 0:np_]),
                        reads=[t_xn[b], t_cst], writes=[t_tp[pb]])
                for j in range(4):
                    c = cg * 4 + j
                    eng = "act" if j % 2 == 0 else "dve"
                    if eng == "act":
                        P.op("act", lambda e, pb=pb, j=j, c=c, r0=r0, np_=np_, m=m: e.activation(
                            out=hT[:, c, r0:r0 + np_], in_=tp[pb][:, j, 0:np_], func=AF.Identity,
                            scale=gain[:, m, c:c + 1], bias=mod[:, 2 + 2 * m, c:c + 1]),
                            reads=[t_tp[pb], t_gain, t_mod], writes=[t_hT[ti]])
                    else:
                        P.op("dve", lambda e, pb=pb, j=j, c=c, r0=r0, np_=np_, m=m: e.tensor_scalar(
                            out=hT[:, c, r0:r0 + np_], in0=tp[pb][:, j, 0:np_], scalar1=gain[:, m, c:c + 1],
                            scalar2=mod[:, 2 + 2 * m, c:c + 1], op0=ALU.mult, op1=ALU.add),
                            reads=[t_tp[pb], t_gain, t_mod], writes=[t_hT[ti]])

        NWB = 3
        wst = [sb("wst%d" % i, [128, 16, 128]) for i in range(2)]; t_wst = [Tok() for _ in range(2)]
        wbf = [sb("wbf%d" % i, [128, 16, 128], BF16) for i in range(NWB)]; t_wbf = [Tok() for _ in range(NWB)]
        pj = [ps("pj%d" % i, [128, 3, 512]) for i in range(2)]; t_pj = [Tok(), Tok()]
        aux = ps("aux", [128, 1, 512]); t_aux = Tok()
        raw = [sb("raw%d" % i, [128, NT]) for i in range(5)]; t_raw = [Tok() for _ in range(5)]
        sqb = [sb("sqb%d" % i, [128, NT], BF16) for i in range(3)]; t_sqb = [Tok() for _ in range(3)]
        rstd = sb("rstd", [128, NT]); t_rstd = Tok()
        nbf = [sb("nbf%d" % i, [128, NT], BF16) for i in range(4)]; t_nbf = [Tok() for _ in range(4)]
        t1 = sb("t1", [128, NT]); t_t1 = Tok()
        t2 = sb("t2", [128, NT]); t_t2 = Tok()
        ob = [sb("ob%d" % i, [128, NT], BF16) for i in range(2)]; t_ob = [Tok(), Tok()]
        of = [sb("of%d" % i, [128, NT]) for i in range(2)]; t_of = [Tok(), Tok()]
        kr_raw = sb("kr_raw", [64, NT]); t_kr = Tok()
        kr_sq = sb("kr_sq", [64, NT], BF16); t_krsq = Tok()
        cnt = {"ob": 0, "of": 0}

        def store_bf(dst_ap, rows, producer):
            i = cnt["ob"] % 2; cnt["ob"] += 1
            producer(ob[i], t_ob[i])
            tk = Tok(); out_toks.append(tk)
            P.dma("sp", dst_ap, ob[i][0:rows, :], reads=[t_ob[i]], writes=[tk])

        def store_f32(dst_ap, rows, producer):
            i = cnt["of"] % 2; cnt["of"] += 1
            producer(of[i], t_of[i])
            tk = Tok(); out_toks.append(tk)
            P.dma("sp", dst_ap, of[i][0:rows, :], reads=[t_of[i]], writes=[tk])

        def ss_to_rstd(pieces, n_eps):
            for bi, (a0, a1) in enumerate(TBLK):
                for pi, (sqt, tk, rows) in enumerate(pieces):
                    P.op("pe", lambda e, sqt=sqt, rows=rows, a0=a0, a1=a1, bi=bi, pi=pi: e.matmul(
                        aux[:, 0, 0:a1 - a0], lhsT=ones[0:rows, :], rhs=sqt[0:rows, a0:a1],
                        start=(pi == 0), stop=(pi == len(pieces) - 1)),
                        reads=[tk, t_ones], writes=[t_aux])
                P.op("dve", lambda e, a0=a0, a1=a1, bi=bi: e.tensor_scalar(
                    out=rstd[:, a0:a1], in0=aux[:, 0, 0:a1 - a0], scalar1=n_eps, scalar2=None,
                    op0=ALU.add), reads=[t_aux], writes=[t_rstd])
            P.op("act", lambda e: e.activation(out=rstd[:], in_=rstd[:], func=AF.Sqrt), reads=[t_rstd], writes=[t_rstd])
            P.op("dve", lambda e: e.reciprocal(out=rstd[:], in_=rstd[:]), reads=[t_rstd], writes=[t_rstd])

        def rope_store(dst_ap, rows, nb, tnb, cst_idx, cs_t):
            P.op("pool", lambda e: e.tensor_tensor(out=t1[0:rows, :], in0=nb[0:rows, :], in1=cs_t[0:rows, 0, :],
                                                    op=ALU.mult), reads=[tnb, t_cs], writes=[t_t1])
            for bi, (a0, a1) in enumerate(TBLK):
                P.op("pe", lambda e, a0=a0, a1=a1, bi=bi: e.matmul(
                    aux[0:rows, 0, 0:a1 - a0], lhsT=cst[0:rows, cst_idx, 0:rows], rhs=nb[0:rows, a0:a1],
                    start=True, stop=True), reads=[tnb, t_cst], writes=[t_aux])
                P.op("dve", lambda e, a0=a0, a1=a1, bi=bi: e.tensor_tensor(
                    out=t2[0:rows, a0:a1], in0=aux[0:rows, 0, 0:a1 - a0], in1=cs_t[0:rows, 1, a0:a1],
                    op=ALU.mult), reads=[t_aux, t_cs], writes=[t_t2])
            store_bf(dst_ap, rows, lambda o, to: P.op("pool", lambda e: e.tensor_tensor(
                out=o[0:rows, :], in0=t1[0:rows, :], in1=t2[0:rows, :], op=ALU.add),
                reads=[t_t1, t_t2], writes=[to]))

        def load_w(ct):
            wb, ws = ct % NWB, ct % 2
            P.dma("act", wst[ws][:], w_d[ct], writes=[t_wst[ws]])
            P.op("dve", lambda e: e.tensor_copy(out=wbf[wb][:], in_=wst[ws][:]), reads=[t_wst[ws]], writes=[t_wbf[wb]])

        zi = 0
        deferred = []
        ci = {"qa": 0, "ka": 0, "va": 0, "ux": 0, "uqkv": 0, "cq": 0, "ckv": 0}
        for ct, (name, c0, ncol) in enumerate(COL_TILES):
            wb = ct % NWB
            if ct == 0:
                load_w(0)
            if ct + 1 < NCT:
                load_w(ct + 1)
            pb = ct % 2
            for bi, (a0, a1) in enumerate(TBLK):
                tks = t_hT[0:4] if bi == 0 else (t_hT[4:8] if bi == 1 else t_hT[8:9])
                for c in range(16):
                    P.op("pe", lambda e, pb=pb, bi=bi, a0=a0, a1=a1, c=c, wb=wb, ncol=ncol: e.matmul(
                        pj[pb][0:ncol, bi, 0:a1 - a0], lhsT=wbf[wb][:, c, 0:ncol], rhs=hT[:, c, a0:a1],
                        start=(c == 0), stop=(c == 15)), reads=[t_wbf[wb]] + tks, writes=[t_pj[pb]])

            if deferred:
                deferred.pop()()

            def evac(dst, tdst, rows, func=AF.Identity, pb=pb):
                for bi, (a0, a1) in enumerate(TBLK):
                    P.op("act", lambda e, bi=bi, a0=a0, a1=a1: e.activation(
                        out=dst[0:rows, a0:a1], in_=pj[pb][0:rows, bi, 0:a1 - a0], func=func),
                        reads=[t_pj[pb]], writes=[tdst])

            if name in ("qa", "ka"):
                i = ci[name]; ci[name] += 1
                evac(raw[0], t_raw[0], 128)
                evac(sqb[0], t_sqb[0], 128, AF.Square)

                def part_b(i=i, name=name):
                    ss_to_rstd([(sqb[0], t_sqb[0], 128)], 128 * EPS)
                    gcol = 0 if name == "qa" else 1
                    P.op("dve", lambda e: e.scalar_tensor_tensor(
                        out=nbf[0][:], in0=raw[0][:], scalar=gs[:, gcol:gcol + 1], in1=rstd[:],
                        op0=ALU.mult, op1=ALU.mult), reads=[t_raw[0], t_gs, t_rstd], writes=[t_nbf[0]])
                    rope_store((qa_o if name == "qa" else ka_o)[i], 128, nbf[0], t_nbf[0], 1, cs)
                deferred.append(part_b)
            elif name == "va":
                i = ci[name]; ci[name] += 1
                store_bf(va_o[i], 128, lambda o, to: evac(o, to, 128))
            elif name == "z":
                store_f32(zs_o[zi], 128, lambda o, to: evac(o, to, 128, AF.Silu)); zi += 1
            elif name == "ux":
                i = ci[name]; ci[name] += 1
                store_f32(ux_o[i], 128, lambda o, to: evac(o, to, 128))
            elif name == "uqkv":
                i = ci[name]; ci[name] += 1
                store_f32(uqkv_o[i], 128, lambda o, to: evac(o, to, 128))
            elif name == "uab":
                store_f32(uab_o, 16, lambda o, to: evac(o, to, 16))
            elif name in ("cq", "ckv"):
                i = ci[name]; ci[name] += 1
                nch = 3 if name == "cq" else 2
                evac(raw[i], t_raw[i], 128)
                evac(sqb[i], t_sqb[i], 128, AF.Square)
                if i == nch - 1:
                    ss_to_rstd([(sqb[k], t_sqb[k], 128) for k in range(nch)], (384 if name == "cq" else 256) * EPS)
                    g0 = 2 if name == "cq" else 5
                    for k in range(nch):
                        P.op("dve", lambda e, k=k, g0=g0: e.scalar_tensor_tensor(
                            out=nbf[k][:], in0=raw[k][:], scalar=gs[:, g0 + k:g0 + k + 1], in1=rstd[:],
                            op0=ALU.mult, op1=ALU.mult), reads=[t_raw[k], t_gs, t_rstd], writes=[t_nbf[k]])
                    if name == "cq":
                        for h in range(4):
                            for (pbi, col0, rows) in ((0, h * 192, 128), (1, h * 192 + 128, 64)):
                                for bi, (a0, a1) in enumerate(TBLK):
                                    for k in range(3):
                                        P.op("pe", lambda e, pbi=pbi, col0=col0, rows=rows, bi=bi, a0=a0, a1=a1, k=k: e.matmul(
                                            pj[pbi][0:rows, bi, 0:a1 - a0], lhsT=wuq[:, k, col0:col0 + rows],
                                            rhs=nbf[k][:, a0:a1], start=(k == 0), stop=(k == 2)),
                                            reads=[t_wuq, t_nbf[k]], writes=[t_pj[pbi]])
                            evac(raw[3], t_raw[3], 128, pb=0)
                            evac(sqb[0], t_sqb[0], 128, AF.Square, pb=0)
                            evac(raw[4], t_raw[4], 64, pb=1)
                            evac(sqb[1], t_sqb[1], 64, AF.Square, pb=1)
                            ss_to_rstd([(sqb[0], t_sqb[0], 128), (sqb[1], t_sqb[1], 64)], 192 * EPS)
                            store_bf(qdn_o[h], 128, lambda o, to: P.op("dve", lambda e: e.scalar_tensor_tensor(
                                out=o[:], in0=raw[3][:], scalar=gs[:, 7:8], in1=rstd[:], op0=ALU.mult, op1=ALU.mult),
                                reads=[t_raw[3], t_gs, t_rstd], writes=[to]))
                            P.op("dve", lambda e: e.scalar_tensor_tensor(
                                out=nbf[3][0:64, :], in0=raw[4][0:64, :], scalar=gs[0:64, 8:9], in1=rstd[0:64, :],
                                op0=ALU.mult, op1=ALU.mult), reads=[t_raw[4], t_gs, t_rstd], writes=[t_nbf[3]])
                            rope_store(qdr_o[h], 64, nbf[3], t_nbf[3], 2, cs64)
            elif name == "kr":
                evac(kr_raw, t_kr, 64)
                evac(kr_sq, t_krsq, 64, AF.Square)
                for h in range(4):
                    for (pbi, col0) in ((0, h * 256), (1, h * 256 + 128)):
                        for bi, (a0, a1) in enumerate(TBLK):
                            for k in range(2):
                                P.op("pe", lambda e, pbi=pbi, col0=col0, bi=bi, a0=a0, a1=a1, k=k: e.matmul(
                                    pj[pbi][:, bi, 0:a1 - a0], lhsT=wukv[:, k, col0:col0 + 128],
                                    rhs=nbf[k][:, a0:a1], start=(k == 0), stop=(k == 1)),
                                    reads=[t_wukv, t_nbf[k]], writes=[t_pj[pbi]])
                    evac(raw[3], t_raw[3], 128, pb=0)
                    evac(sqb[0], t_sqb[0], 128, AF.Square, pb=0)
                    store_bf(vd_o[h], 128, lambda o, to: evac(o, to, 128, pb=1))
                    ss_to_rstd([(sqb[0], t_sqb[0], 128), (kr_sq, t_krsq, 64)], 192 * EPS)
                    store_bf(kdn_o[h], 128, lambda o, to: P.op("dve", lambda e: e.scalar_tensor_tensor(
                        out=o[:], in0=raw[3][:], scalar=gs[:, 9:10], in1=rstd[:], op0=ALU.mult, op1=ALU.mult),
                        reads=[t_raw[3], t_gs, t_rstd], writes=[to]))
                    P.op("dve", lambda e: e.scalar_tensor_tensor(
                        out=nbf[2][0:64, :], in0=kr_raw[0:64, :], scalar=gs[0:64, 10:11], in1=rstd[0:64, :],
                        op0=ALU.mult, op1=ALU.mult), reads=[t_kr, t_gs, t_rstd], writes=[t_nbf[2]])
                    rope_store(kdr_o[h], 64, nbf[2], t_nbf[2], 2, cs64)
        P.wait_all("sp", out_toks)
        P.emit()
    return nc


def _fm(v):
    v = np.asarray(v, np.float32)
    return np.ascontiguousarray(v.reshape(-1, 128).T)


def _rope_tables():
    t = np.arange(SEQ)
    rows = (t // 64).astype(np.float64)
    cols = (t % 64).astype(np.float64)

    def tab(half, reps):
        inv = 10000.0 ** (-np.arange(half, dtype=np.float64) / half)
        ar = rows[None, :] * inv[:, None]
        ac = cols[None, :] * inv[:, None]
        ang = np.concatenate([ar, ar, ac, ac], axis=0)
        return np.cos(ang), np.sin(ang)
    c128, s128 = tab(32, 4)
    c64, s64 = tab(16, 4)
    return (c128.astype(np.float32), s128.astype(np.float32), c64.astype(np.float32), s64.astype(np.float32))


def _rot_mats():
    def rot(n):
        q = n // 4
        R = np.zeros((n, n), np.float32)
        for base in (0, 2 * q):
            for i in range(q):
                R[base + i, base + q + i] = -1.0
                R[base + q + i, base + i] = 1.0
        return R
    cst = np.zeros((128, 3, 128), np.float32)
    cst[:, 0, :] = np.eye(128, dtype=np.float32)
    cst[:, 1, :] = rot(128).T
    cst[0:64, 2, 0:64] = rot(64).T
    return cst.astype(NPBF)


_CONST = {}


def _consts():
    if not _CONST:
        _CONST["rope"] = _rope_tables()
        _CONST["cst"] = _rot_mats()
    return _CONST


def prep_k1(inputs, l, x_full, ctx_full, ada):
    cs = _consts()
    c128, s128, c64, s64 = cs["rope"]
    w_in = np.asarray(inputs["w_in"][l], np.float32)
    wt = np.zeros((NCT, 128, 16, 128), np.float32)
    for ct, (name, c0, ncol) in enumerate(COL_TILES):
        wt[ct, :, :, 0:ncol] = w_in[:, c0:c0 + ncol].reshape(16, 128, ncol).transpose(1, 0, 2)
    mod = np.stack([_fm(inputs["norm_w"][l]), _fm(ada[l, 0, 2048:4096]), _fm(ada[l, 0, 0:2048]),
                    _fm(ada[l, 1, 2048:4096]), _fm(ada[l, 1, 0:2048])], axis=1)
    gn = np.zeros((128, 8), np.float32)
    gn[:, 0] = inputs["attn_q_norm"][l]
    gn[:, 1] = inputs["attn_k_norm"][l]
    gn[:, 2:5] = _fm(inputs["mla_q_norm"][l])
    gn[:, 5:7] = _fm(inputs["mla_kv_norm"][l])
    g2 = np.zeros((128, 4), np.float32)
    g2[:, 0] = inputs["mla_q_qk_norm"][l][0:128]
    g2[0:64, 1] = inputs["mla_q_qk_norm"][l][128:192]
    g2[:, 2] = inputs["mla_k_qk_norm"][l][0:128]
    g2[0:64, 3] = inputs["mla_k_qk_norm"][l][128:192]
    wuq = np.ascontiguousarray(np.asarray(inputs["mla_w_uq"][l], np.float32).reshape(3, 128, 768).transpose(1, 0, 2))
    wukv = np.ascontiguousarray(np.asarray(inputs["mla_w_ukv"][l], np.float32).reshape(2, 128, 1024).transpose(1, 0, 2))
    in_maps = []
    for i in range(NCORES):
        xs = np.concatenate([x_full[i * TOK:(i + 1) * TOK], ctx_full[i * CTK:(i + 1) * CTK]], axis=0)
        csa = np.zeros((128, 2, NT), np.float32)
        csa[:, 0, 0:TOK] = c128[:, i * TOK:(i + 1) * TOK]
        csa[:, 1, 0:TOK] = s128[:, i * TOK:(i + 1) * TOK]
        csa[:, 0, TOK:] = 1.0
        csb = np.zeros((64, 2, NT), np.float32)
        csb[:, 0, 0:TOK] = c64[:, i * TOK:(i + 1) * TOK]
        csb[:, 1, 0:TOK] = s64[:, i * TOK:(i + 1) * TOK]
        csb[:, 0, TOK:] = 1.0
        in_maps.append({"x": np.ascontiguousarray(xs, np.float32), "mod": mod, "w": wt, "gn": gn, "g2": g2,
                        "wuq": wuq, "wukv": wukv, "cs": csa, "cs64": csb, "cst": cs["cst"]})
    return in_maps


NKT = NALL // 128
VW = 136


def build_k2a():
    nc = bass.Bass("TRN2", target_bir_lowering=False)
    dt_in = lambda name, shape, dt=F32: nc.dram_tensor(name, shape, dt, kind="ExternalInput").ap()
    dt_out = lambda name, shape, dt=F32: nc.dram_tensor(name, shape, dt, kind="ExternalOutput").ap()
    qm_d = dt_in("qm", [128, 8, NT], BF16)
    qr_d = dt_in("qr", [64, 4, NT], BF16)
    km_d = dt_in("km", [6, 128, NALL], BF16)
    kr_d = dt_in("kr", [4, 64, NALL], BF16)
    va_d = dt_in("va", [6, 128, NKT, VW], BF16)
    o_d = dt_out("o", [8, NT, 128])
    out_toks = []
    with contextlib.ExitStack() as es:
        P = Prog(nc, es)
        sb = lambda name, shape, dt=F32: es.enter_context(nc.sbuf_tensor("s_" + name, shape, dt))
        ps = lambda name, shape, dt=F32: es.enter_context(nc.psum_tensor("p_" + name, shape, dt))
        qm = sb("qm", [128, 8, NT], BF16); qr = sb("qr", [64, 4, NT], BF16); t_q = Tok()
        km = [sb("km%d" % i, [128, NALL], BF16) for i in range(2)]; t_km = [Tok(), Tok()]
        kr = [sb("kr%d" % i, [64, NALL], BF16) for i in range(2)]; t_kr = [Tok(), Tok()]
        va = [sb("va%d" % i, [128, NKT, VW], BF16) for i in range(2)]; t_va = [Tok(), Tok()]
        NS = 4
        s_ps = [ps("s%d" % i, [128, 512]) for i in range(NS)]; t_s = [Tok() for _ in range(NS)]
        pT = [sb("pT%d" % i, [128, 512], BF16) for i in range(NS)]; t_p = [Tok() for _ in range(NS)]
        o_ps = [ps("o%d" % i, [128, 512]) for i in range(4)]; t_o = [Tok() for _ in range(4)]
        rcp = sb("rcp", [128, 4]); t_rcp = Tok()
        o_sb = [sb("osb%d" % i, [128, 4, 128]) for i in range(2)]; t_osb = [Tok(), Tok()]
        P.dma("sp", qm[:], qm_d, writes=[t_q])
        P.dma("sp", qr[:], qr_d, writes=[t_q])
        slot = -1
        kvi = 0
        nfin = 0
        def load_kv(h, sl, kvi):
            P.dma("sp", km[sl][:], km_d[kvi], writes=[t_km[sl]])
            if h >= 4:
                P.dma("sp", kr[sl][:], kr_d[h - 4], writes=[t_kr[sl]])
            P.dma("sp", va[sl][:], va_d[kvi], writes=[t_va[sl]])
        slot_of = {0: 0, 1: 0, 2: 1, 3: 1, 4: 0, 5: 1, 6: 0, 7: 1}
        issue_at = {0: [(0, 0, 0), (2, 1, 1)], 2: [(4, 0, 2)], 4: [(5, 1, 3)], 5: [(6, 0, 4)], 6: [(7, 1, 5)]}
        for h in range(8):
            mla = h >= 4
            sl = slot_of[h]
            for args in issue_at.get(h, []):
                load_kv(*args)
            scale = (192.0 if mla else 128.0) ** -0.5
            def do_tile(h, sl, mla, scale, q0, q1, nkt, ob):
                nq = q1 - q0
                nj = (nq + 127) // 128

                def S(kt):
                    i = kt % NS
                    rd = [t_q, t_km[sl]] + ([t_kr[sl]] if mla else [])
                    P.op("pe", lambda e, i=i, kt=kt: e.matmul(
                        s_ps[i][:, 0:nq], lhsT=km[sl][:, kt * 128:(kt + 1) * 128], rhs=qm[:, h, q0:q1],
                        start=True, stop=not mla), reads=rd, writes=[t_s[i]])
                    if mla:
                        P.op("pe", lambda e, i=i, kt=kt: e.matmul(
                            s_ps[i][:, 0:nq], lhsT=kr[sl][:, kt * 128:(kt + 1) * 128], rhs=qr[:, h - 4, q0:q1],
                            start=False, stop=True), reads=rd, writes=[t_s[i]])

                def E(kt):
                    i = kt % NS
                    P.op("act", lambda e, i=i: e.activation(out=pT[i][:, 0:nq], in_=s_ps[i][:, 0:nq], func=AF.Exp,
                                                            scale=scale), reads=[t_s[i]], writes=[t_p[i]])

                def PV(kt):
                    i = kt % NS
                    for j in range(nj):
                        m = min(128, nq - j * 128)
                        P.op("pe", lambda e, i=i, j=j, m=m, kt=kt: e.matmul(
                            o_ps[j][0:m, 0:129], lhsT=pT[i][:, j * 128:j * 128 + m], rhs=va[sl][:, kt, 0:129],
                            start=(kt == 0), stop=(kt == nkt - 1)), reads=[t_p[i], t_va[sl]], writes=[t_o[j]])

                for kt in range(min(NS, nkt)):
                    S(kt)
                    E(kt)
                for kt in range(nkt):
                    PV(kt)
                    if kt + NS < nkt:
                        S(kt + NS)
                        E(kt + NS)
                for j in range(nj):
                    m = min(128, nq - j * 128)
                    P.op("dve", lambda e, j=j, m=m: e.reciprocal(out=rcp[0:m, j:j + 1], in_=o_ps[j][0:m, 128:129]),
                         reads=[t_o[j]], writes=[t_rcp])
                    P.op("dve", lambda e, j=j, m=m, ob=ob: e.tensor_scalar(
                        out=o_sb[ob][0:m, j, :], in0=o_ps[j][0:m, 0:128], scalar1=rcp[0:m, j:j + 1], scalar2=None,
                        op0=ALU.mult), reads=[t_o[j], t_rcp], writes=[t_osb[ob]])
                tk = Tok(); out_toks.append(tk)
                if nq == 512:
                    P.dma("sp", o_d[h, q0:q1, :].rearrange("(j p) d -> p j d", p=128), o_sb[ob][:],
                          reads=[t_osb[ob]], writes=[tk])
                else:
                    P.dma("sp", o_d[h, q0:q1, :], o_sb[ob][0:nq, 0, :], reads=[t_osb[ob]], writes=[tk])

            for (q0, q1, nkt) in ((0, 512, NKT), (512, 1024, NKT), (1024, NT, 2)):
                do_tile(h, sl, mla, scale, q0, q1, nkt, nfin % 2)
                nfin += 1
        P.wait_all("sp", out_toks)
        P.emit()
    return nc


SEGS = ((0, CTX), (CTX, NALL))
BLK512 = [(i, min(i + 512, NALL)) for i in range(0, NALL, 512)]


def emit_conv5(P, eng_a, eng_b, y, ty, x, tx, w, tw, wcol0, bias_ap=None):
    for (s0, s1) in SEGS:
        if bias_ap is None:
            P.op(eng_a, lambda e, s0=s0, s1=s1: e.tensor_scalar(
                out=y[:, s0:s1], in0=x[:, s0:s1], scalar1=w[:, wcol0 + 2:wcol0 + 3], scalar2=None, op0=ALU.mult),
                reads=[tx, tw], writes=[ty])
        else:
            P.op(eng_a, lambda e, s0=s0, s1=s1: e.tensor_scalar(
                out=y[:, s0:s1], in0=x[:, s0:s1], scalar1=w[:, wcol0 + 2:wcol0 + 3], scalar2=bias_ap,
                op0=ALU.mult, op1=ALU.add), reads=[tx, tw], writes=[ty])
        for o in (-2, -1, 1, 2):
            a0 = s0 + max(0, -o)
            a1 = s1 - max(0, o)
            P.op(eng_a, lambda e, a0=a0, a1=a1, o=o: e.scalar_tensor_tensor(
                out=y[:, a0:a1], in0=x[:, a0 + o:a1 + o], scalar=w[:, wcol0 + o + 2:wcol0 + o + 3], in1=y[:, a0:a1],
                op0=ALU.mult, op1=ALU.add), reads=[tx, tw, ty], writes=[ty])


def build_k2b():
    nc = bass.Bass("TRN2", target_bir_lowering=False)
    dt_in = lambda name, shape, dt=F32: nc.dram_tensor(name, shape, dt, kind="ExternalInput").ap()
    dt_out = lambda name, shape, dt=F32: nc.dram_tensor(name, shape, dt, kind="ExternalOutput").ap()
    ux_d = dt_in("ux", [128, NALL])
    prm_d = dt_in("prm", [128, 16])
    w_d = dt_in("w", [128, 2, 128])
    h_d = dt_out("h", [128, NALL])
    with contextlib.ExitStack() as es:
        P = Prog(nc, es)
        sb = lambda name, shape, dt=F32: es.enter_context(nc.sbuf_tensor("s_" + name, shape, dt))
        ps = lambda name, shape, dt=F32: es.enter_context(nc.psum_tensor("p_" + name, shape, dt))
        ux = sb("ux", [128, NALL]); t_ux = Tok()
        xs = sb("xs", [128, NALL]); t_xs = Tok()
        xb = sb("xb", [128, NALL], BF16); t_xb = Tok()
        av = sb("av", [128, NALL]); t_av = Tok()
        bv = sb("bv", [128, NALL]); t_bv = Tok()
        prm = sb("prm", [128, 16]); t_prm = Tok()
        c1 = sb("c1", [128, 2]); t_c1 = Tok()
        wf = sb("wf", [128, 2, 128]); wb = sb("wb", [128, 2, 128], BF16); t_w = Tok()
        rr = [sb("rr%d" % i, [128, 512]) for i in range(2)]; t_rr = [Tok(), Tok()]
        ii = [sb("ii%d" % i, [128, 512]) for i in range(2)]; t_ii = [Tok(), Tok()]
        pr = [ps("pr%d" % i, [128, 512]) for i in range(2)]; t_pr = [Tok(), Tok()]
        pi = [ps("pi%d" % i, [128, 512]) for i in range(2)]; t_pi = [Tok(), Tok()]
        P.dma("sp", ux[:], ux_d, writes=[t_ux])
        P.dma("sp", prm[:], prm_d, writes=[t_prm])
        P.dma("sp", wf[:], w_d, writes=[t_w])
        P.op("pool", lambda e: e.tensor_copy(out=wb[:], in_=wf[:]), reads=[t_w], writes=[t_w])
        P.op("act", lambda e: e.activation(out=c1[:, 0:1], in_=prm[:, 8:9], func=AF.Exp, scale=-1.0), reads=[t_prm], writes=[t_c1])
        P.op("act", lambda e: e.activation(out=c1[:, 0:1], in_=c1[:, 0:1], func=AF.Ln, bias=1.0), reads=[t_c1], writes=[t_c1])
        P.op("dve", lambda e: e.tensor_scalar(out=c1[:, 1:2], in0=c1[:, 0:1], scalar1=-16.0, scalar2=None, op0=ALU.mult), reads=[t_c1], writes=[t_c1])
        P.op("dve", lambda e: e.tensor_scalar(out=c1[:, 0:1], in0=c1[:, 0:1], scalar1=-8.0, scalar2=None, op0=ALU.mult), reads=[t_c1], writes=[t_c1])
        emit_conv5(P, "dve", "pool", xs, t_xs, ux, t_ux, prm, t_prm, 0, bias_ap=prm[:, 5:6])
        P.op("pool", lambda e: e.tensor_copy(out=xb[:], in_=xs[:]), reads=[t_xs], writes=[t_xb])
        for bi, (a0, a1) in enumerate(BLK512):
            n = a1 - a0
            b = bi % 2
            P.op("pe", lambda e, b=b, a0=a0, a1=a1, n=n: e.matmul(pr[b][:, 0:n], lhsT=wb[:, 0, :], rhs=xb[:, a0:a1], start=True, stop=True),
                 reads=[t_w, t_xb], writes=[t_pr[b]])
            P.op("pe", lambda e, b=b, a0=a0, a1=a1, n=n: e.matmul(pi[b][:, 0:n], lhsT=wb[:, 1, :], rhs=xb[:, a0:a1], start=True, stop=True),
                 reads=[t_w, t_xb], writes=[t_pi[b]])
            P.op("act", lambda e, b=b, n=n: e.activation(out=rr[b][:, 0:n], in_=pr[b][:, 0:n], func=AF.Sigmoid, bias=prm[:, 6:7]),
                 reads=[t_pr[b], t_prm], writes=[t_rr[b]])
            P.op("act", lambda e, b=b, n=n: e.activation(out=ii[b][:, 0:n], in_=pi[b][:, 0:n], func=AF.Sigmoid, bias=prm[:, 7:8]),
                 reads=[t_pi[b], t_prm], writes=[t_ii[b]])
            P.op("act", lambda e, b=b, a0=a0, a1=a1, n=n: e.activation(out=av[:, a0:a1], in_=rr[b][:, 0:n], func=AF.Exp, scale=c1[:, 0:1]),
                 reads=[t_rr[b], t_c1], writes=[t_av])
            P.op("act", lambda e, b=b, n=n: e.activation(out=rr[b][:, 0:n], in_=rr[b][:, 0:n], func=AF.Exp, scale=c1[:, 1:2]),
                 reads=[t_rr[b], t_c1], writes=[t_rr[b]])
            P.op("act", lambda e, b=b, n=n: e.activation(out=rr[b][:, 0:n], in_=rr[b][:, 0:n], func=AF.Sqrt, scale=-1.0, bias=1.0),
                 reads=[t_rr[b]], writes=[t_rr[b]])
            P.op("dve", lambda e, b=b, a0=a0, a1=a1, n=n: e.tensor_tensor(out=ii[b][:, 0:n], in0=ii[b][:, 0:n], in1=xs[:, a0:a1], op=ALU.mult),
                 reads=[t_ii[b], t_xs], writes=[t_ii[b]])
            P.op("dve", lambda e, b=b, a0=a0, a1=a1, n=n: e.tensor_tensor(out=bv[:, a0:a1], in0=ii[b][:, 0:n], in1=rr[b][:, 0:n], op=ALU.mult),
                 reads=[t_ii[b], t_rr[b]], writes=[t_bv])
        SC = 2112
        for i, s0 in enumerate(range(0, NALL, SC)):
            init = 0.0 if i == 0 else ux[:, s0 - 1:s0]
            P.op("dve", lambda e, s0=s0, init=init: e.tensor_tensor_scan(
                out=ux[:, s0:s0 + SC], data0=av[:, s0:s0 + SC], data1=bv[:, s0:s0 + SC], initial=init,
                op0=ALU.mult, op1=ALU.add), reads=[t_av, t_bv, t_ux], writes=[t_ux])
        tk = Tok()
        P.dma("sp", h_d, ux[:], reads=[t_ux], writes=[tk])
        P.wait_all("sp", [tk])
        P.emit()
    return nc


NCH = NALL // 64


def build_c1():
    nc = bass.Bass("TRN2", target_bir_lowering=False)
    dt_in = lambda name, shape, dt=F32: nc.dram_tensor(name, shape, dt, kind="ExternalInput").ap()
    dt_out = lambda name, shape, dt=F32: nc.dram_tensor(name, shape, dt, kind="ExternalOutput").ap()
    u_d = dt_in("u", [3, 128, NALL])
    cw_d = dt_in("cw", [128, 16])
    ab_d = dt_in("ab", [64, 2, NCH])
    sc_d = dt_in("sc", [64, 2])
    o_d = dt_out("qkv", [3, 128, NALL])
    bg_d = dt_out("bg", [64, 2, NCH])
    with contextlib.ExitStack() as es:
        P = Prog(nc, es)
        sb = lambda name, shape, dt=F32: es.enter_context(nc.sbuf_tensor("s_" + name, shape, dt))
        ps = lambda name, shape, dt=F32: es.enter_context(nc.psum_tensor("p_" + name, shape, dt))
        x = [sb("x%d" % i, [128, NALL]) for i in range(2)]; t_x = [Tok(), Tok()]
        y = [sb("y%d" % i, [128, NALL]) for i in range(2)]; t_y = [Tok(), Tok()]
        sq = sb("sq", [128, NALL]); t_sq = Tok()
        cw = sb("cw", [128, 16]); t_cw = Tok()
        ones = sb("ones", [128, 128]); t_ones = Tok()
        ab = sb("ab", [64, 2, NCH]); t_ab = Tok()
        sc = sb("sc", [64, 2]); t_sc = Tok()
        bg = sb("bg", [64, 2, NCH]); t_bg = Tok()
        pp = [ps("pp%d" % i, [128, 512]) for i in range(2)]; t_pp = [Tok(), Tok()]
        P.dma("sp", cw[:], cw_d, writes=[t_cw])
        P.dma("sp", ab[:], ab_d, writes=[t_ab])
        P.dma("sp", sc[:], sc_d, writes=[t_sc])
        P.op("pool", lambda e: e.memset(ones[:], 1.0), writes=[t_ones])
        P.op("act", lambda e: e.activation(out=bg[:, 0, :], in_=ab[:, 0, :], func=AF.Sigmoid), reads=[t_ab], writes=[t_bg])
        P.op("act", lambda e: e.activation(out=bg[:, 1, :], in_=ab[:, 1, :], func=AF.Exp, bias=sc[:, 1:2]), reads=[t_ab, t_sc], writes=[t_bg])
        P.op("act", lambda e: e.activation(out=bg[:, 1, :], in_=bg[:, 1, :], func=AF.Ln, bias=1.0), reads=[t_bg], writes=[t_bg])
        P.op("act", lambda e: e.activation(out=sc[:, 0:1], in_=sc[:, 0:1], func=AF.Exp), reads=[t_sc], writes=[t_sc])
        P.op("dve", lambda e: e.tensor_scalar(out=bg[:, 1, :], in0=bg[:, 1, :], scalar1=sc[:, 0:1], scalar2=-1.0,
                                              op0=ALU.mult, op1=ALU.mult), reads=[t_bg, t_sc], writes=[t_bg])
        tk_bg = Tok()
        P.dma("sp", bg_d, bg[:], reads=[t_bg], writes=[tk_bg])
        outs = [tk_bg]
        P.dma("sp", x[0][:], u_d[0], writes=[t_x[0]])
        P.dma("act", x[1][:], u_d[1], writes=[t_x[1]])
        for i in range(3):
            b = i % 2
            if i == 2:
                P.dma("act", x[b][:], u_d[i], writes=[t_x[b]])
            emit_conv5(P, "dve", "pool", y[b], t_y[b], x[b], t_x[b], cw, t_cw, 5 * i)
            P.op("act", lambda e, b=b: e.activation(out=y[b][:], in_=y[b][:], func=AF.Silu), reads=[t_y[b]], writes=[t_y[b]])
            if i < 2:
                P.op("pool", lambda e, b=b: e.tensor_tensor(out=sq[:], in0=y[b][:], in1=y[b][:], op=ALU.mult),
                     reads=[t_y[b]], writes=[t_sq])
                for bi, (a0, a1) in enumerate(BLK512):
                    n = a1 - a0
                    pb = bi % 2
                    P.op("pe", lambda e, pb=pb, a0=a0, a1=a1, n=n: e.matmul(pp[pb][:, 0:n], lhsT=ones[:], rhs=sq[:, a0:a1],
                                                                      start=True, stop=True), reads=[t_ones, t_sq], writes=[t_pp[pb]])
                    P.op("act", lambda e, pb=pb, a0=a0, a1=a1, n=n, b=b: e.activation(out=x[b][:, a0:a1], in_=pp[pb][:, 0:n], func=AF.Sqrt, bias=EPS),
                         reads=[t_pp[pb]], writes=[t_x[b]])
                P.op("dve", lambda e, b=b: e.reciprocal(out=x[b][:], in_=x[b][:]), reads=[t_x[b]], writes=[t_x[b]])
                fac = (128.0 ** -0.5) if i == 0 else 1.0
                P.op("dve", lambda e, b=b, fac=fac: e.scalar_tensor_tensor(out=y[b][:], in0=y[b][:], scalar=fac, in1=x[b][:],
                                                                       op0=ALU.mult, op1=ALU.mult), reads=[t_y[b], t_x[b]], writes=[t_y[b]])
            tk = Tok(); outs.append(tk)
            P.dma("sp", o_d[i], y[b][:], reads=[t_y[b]], writes=[tk])
        P.wait_all("sp", outs)
        P.emit()
    return nc


GRP = 8


def _dn_consts():
    c = np.zeros((64, 6, 64), np.float32)
    p = np.arange(64)[:, None]
    f = np.arange(64)[None, :]
    c[:, 0] = (p <= f)
    c[:, 1] = -(p <= f).astype(np.float32)
    c[:, 2] = -(p > f).astype(np.float32)
    c[:, 3] = -(f > p).astype(np.float32)
    c[:, 4] = (f >= p)
    c[:, 5] = (f == p)
    return c


def build_c2():
    nc = bass.Bass("TRN2", target_bir_lowering=False)
    dt_in = lambda name, shape, dt=F32: nc.dram_tensor(name, shape, dt, kind="ExternalInput").ap()
    dt_out = lambda name, shape, dt=F32: nc.dram_tensor(name, shape, dt, kind="ExternalOutput").ap()
    qT_d = dt_in("qT", [128, NALL]); kT_d = dt_in("kT", [128, NALL])
    ktm_d = dt_in("ktm", [64, NCH, 128]); vtm_d = dt_in("vtm", [64, NCH, 128])
    bg_d = dt_in("bg", [64, 2, NCH]); bbc_d = dt_in("bbc", [64, NALL]); cst_d = dt_in("cst", [64, 6, 64]); id_d = dt_in("ident", [128, 128])
    o_d = dt_out("o", [64, NCH, 128])
    with contextlib.ExitStack() as es:
        P = Prog(nc, es)
        sb = lambda name, shape, dt=F32: es.enter_context(nc.sbuf_tensor("s_" + name, shape, dt))
        ps = lambda name, shape, dt=F32: es.enter_context(nc.psum_tensor("p_" + name, shape, dt))
        cst = sb("cst", [64, 6, 64]); t_cst = Tok()
        bg = sb("bg", [64, 2, NCH]); t_bg = Tok()
        bbc = sb("bbc", [64, NALL]); t_bbc = Tok()
        ones = sb("ones", [64, 128]); t_ones = Tok()
        gb = sb("gb", [64, NCH, 64]); t_gb = Tok()
        gc = sb("gc", [64, NCH]); eg = sb("eg", [64, NCH]); beg = sb("beg", [64, NCH]); dk = sb("dk", [64, NCH]); t_gc = Tok()
        egl = sb("egl", [128, NCH]); t_egl = Tok()
        S = sb("S", [128, 128]); t_S = Tok()
        qT = [sb("qT%d" % i, [128, GRP * 64]) for i in range(2)]; kT = [sb("kT%d" % i, [128, GRP * 64]) for i in range(2)]
        ktm = [sb("ktm%d" % i, [64, GRP, 128]) for i in range(2)]; vtm = [sb("vtm%d" % i, [64, GRP, 128]) for i in range(2)]
        t_in = [Tok(), Tok()]
        E8 = sb("E8", [64, GRP, 64]); ET8 = sb("ET8", [64, GRP, 64]); t_E = Tok(); t_ET = Tok(); t_QK = [Tok(), Tok()]
        Pa = sb("Pa", [64, GRP, 64]); Pb = sb("Pb", [64, GRP, 64]); Pta = sb("Pta", [64, GRP, 64]); Ptb = sb("Ptb", [64, GRP, 64])
        t_P = [Tok(), Tok()]; t_Pt = [Tok(), Tok()]
        Tt = sb("Tt", [64, GRP, 64]); t_Tt = Tok()
        Rv = sb("Rv", [64, GRP, 128]); Rw = sb("Rw", [64, GRP, 128]); t_R = Tok()
        QK = [sb("QK%d" % i, [64, GRP, 64]) for i in range(2)]; kd = [sb("kd%d" % i, [64, GRP, 128]) for i in range(2)]
        U8 = [sb("U8%d" % i, [64, GRP, 128]) for i in range(2)]; W8 = [sb("W8%d" % i, [128, GRP, 64]) for i in range(2)]
        t_prep = [Tok(), Tok()]
        o_sb = [sb("osb%d" % i, [64, GRP, 128]) for i in range(2)]; t_osb = [Tok(), Tok()]
        vn = [sb("vn%d" % i, [64, 128]) for i in range(2)]; t_vn = [Tok(), Tok()]
        tmp = [sb("tmp%d" % i, [64, 128]) for i in range(2)]; t_tmp = [Tok(), Tok()]
        gb0 = ps("g0", [128, 512]); gb1 = ps("g1", [128, 512]); gb2 = ps("g2", [128, 512]); t_g = [Tok(), Tok(), Tok()]
        g0, g1, g2 = [t[0:64, :].rearrange("p (g c) -> p g c", c=64) for t in (gb0, gb1, gb2)]
        gm = [t[:, :].rearrange("p (g c) -> p g c", c=128) for t in (gb0, gb1, gb2)]
        gU = ps("gU", [64, 4, 128]); t_gU = Tok()
        gW = ps("gW", [128, GRP, 64]); t_gW = Tok()
        q3 = ps("q3", [64, 128]); t_q3 = Tok()
        q12 = ps("q12", [64, 2, 128]); t_q12 = Tok(); t_q12a = t_q12; t_q12b = t_q12
        q4 = ps("q4", [128, 128]); t_q4 = Tok()
        P.dma("sp", cst[:], cst_d, writes=[t_cst])
        P.dma("sp", bg[:], bg_d, writes=[t_bg])
        P.dma("sp", bbc[:], bbc_d, writes=[t_bbc])
        P.op("pool", lambda e: e.memset(ones[:], 1.0), writes=[t_ones])
        P.op("pool", lambda e: e.memset(S[:], 0.0), writes=[t_S])
        P.op("dve", lambda e: e.tensor_copy(out=gb[:], in_=bg[:, 1, :].unsqueeze(2).broadcast_to([64, NCH, 64])),
             reads=[t_bg], writes=[t_gb])
        P.op("pe", lambda e: e.matmul(g0[:, 0:3, :].rearrange("p a b -> p (a b)")[:, 0:NCH], lhsT=cst[:, 0, :], rhs=bg[:, 1, :],
                                      start=True, stop=True), reads=[t_cst, t_bg], writes=[t_g[0]])
        P.op("pe", lambda e: e.matmul(gW[:, 0:3, :].rearrange("p a b -> p (a b)")[:, 0:NCH], lhsT=ones[:, :], rhs=bg[:, 1, :],
                                      start=True, stop=True), reads=[t_ones, t_bg], writes=[t_gW])
        g0f = g0[:, 0:3, :].rearrange("p a b -> p (a b)")[:, 0:NCH]
        gWf = gW[:, 0:3, :].rearrange("p a b -> p (a b)")[:, 0:NCH]
        P.op("dve", lambda e: e.tensor_copy(out=gc[:], in_=g0f), reads=[t_g[0]], writes=[t_gc])
        P.op("act", lambda e: e.activation(out=eg[:], in_=g0f, func=AF.Exp), reads=[t_g[0]], writes=[t_gc])
        P.op("act", lambda e: e.activation(out=egl[:], in_=gWf, func=AF.Exp), reads=[t_gW], writes=[t_egl])
        P.op("dve", lambda e: e.tensor_tensor(out=dk[:], in0=gWf[0:64, :], in1=gc[:], op=ALU.subtract), reads=[t_gW, t_gc], writes=[t_gc])
        P.op("act", lambda e: e.activation(out=dk[:], in_=dk[:], func=AF.Exp), reads=[t_gc], writes=[t_gc])
        P.op("dve", lambda e: e.tensor_tensor(out=beg[:], in0=eg[:], in1=bg[:, 0, :], op=ALU.mult), reads=[t_gc, t_bg], writes=[t_gc])

        def bc_last(ap2d, n):
            return ap2d.unsqueeze(2).broadcast_to([64, ap2d.shape[1], n])

        def bc_mid(ap2d, g):
            return ap2d.unsqueeze(1).broadcast_to([64, g, ap2d.shape[1]])

        def prep_stages(gi, n0, G):
            st = []
            b = gi % 2
            c0, c1 = n0 * 64, (n0 + G) * 64

            def s_dma():
                P.dma("sp", qT[b][:, 0:G * 64], qT_d[:, c0:c1], writes=[t_in[b]])
                P.dma("sp", kT[b][:, 0:G * 64], kT_d[:, c0:c1], writes=[t_in[b]])
                P.dma("sp", ktm[b][:, 0:G, :], ktm_d[:, n0:n0 + G, :], writes=[t_in[b]])
                P.dma("sp", vtm[b][:, 0:G, :], vtm_d[:, n0:n0 + G, :], writes=[t_in[b]])
            st.append(s_dma)

            def s_mm0():
                for j in range(G):
                    n = n0 + j
                    ks = kT[b][:, j * 64:(j + 1) * 64]
                    P.op("pe", lambda e, j=j, n=n: e.matmul(g0[:, j, :], lhsT=cst[:, 0, :], rhs=gb[:, n, :], start=True, stop=False),
                         reads=[t_cst, t_gb], writes=[t_g[0]])
                    P.op("pe", lambda e, j=j, n=n: e.matmul(g0[:, j, :], lhsT=gb[:, n, :], rhs=cst[:, 1, :], start=False, stop=True),
                         reads=[t_cst, t_gb], writes=[t_g[0]])
                    P.op("pe", lambda e, j=j, ks=ks: e.matmul(g1[:, j, :], lhsT=ks, rhs=ks, start=True, stop=True),
                         reads=[t_in[b]], writes=[t_g[1]])
                    P.op("pe", lambda e, j=j, ks=ks: e.matmul(g2[:, j, :], lhsT=ks, rhs=qT[b][:, j * 64:(j + 1) * 64], start=True, stop=True),
                         reads=[t_in[b]], writes=[t_g[2]])
            st.append(s_mm0)

            def s_min():
                P.op("dve", lambda e: e.tensor_scalar(out=E8[:, 0:G, :], in0=g0[:, 0:G, :], scalar1=0.0, scalar2=None, op0=ALU.min),
                     reads=[t_g[0]], writes=[t_E])
                P.op("dve", lambda e: e.tensor_scalar(out=ET8[:, 0:G, :], in0=g0[:, 0:G, :], scalar1=-1.0, scalar2=0.0, op0=ALU.mult, op1=ALU.min),
                     reads=[t_g[0]], writes=[t_ET])
                P.op("pool", lambda e: e.tensor_tensor(out=Rv[:, 0:G, :], in0=vtm[b][:, 0:G, :], in1=bc_last(bg[:, 0, n0:n0 + G], 128), op=ALU.mult), reads=[t_in[b], t_bg], writes=[t_R])
                P.op("pool", lambda e: e.tensor_tensor(out=Rw[:, 0:G, :], in0=ktm[b][:, 0:G, :], in1=bc_last(beg[:, n0:n0 + G], 128), op=ALU.mult), reads=[t_in[b], t_gc], writes=[t_R])
                P.op("pool", lambda e: e.tensor_tensor(out=kd[b][:, 0:G, :], in0=ktm[b][:, 0:G, :], in1=bc_last(dk[:, n0:n0 + G], 128), op=ALU.mult), reads=[t_in[b], t_gc], writes=[t_prep[b]])
            st.append(s_min)

            def s_exp():
                P.op("act", lambda e: e.activation(out=E8[:, 0:G, :], in_=E8[:, 0:G, :], func=AF.Exp), reads=[t_E], writes=[t_E])
                P.op("act", lambda e: e.activation(out=ET8[:, 0:G, :], in_=ET8[:, 0:G, :], func=AF.Exp), reads=[t_ET], writes=[t_ET])
            st.append(s_exp)

            def s_mul1():
                P.op("dve", lambda e: e.tensor_tensor(out=Pa[:, 0:G, :], in0=g1[:, 0:G, :], in1=E8[:, 0:G, :], op=ALU.mult), reads=[t_g[1], t_E], writes=[t_P[0]])
                P.op("dve", lambda e: e.tensor_tensor(out=Pta[:, 0:G, :], in0=g1[:, 0:G, :], in1=ET8[:, 0:G, :], op=ALU.mult), reads=[t_g[1], t_ET], writes=[t_Pt[0]])
                P.op("dve", lambda e: e.tensor_tensor(out=QK[b][:, 0:G, :], in0=g2[:, 0:G, :], in1=ET8[:, 0:G, :], op=ALU.mult), reads=[t_g[2], t_ET], writes=[t_QK[b]])
            st.append(s_mul1)

            def s_mul2():
                P.op("pool", lambda e: e.tensor_tensor(out=Pa[:, 0:G, :], in0=Pa[:, 0:G, :], in1=bc_mid(cst[:, 2, :], G), op=ALU.mult), reads=[t_cst, t_P[0]], writes=[t_P[0]])
                P.op("pool", lambda e: e.tensor_tensor(out=Pta[:, 0:G, :], in0=Pta[:, 0:G, :], in1=bc_mid(cst[:, 3, :], G), op=ALU.mult), reads=[t_cst, t_Pt[0]], writes=[t_Pt[0]])
                P.op("pool", lambda e: e.tensor_tensor(out=QK[b][:, 0:G, :], in0=QK[b][:, 0:G, :], in1=bc_mid(cst[:, 4, :], G), op=ALU.mult), reads=[t_cst, t_QK[b]], writes=[t_QK[b]])
            st.append(s_mul2)

            def s_mul3():
                P.op("pool", lambda e: e.tensor_tensor(out=Pa[:, 0:G, :], in0=Pa[:, 0:G, :], in1=bc_last(bg[:, 0, n0:n0 + G], 64), op=ALU.mult), reads=[t_bg, t_P[0]], writes=[t_P[0]])
                P.op("pool", lambda e: e.tensor_tensor(out=Pta[:, 0:G, :], in0=Pta[:, 0:G, :], in1=bbc[:, c0:c1].rearrange("p (g c) -> p g c", c=64), op=ALU.mult),
                     reads=[t_bbc, t_Pt[0]], writes=[t_Pt[0]])
            st.append(s_mul3)

            def s_tt0():
                P.op("pool", lambda e: e.tensor_tensor(out=Tt[:, 0:G, :], in0=Pta[:, 0:G, :], in1=bc_mid(cst[:, 5, :], G), op=ALU.add), reads=[t_cst, t_Pt[0]], writes=[t_Tt])
            st.append(s_tt0)
            Pk, Ptk = [Pa, Pb], [Pta, Ptb]
            for k in range(5):
                a, nx = k % 2, (k + 1) % 2

                def s_pw(a=a):
                    for j in range(G):
                        P.op("pe", lambda e, j=j: e.matmul(g0[:, j, :], lhsT=Ptk[a][:, j, :], rhs=Pk[a][:, j, :], start=True, stop=True),
                             reads=[t_P[a], t_Pt[a]], writes=[t_g[0]])
                        P.op("pe", lambda e, j=j: e.matmul(g1[:, j, :], lhsT=Pk[a][:, j, :], rhs=Ptk[a][:, j, :], start=True, stop=True),
                             reads=[t_P[a], t_Pt[a]], writes=[t_g[1]])
                st.append(s_pw)

                def s_cp(nx=nx):
                    P.op("act", lambda e: e.activation(out=Pk[nx][:, 0:G, :], in_=g0[:, 0:G, :], func=AF.Identity), reads=[t_g[0]], writes=[t_P[nx]])
                    P.op("dve", lambda e: e.tensor_copy(out=Ptk[nx][:, 0:G, :], in_=g1[:, 0:G, :]), reads=[t_g[1]], writes=[t_Pt[nx]])
                st.append(s_cp)

                def s_tm(nx=nx):
                    for j in range(G):
                        P.op("pe", lambda e, j=j: e.matmul(g2[:, j, :], lhsT=Pk[nx][:, j, :], rhs=Tt[:, j, :], start=True, stop=True),
                             reads=[t_P[nx], t_Tt], writes=[t_g[2]])
                st.append(s_tm)

                def s_ta():
                    P.op("dve", lambda e: e.tensor_tensor(out=Tt[:, 0:G, :], in0=Tt[:, 0:G, :], in1=g2[:, 0:G, :], op=ALU.add), reads=[t_g[2], t_Tt], writes=[t_Tt])
                st.append(s_ta)

            def s_w():
                for j in range(G):
                    P.op("pe", lambda e, j=j: e.matmul(gW[:, j, :], lhsT=Rw[:, j, :], rhs=Tt[:, j, :], start=True, stop=True), reads=[t_Tt, t_R], writes=[t_gW])
            st.append(s_w)

            def s_wc():
                P.op("dve", lambda e: e.tensor_copy(out=W8[b][:, 0:G, :], in_=gW[:, 0:G, :]), reads=[t_gW], writes=[t_prep[b]])
            st.append(s_wc)
            for h0 in range(0, G, 4):
                h1 = min(G, h0 + 4)

                def s_u(h0=h0, h1=h1):
                    for j in range(h0, h1):
                        P.op("pe", lambda e, j=j: e.matmul(gU[:, j - h0, :], lhsT=Tt[:, j, :], rhs=Rv[:, j, :], start=True, stop=True), reads=[t_Tt, t_R], writes=[t_gU])
                st.append(s_u)

                def s_uc(h0=h0, h1=h1):
                    P.op("act", lambda e: e.activation(out=U8[b][:, h0:h1, :], in_=gU[:, 0:h1 - h0, :], func=AF.Identity), reads=[t_gU], writes=[t_prep[b]])
                st.append(s_uc)
            return st

        def seq_group(gi, n0, G, filler):
            b = gi % 2
            per = (len(filler) + G - 1) // G if filler else 0
            for j in range(G):
                n = n0 + j
                v = n % 2
                P.op("pe", lambda e, j=j: e.matmul(q3[:, :], lhsT=W8[b][:, j, :], rhs=S[:, :], start=True, stop=True), reads=[t_prep[b], t_S], writes=[t_q3])
                P.op("pe", lambda e, j=j: e.matmul(q12[:, 0, :], lhsT=qT[b][:, j * 64:(j + 1) * 64], rhs=S[:, :], start=True, stop=True), reads=[t_in[b], t_S], writes=[t_q12])
                P.op("dve", lambda e, j=j, v=v: e.tensor_tensor(out=vn[v][:], in0=U8[b][:, j, :], in1=q3[:, :], op=ALU.subtract), reads=[t_prep[b], t_q3], writes=[t_vn[v]])
                P.op("pe", lambda e, j=j, v=v: e.matmul(q4[:, :], lhsT=kd[b][:, j, :], rhs=vn[v][:], start=True, stop=True), reads=[t_prep[b], t_vn[v]], writes=[t_q4])
                P.op("pe", lambda e, j=j, v=v: e.matmul(q12[:, 1, :], lhsT=QK[b][:, j, :], rhs=vn[v][:], start=True, stop=True), reads=[t_QK[b], t_vn[v]], writes=[t_q12])
                P.op("dve", lambda e, n=n: e.scalar_tensor_tensor(out=S[:], in0=S[:], scalar=egl[:, n:n + 1], in1=q4[:, :], op0=ALU.mult, op1=ALU.add),
                     reads=[t_S, t_egl, t_q4], writes=[t_S])
                P.op("act", lambda e, n=n, v=v: e.activation(out=tmp[v][:], in_=q12[:, 0, :], func=AF.Identity, scale=eg[:, n:n + 1]), reads=[t_q12, t_gc], writes=[t_tmp[v]])
                P.op("dve", lambda e, j=j, v=v: e.tensor_tensor(out=o_sb[b][:, j, :], in0=tmp[v][:], in1=q12[:, 1, :], op=ALU.add), reads=[t_tmp[v], t_q12], writes=[t_osb[b]])
                for f in filler[j * per:(j + 1) * per]:
                    f()
            tk = Tok(); outs.append(tk)
            P.dma("sp", o_d[:, n0:n0 + G, :], o_sb[b][:, 0:G, :], reads=[t_osb[b]], writes=[tk])

        outs = []
        groups = [(gi, n0, min(GRP, NCH - n0)) for gi, n0 in enumerate(range(0, NCH, GRP))]
        for f in prep_stages(*groups[0]):
            f()
        for idx, grp in enumerate(groups):
            filler = prep_stages(*groups[idx + 1]) if idx + 1 < len(groups) else []
            seq_group(grp[0], grp[1], grp[2], filler)
        P.wait_all("sp", outs)
        P.emit()
    return nc


def build_k3():
    nc = bass.Bass("TRN2", target_bir_lowering=False)
    dt_in = lambda name, shape, dt=F32: nc.dram_tensor(name, shape, dt, kind="ExternalInput").ap()
    dt_out = lambda name, shape, dt=F32: nc.dram_tensor(name, shape, dt, kind="ExternalOutput").ap()
    m1_d = dt_in("m1", [16, 128, NT])
    m2_d = dt_in("m2", [8, 128, NT])
    zs_d = dt_in("zs", [16, 128, NT])
    nw_d = dt_in("nw", [128, 1])
    gt_d = dt_in("gt", [128, 2, D])
    x_d = dt_in("x", [NT, D])
    w_d = dt_in("w", [4, 128, 16, 512])
    xo_d = dt_out("xo", [NT, D])
    with contextlib.ExitStack() as es:
        P = Prog(nc, es)
        sb = lambda name, shape, dt=F32: es.enter_context(nc.sbuf_tensor("s_" + name, shape, dt))
        ps = lambda name, shape, dt=F32: es.enter_context(nc.psum_tensor("p_" + name, shape, dt))
        yT = sb("yT", [128, 16, NT], BF16); t_y = [Tok() for _ in range(16)]
        a_sb = [sb("a%d" % i, [128, NT]) for i in range(2)]; t_a = [Tok(), Tok()]
        b_sb = [sb("b%d" % i, [128, NT]) for i in range(2)]; t_b = [Tok(), Tok()]
        z_sb = [sb("z%d" % i, [128, NT]) for i in range(2)]; t_z = [Tok(), Tok()]
        sq = sb("sq", [128, NT]); t_sq = Tok()
        rs = sb("rs", [128, NT]); t_rs = Tok()
        nw = sb("nw", [128, 1]); t_nw = Tok()
        gt = sb("gt", [128, 2, D]); t_gt = Tok()
        ones = sb("ones", [128, 128]); t_ones = Tok()
        wst = [sb("wst%d" % i, [128, 16, 512]) for i in range(2)]; t_wst = [Tok(), Tok()]
        wbf = [sb("wbf%d" % i, [128, 16, 512], BF16) for i in range(2)]; t_wbf = [Tok(), Tok()]
        xt = [sb("xt%d" % i, [128, 512]) for i in range(2)]; t_xt = [Tok(), Tok()]
        xo = [sb("xo%d" % i, [128, 512]) for i in range(2)]; t_xo = [Tok(), Tok()]
        pn = ps("pn", [128, 512]); t_pn = Tok()
        po = [ps("po%d" % i, [128, 512]) for i in range(2)]; t_po = [Tok(), Tok()]
        P.dma("sp", nw[:], nw_d, writes=[t_nw])
        P.dma("sp", gt[:], gt_d, writes=[t_gt])
        P.op("pool", lambda e: e.memset(ones[:], 1.0), writes=[t_ones])
        P.op("dve", lambda e: e.tensor_scalar(out=nw[:], in0=nw[:], scalar1=128.0 ** 0.5, scalar2=None, op0=ALU.mult), reads=[t_nw], writes=[t_nw])

        def do_chunk(c):
            b = c % 2
            P.dma("sp", a_sb[b][:], m1_d[c], writes=[t_a[b]])
            P.dma("act", z_sb[b][:], zs_d[c], writes=[t_z[b]])
            if 4 <= c < 12:
                P.dma("act", b_sb[b][:], m2_d[c - 4], writes=[t_b[b]])
                P.op("pool", lambda e: e.tensor_tensor(out=a_sb[b][:], in0=a_sb[b][:], in1=b_sb[b][:], op=ALU.add),
                     reads=[t_a[b], t_b[b]], writes=[t_a[b]])
            if 8 <= c < 12:
                P.op("pool", lambda e: e.tensor_tensor(out=sq[:], in0=a_sb[b][:], in1=a_sb[b][:], op=ALU.mult), reads=[t_a[b]], writes=[t_sq])
                for (a0, a1) in TBLK:
                    P.op("pe", lambda e, a0=a0, a1=a1: e.matmul(pn[:, 0:a1 - a0], lhsT=ones[:], rhs=sq[:, a0:a1], start=True, stop=True),
                         reads=[t_ones, t_sq], writes=[t_pn])
                    P.op("act", lambda e, a0=a0, a1=a1: e.activation(out=rs[:, a0:a1], in_=pn[:, 0:a1 - a0], func=AF.Sqrt, bias=128 * EPS),
                         reads=[t_pn], writes=[t_rs])
                P.op("dve", lambda e: e.reciprocal(out=rs[:], in_=rs[:]), reads=[t_rs], writes=[t_rs])
                P.op("dve", lambda e: e.scalar_tensor_tensor(out=a_sb[b][:], in0=a_sb[b][:], scalar=nw[:, 0:1], in1=rs[:], op0=ALU.mult, op1=ALU.mult),
                     reads=[t_a[b], t_nw, t_rs], writes=[t_a[b]])
            P.op("dve", lambda e: e.tensor_tensor(out=yT[:, c, :], in0=a_sb[b][:], in1=z_sb[b][:], op=ALU.mult),
                 reads=[t_a[b], t_z[b]], writes=[t_y[c]])
        outs = []
        cnt = 0
        def load_wo(cb):
            wbi = cb % 2
            P.dma("act", wst[wbi][:], w_d[cb], writes=[t_wst[wbi]])
            for hh in range(2):
                P.op("dve" if hh == 0 else "pool", lambda e, hh=hh: e.tensor_copy(out=wbf[wbi][:, hh * 8:(hh + 1) * 8, :], in_=wst[wbi][:, hh * 8:(hh + 1) * 8, :]),
                     reads=[t_wst[wbi]], writes=[t_wbf[wbi]])
        load_wo(0)
        for c in range(16):
            do_chunk(c)
        for cb in range(4):
            wbi = cb % 2
            if cb + 1 < 4:
                load_wo(cb + 1)

            def do_tile(ti, cb, wbi, i):
                r0 = ti * 128
                np_ = 128 if ti < 8 else CTK
                m = 0 if ti < 8 else 1
                P.dma("act", xt[i][0:np_, :], x_d[r0:r0 + np_, cb * 512:(cb + 1) * 512], writes=[t_xt[i]])
                for c in range(16):
                    P.op("pe", lambda e, c=c: e.matmul(po[i][0:np_, :], lhsT=yT[:, c, r0:r0 + np_], rhs=wbf[wbi][:, c, :],
                                                      start=(c == 0), stop=(c == 15)), reads=[t_y[c], t_wbf[wbi]], writes=[t_po[i]])
                P.op("dve", lambda e: e.tensor_tensor(out=xo[i][0:np_, :], in0=po[i][0:np_, :], in1=gt[0:np_, m, cb * 512:(cb + 1) * 512], op=ALU.mult),
                     reads=[t_po[i], t_gt], writes=[t_xo[i]])
                P.op("pool", lambda e: e.tensor_tensor(out=xo[i][0:np_, :], in0=xo[i][0:np_, :], in1=xt[i][0:np_, :], op=ALU.add),
                     reads=[t_xo[i], t_xt[i]], writes=[t_xo[i]])
                tk = Tok(); outs.append(tk)
                P.dma("sp", xo_d[r0:r0 + np_, cb * 512:(cb + 1) * 512], xo[i][0:np_, :], reads=[t_xo[i]], writes=[tk])
            for ti in range(9):
                do_tile(ti, cb, wbi, cnt % 2)
                cnt += 1
        P.wait_all("sp", outs)
        P.emit()
    return nc


def _gather(res, name):
    ctx = np.concatenate([np.asarray(r[name])[..., TOK:] for r in res], axis=-1)
    lat = np.concatenate([np.asarray(r[name])[..., :TOK] for r in res], axis=-1)
    return np.concatenate([ctx, lat], axis=-1)


def _flipseg(a):
    return np.concatenate([a[..., :CTX][..., ::-1], a[..., CTX:][..., ::-1]], axis=-1)


def _core_slice(a, i):
    return np.concatenate([a[..., CTX + i * TOK:CTX + (i + 1) * TOK], a[..., i * CTK:(i + 1) * CTK]], axis=-1)


def _taps5(cw4, d):
    out = np.zeros((cw4.shape[1], 5), np.float32)
    if d == 0:
        out[:, 0:4] = cw4.T
    else:
        out[:, 1] = cw4[3]; out[:, 2] = cw4[2]; out[:, 3] = cw4[1]; out[:, 4] = cw4[0]
    return out


_PROGS = {}


def _prog(name, builder):
    if name not in _PROGS:
        _PROGS[name] = builder()
    return _PROGS[name]


def _forward(inputs, depth=DEPTH):
    f32 = lambda a: np.ascontiguousarray(np.asarray(a, np.float32))
    x_full = f32(inputs["x"][0]).copy()
    ctx_full = f32(inputs["ctx"][0]).copy()
    ada = run_k0(inputs)
    for l in range(depth):
        im1 = prep_k1(inputs, l, x_full, ctx_full, ada)
        r1 = _run(_prog("k1", build_k1), im1, "k1")
        ka = _gather(r1, "ka"); kdn = _gather(r1, "kdn"); kdr = _gather(r1, "kdr")
        va = _gather(r1, "va"); vd = _gather(r1, "vd")
        km = np.ascontiguousarray(np.concatenate([ka, kdn], axis=0))
        vall = np.concatenate([va, vd], axis=0)
        vaug = np.zeros((6, 128, NKT, VW), NPBF)
        vaug[:, :, :, 0:128] = vall.transpose(0, 2, 1).reshape(6, NKT, 128, 128).transpose(0, 2, 1, 3)
        vaug[:, :, :, 128] = 1.0
        im2 = []
        for i in range(NCORES):
            qm = np.ascontiguousarray(np.concatenate([np.asarray(r1[i]["qa"]), np.asarray(r1[i]["qdn"])], axis=0).transpose(1, 0, 2))
            qr = np.ascontiguousarray(np.asarray(r1[i]["qdr"]).transpose(1, 0, 2))
            im2.append({"qm": qm, "qr": qr, "km": km, "kr": np.ascontiguousarray(kdr), "va": vaug})
        r2a = _run(_prog("k2a", build_k2a), im2, "k2a")
        ux = _gather(r1, "ux")
        cwl = f32(inputs["lru_conv_w"][l])
        im3 = []
        for i in range(NCORES):
            d, blk = i // 4, i % 4
            sl = slice(blk * 128, (blk + 1) * 128)
            u = ux[blk] if d == 0 else _flipseg(ux[blk])
            prm = np.zeros((128, 16), np.float32)
            prm[:, 0:5] = _taps5(cwl[:, sl], d)
            prm[:, 5] = inputs["lru_conv_b"][l][sl]
            prm[:, 6] = inputs["lru_b_a"][l][d][sl]
            prm[:, 7] = inputs["lru_b_x"][l][d][sl]
            prm[:, 8] = inputs["lru_lambda"][l][d][sl]
            w = np.ascontiguousarray(np.stack([f32(inputs["lru_w_a"][l][d][blk]), f32(inputs["lru_w_x"][l][d][blk])], axis=1))
            im3.append({"ux": f32(u), "prm": prm, "w": w})
        r2b = _run(_prog("k2b", build_k2b), im3, "k2b")
        hdir = [[None] * 4, [None] * 4]
        for i in range(NCORES):
            d, blk = i // 4, i % 4
            h = np.asarray(r2b[i]["h"])
            hdir[d][blk] = h if d == 0 else _flipseg(h)
        uqkv = _gather(r1, "uqkv")
        uab = _gather(r1, "uab")
        cwd = f32(inputs["dn_conv_w"][l])
        im4 = []
        for i in range(NCORES):
            d, hd = i // 4, i % 4
            u = np.stack([uqkv[hd], uqkv[4 + hd], uqkv[8 + hd]], axis=0)
            br, ar = uab[d * 8 + hd], uab[d * 8 + 4 + hd]
            if d == 1:
                u = _flipseg(u); br = _flipseg(br); ar = _flipseg(ar)
            cw = np.zeros((128, 16), np.float32)
            for j in range(3):
                c = j * 4 + hd
                cw[:, 5 * j:5 * j + 5] = _taps5(cwd[:, c * 128:(c + 1) * 128], d)
            ab = np.ascontiguousarray(np.stack([br.reshape(NCH, 64).T, ar.reshape(NCH, 64).T], axis=1))
            sc = np.zeros((64, 2), np.float32)
            sc[:, 0] = inputs["dn_a_log"][l][d][hd]
            sc[:, 1] = inputs["dn_dt_bias"][l][d][hd]
            im4.append({"u": f32(u), "cw": cw, "ab": f32(ab), "sc": sc})
        rc1 = _run(_prog("c1", build_c1), im4, "c1")
        dnc = _dn_consts()
        im5 = []
        for i in range(NCORES):
            qkv = np.asarray(rc1[i]["qkv"]); bg = np.asarray(rc1[i]["bg"])
            tm = lambda a: np.ascontiguousarray(a.T.reshape(NCH, 64, 128).transpose(1, 0, 2))
            brow = bg[:, 0, :].T.reshape(-1)
            im5.append({"qT": f32(qkv[0]), "kT": f32(qkv[1]), "ktm": tm(qkv[1]), "vtm": tm(qkv[2]), "bg": f32(bg),
                        "bbc": np.ascontiguousarray(np.broadcast_to(brow[None, :], (64, NALL))), "cst": dnc,
                        "ident": np.eye(128, dtype=np.float32)})
        rc2 = _run(_prog("c2", build_c2), im5, "c2")
        odir = [[None] * 4, [None] * 4]
        for i in range(NCORES):
            d, hd = i // 4, i % 4
            o = np.asarray(rc2[i]["o"]).transpose(1, 0, 2).reshape(NALL, 128).T
            odir[d][hd] = o if d == 0 else _flipseg(o)
        w_out = f32(inputs["w_out"][l])
        wl = np.ascontiguousarray(w_out.reshape(16, 128, 4, 512).transpose(2, 1, 0, 3))
        gt = np.ascontiguousarray(np.broadcast_to(np.stack([ada[l, 0, 4096:], ada[l, 1, 4096:]], axis=0)[None], (128, 2, D)))
        nw = f32(inputs["dn_norm_w"][l]).reshape(128, 1)
        im6 = []
        for i in range(NCORES):
            o2 = np.asarray(r2a[i]["o"])
            m1 = np.empty((16, 128, NT), np.float32)
            m2 = np.empty((8, 128, NT), np.float32)
            for j in range(4):
                m1[j] = o2[j].T
                m1[12 + j] = o2[4 + j].T
                m1[4 + j] = _core_slice(hdir[0][j], i)
                m2[j] = _core_slice(hdir[1][j], i)
                m1[8 + j] = _core_slice(odir[0][j], i)
                m2[4 + j] = _core_slice(odir[1][j], i)
            im6.append({"m1": m1, "m2": m2, "zs": f32(r1[i]["zs"]), "nw": nw, "gt": gt, "x": im1[i]["x"], "w": wl})
        r3 = _run(_prog("k3", build_k3), im6, "k3")
        for i in range(NCORES):
            xo = np.asarray(r3[i]["xo"])
            x_full[i * TOK:(i + 1) * TOK] = xo[:TOK]
            ctx_full[i * CTK:(i + 1) * CTK] = xo[TOK:]
    return x_full, ctx_full


def kernel(**inputs):
    x_full, _ = _forward(inputs, DEPTH)
    return np.ascontiguousarray(x_full[None].astype(np.float32))
```
